# Optimizing a Trainium2 kernel written in Bass

```python
import math
import jax
import jax.numpy as jnp
from jax import lax
import numpy as np

D_MODEL = 1024
BATCH = 8
SEQ = 2048
DEPTH = 4

FFN_DIM = 2816
BRANCH_WIDTH = 512
N_BRANCH = 4
SGU_WIDTH = BRANCH_WIDTH
SGU_GROUPS = 4
SGU_CHUNK = 128
SSM_HEADS = 8
SSM_HEAD_DIM = 64
SSM_INNER = SSM_HEADS * SSM_HEAD_DIM
SSM_GROUPS = 2
SSM_STATE = 128
SSM_CONV = 4
SSM_CHUNK = 128
SSM_CONV_DIM = SSM_INNER + 2 * SSM_GROUPS * SSM_STATE
NSA_HEADS = 8
NSA_KV_GROUPS = 2
NSA_HEAD_DIM = 64
NSA_WIDTH = NSA_HEADS * NSA_HEAD_DIM
CMP_BLOCK = 32
CMP_STRIDE = 16
CMP_HIDDEN = 128
SEL_BLOCK = 64
SEL_TOP_N = 8
WINDOW = 256
Q_BLOCK = 128
CONV_WIDTH = BRANCH_WIDTH
CONV_KERNEL = 31
EPS = 1e-6
NEG = -1e30
FORCE = 1e9
NSA_KV_WIDTH = NSA_KV_GROUPS * NSA_HEAD_DIM
SPLIT_SIZES = (2 * SGU_WIDTH, SSM_INNER, SSM_CONV_DIM, SSM_HEADS, NSA_WIDTH,
               NSA_KV_WIDTH, NSA_KV_WIDTH, NSA_KV_WIDTH, NSA_KV_WIDTH, NSA_KV_WIDTH, NSA_KV_WIDTH,
               3 * NSA_HEADS, 2 * CONV_WIDTH, N_BRANCH * D_MODEL)
IN_PROJ_DIM = 2 * SGU_WIDTH + SSM_INNER + SSM_CONV_DIM + SSM_HEADS + NSA_WIDTH + 6 * NSA_KV_WIDTH + 3 * NSA_HEADS + 2 * CONV_WIDTH + N_BRANCH * D_MODEL

kernel_name = "hybrid_sgu_ssd_nsa_conformer_macaron"


def rms_norm(x, g):
    xf = x.astype(jnp.float32)
    y = xf * lax.rsqrt(jnp.mean(xf * xf, axis=-1, keepdims=True) + EPS)
    return (y * g.astype(jnp.float32)).astype(x.dtype)


def swiglu(x, w_in, w_out):
    gate, up = jnp.split(x @ w_in, 2, axis=-1)
    return (jax.nn.silu(gate) * up) @ w_out


def causal_depthwise_conv(x, w, b):
    k, c = w.shape
    y = lax.conv_general_dilated(x, w[:, None, :].astype(x.dtype), window_strides=(1,),
                                 padding=[(k - 1, 0)], dimension_numbers=("NWC", "WIO", "NWC"),
                                 feature_group_count=c)
    return y + b.astype(x.dtype)


def alibi_slopes(n):
    return np.array([2.0 ** (-8.0 * (i + 1) / n) for i in range(n)], dtype=np.float32)


def segsum(a):
    t = a.shape[-1]
    ar = jnp.broadcast_to(a[..., :, None], a.shape + (t,))
    cs = jnp.cumsum(jnp.where(jnp.tril(jnp.ones((t, t), bool), -1), ar, 0.0), axis=-2)
    return jnp.where(jnp.tril(jnp.ones((t, t), bool)), cs, -jnp.inf)


def ssd_scan(x, a, b, c):
    bsz, s, h, p = x.shape
    n = b.shape[-1]
    nc = s // SSM_CHUNK
    x = x.reshape(bsz, nc, SSM_CHUNK, h, p)
    b = b.reshape(bsz, nc, SSM_CHUNK, h, n)
    c = c.reshape(bsz, nc, SSM_CHUNK, h, n)
    a = a.reshape(bsz, nc, SSM_CHUNK, h).transpose(0, 3, 1, 2)
    a_cs = jnp.cumsum(a, axis=-1)
    scores = jnp.einsum("bclhn,bcshn->bhcls", c, b) * jnp.exp(segsum(a))
    y_diag = jnp.einsum("bhcls,bcshp->bclhp", scores, x)
    decay_states = jnp.exp(a_cs[..., -1:] - a_cs)
    states = jnp.einsum("bclhn,bhcl,bclhp->bchpn", b, decay_states, x)
    states = jnp.concatenate([jnp.zeros_like(states[:, :1]), states], axis=1)
    decay_chunk = jnp.exp(segsum(jnp.pad(a_cs[..., -1], ((0, 0), (0, 0), (1, 0)))))
    states = jnp.einsum("bhzc,bchpn->bzhpn", decay_chunk, states)[:, :-1]
    y_off = jnp.einsum("bclhn,bchpn,bhcl->bclhp", c, states, jnp.exp(a_cs))
    return (y_diag + y_off).reshape(bsz, s, h, p)


def sgu_mixer(uv, v_norm, w_s, b_s):
    bsz, s, _ = uv.shape
    u, v = jnp.split(jax.nn.gelu(uv), 2, axis=-1)
    v = rms_norm(v, v_norm).reshape(bsz, s // SGU_CHUNK, SGU_CHUNK, SGU_GROUPS, SGU_WIDTH // SGU_GROUPS)
    w = w_s * jnp.tril(jnp.ones((SGU_CHUNK, SGU_CHUNK), w_s.dtype))
    mixed = jnp.einsum("gts,bcsgd->bctgd", w, v) + b_s.T[None, None, :, :, None]
    return u * mixed.reshape(bsz, s, SGU_WIDTH)


def mamba2_mixer(z, xbc, dt, conv_w, conv_b, dt_bias, a_log, d_skip, norm_g):
    bsz, s, _ = z.shape
    f32 = jnp.float32
    xbc = jax.nn.silu(causal_depthwise_conv(xbc, conv_w, conv_b))
    xs, bm, cm = jnp.split(xbc, [SSM_INNER, SSM_INNER + SSM_GROUPS * SSM_STATE], axis=-1)
    rep = SSM_HEADS // SSM_GROUPS
    xs = xs.reshape(bsz, s, SSM_HEADS, SSM_HEAD_DIM).astype(f32)
    bm = jnp.repeat(bm.reshape(bsz, s, SSM_GROUPS, SSM_STATE).astype(f32), rep, axis=2)
    cm = jnp.repeat(cm.reshape(bsz, s, SSM_GROUPS, SSM_STATE).astype(f32), rep, axis=2)
    dt = jax.nn.softplus(dt.astype(f32) + dt_bias.astype(f32))
    a = -jnp.exp(a_log.astype(f32))
    y = ssd_scan(xs * dt[..., None], dt * a, bm, cm)
    y = (y + xs * d_skip.astype(f32)[:, None]).reshape(bsz, s, SSM_INNER)
    return rms_norm(y * jax.nn.silu(z.astype(f32)), norm_g).astype(z.dtype)


def nsa_mixer(q, kc, vc, ks, vs, kw, vw, gate, q_norm, k_norm,
              pe_k, w1_k, w2_k, pe_v, w1_v, w2_v):
    bsz, s, _ = q.shape
    G, R, E = NSA_KV_GROUPS, NSA_HEADS // NSA_KV_GROUPS, NSA_HEAD_DIM
    f32 = jnp.float32
    n_blk = s // Q_BLOCK
    n_cmp = (s - CMP_BLOCK) // CMP_STRIDE + 1
    n_sel = s // SEL_BLOCK
    top_n = min(SEL_TOP_N, n_sel)
    slopes = jnp.asarray(alibi_slopes(NSA_HEADS).reshape(G, R))

    def kv(t):
        return t.reshape(bsz, s, G, E)

    q = rms_norm(q.reshape(bsz, s, G, R, E), q_norm) * (E ** -0.5)
    ks = rms_norm(kv(ks), k_norm)
    kw = rms_norm(kv(kw), k_norm)

    cidx = np.arange(n_cmp)[:, None] * CMP_STRIDE + np.arange(CMP_BLOCK)[None, :]

    def compress(k, pe, w1, w2):
        blk = k[:, cidx] + pe[:, None, :]
        blk = blk.transpose(0, 1, 3, 2, 4).reshape(bsz, n_cmp, G, CMP_BLOCK * E)
        return jax.nn.gelu(blk @ w1) @ w2

    k_cmp = rms_norm(compress(kv(kc), pe_k, w1_k, w2_k), k_norm)
    v_cmp = compress(kv(vc), pe_v, w1_v, w2_v)
    c_start = np.arange(n_cmp) * CMP_STRIDE
    c_last = jnp.asarray(c_start + CMP_BLOCK - 1)
    c_mid = jnp.asarray((c_start + (CMP_BLOCK - 1) / 2.0).astype(np.float32))
    s_start = np.arange(n_sel) * SEL_BLOCK
    overlap = jnp.asarray(((c_start[:, None] <= s_start[None, :] + SEL_BLOCK - 1)
                           & (c_start[:, None] + CMP_BLOCK - 1 >= s_start[None, :])).astype(np.float32))

    ksb = ks.reshape(bsz, n_sel, SEL_BLOCK, G, E).transpose(0, 3, 1, 2, 4)
    vsb = kv(vs).reshape(bsz, n_sel, SEL_BLOCK, G, E).transpose(0, 3, 1, 2, 4)
    kw_pad = jnp.pad(kw, ((0, 0), (WINDOW, 0), (0, 0), (0, 0)))
    vw_pad = jnp.pad(kv(vw), ((0, 0), (WINDOW, 0), (0, 0), (0, 0)))
    bi = jnp.arange(bsz)[:, None, None, None]
    gi = jnp.arange(G)[None, :, None, None]
    sel_ids = jnp.arange(n_sel)

    def block(args):
        j, qb, gb = args
        t = j * Q_BLOCK + jnp.arange(Q_BLOCK)
        tf = t.astype(f32)
        mask_c = c_last[None, :] <= t[:, None]
        s_c = jnp.einsum("bqgrd,bcgd->bgrqc", qb, k_cmp).astype(f32) \
            - slopes[:, :, None, None] * (tf[:, None] - c_mid[None, :])
        p_c = jax.nn.softmax(jnp.where(mask_c, s_c, NEG), axis=-1) * mask_c
        o_c = jnp.einsum("bgrqc,bcgd->bqgrd", p_c.astype(v_cmp.dtype), v_cmp)
        imp = jnp.einsum("bgrqc,cn->bgqn", p_c, overlap)
        cur = (t // SEL_BLOCK)[:, None]
        forced = (sel_ids == 0) | (sel_ids == cur) | (sel_ids == cur - 1)
        valid = sel_ids * SEL_BLOCK <= t[:, None]
        score = jnp.where(forced, FORCE, jnp.where(valid, imp, NEG))
        _, idx = lax.top_k(score, top_n)
        kb = ksb[bi, gi, idx]
        vb = vsb[bi, gi, idx]
        pos = idx[..., None] * SEL_BLOCK + jnp.arange(SEL_BLOCK)
        rel_s = t[None, None, :, None, None] - pos
        s_s = jnp.einsum("bqgrd,bgqnld->bgrqnl", qb, kb).astype(f32) \
            - slopes[:, :, None, None, None] * rel_s[:, :, None].astype(f32)
        s_s = jnp.where((rel_s >= 0)[:, :, None], s_s, NEG).reshape(bsz, G, R, Q_BLOCK, top_n * SEL_BLOCK)
        p_s = jax.nn.softmax(s_s, axis=-1).reshape(bsz, G, R, Q_BLOCK, top_n, SEL_BLOCK)
        o_s = jnp.einsum("bgrqnl,bgqnld->bqgrd", p_s.astype(vb.dtype), vb)
        kwin = lax.dynamic_slice_in_dim(kw_pad, j * Q_BLOCK, Q_BLOCK + WINDOW, axis=1)
        vwin = lax.dynamic_slice_in_dim(vw_pad, j * Q_BLOCK, Q_BLOCK + WINDOW, axis=1)
        kpos = j * Q_BLOCK - WINDOW + jnp.arange(Q_BLOCK + WINDOW)
        rel_w = t[:, None] - kpos[None, :]
        mask_w = (rel_w >= 0) & (rel_w < WINDOW) & (kpos[None, :] >= 0)
        s_w = jnp.einsum("bqgrd,bkgd->bgrqk", qb, kwin).astype(f32) \
            - slopes[:, :, None, None] * rel_w.astype(f32)
        p_w = jax.nn.softmax(jnp.where(mask_w, s_w, NEG), axis=-1)
        o_w = jnp.einsum("bgrqk,bkgd->bqgrd", p_w.astype(vwin.dtype), vwin)
        return gb[..., 0:1] * o_c + gb[..., 1:2] * o_s + gb[..., 2:3] * o_w

    qbs = q.reshape(bsz, n_blk, Q_BLOCK, G, R, E).swapaxes(0, 1)
    gbs = jax.nn.sigmoid(gate).reshape(bsz, n_blk, Q_BLOCK, G, R, 3).swapaxes(0, 1)
    out = lax.map(block, (jnp.arange(n_blk), qbs, gbs))
    return out.swapaxes(0, 1).reshape(bsz, s, NSA_WIDTH)


def conv_module(ab, dw_w, dw_b, norm_g):
    a, b = jnp.split(ab, 2, axis=-1)
    y = causal_depthwise_conv(a * jax.nn.sigmoid(b), dw_w, dw_b)
    return jax.nn.silu(rms_norm(y, norm_g))


def setup_inputs(seed: int = 0) -> dict:
    key = jax.random.key(seed)
    keys = jax.random.split(key, 48)
    count = [0]
    L = DEPTH

    def nxt():
        k = keys[count[0]]
        count[0] += 1
        return k

    def nrm(shape, scale):
        return jax.random.normal(nxt(), shape, jnp.float32) * scale

    def gain(n):
        return 1.0 + nrm((L, n), 0.02)

    dt0 = jnp.exp(jax.random.uniform(nxt(), (L, SSM_HEADS), jnp.float32, math.log(1e-3), math.log(1e-1)))
    cmp_in = CMP_BLOCK * NSA_HEAD_DIM
    return {
        "x": nrm((BATCH, SEQ, D_MODEL), 1.0),
        "ffn1_norm": gain(D_MODEL),
        "ffn1_w_in": nrm((L, D_MODEL, 2 * FFN_DIM), D_MODEL ** -0.5),
        "ffn1_w_out": nrm((L, FFN_DIM, D_MODEL), FFN_DIM ** -0.5),
        "mix_norm": gain(D_MODEL),
        "w_in": nrm((L, D_MODEL, IN_PROJ_DIM), D_MODEL ** -0.5),
        "sgu_v_norm": gain(SGU_WIDTH),
        "sgu_w": nrm((L, SGU_GROUPS, SGU_CHUNK, SGU_CHUNK), SGU_CHUNK ** -0.5),
        "sgu_b": 1.0 + nrm((L, SGU_GROUPS, SGU_CHUNK), 0.1),
        "ssm_conv_w": nrm((L, SSM_CONV, SSM_CONV_DIM), SSM_CONV ** -0.5),
        "ssm_conv_b": nrm((L, SSM_CONV_DIM), 0.01),
        "ssm_dt_bias": dt0 + jnp.log(-jnp.expm1(-dt0)),
        "ssm_a_log": jnp.log(jax.random.uniform(nxt(), (L, SSM_HEADS), jnp.float32, 1.0, 16.0)),
        "ssm_d": 1.0 + nrm((L, SSM_HEADS), 0.1),
        "ssm_norm": gain(SSM_INNER),
        "nsa_q_norm": gain(NSA_HEAD_DIM),
        "nsa_k_norm": gain(NSA_HEAD_DIM),
        "nsa_pe_k": nrm((L, CMP_BLOCK, NSA_HEAD_DIM), 0.5),
        "nsa_w1_k": nrm((L, cmp_in, CMP_HIDDEN), cmp_in ** -0.5),
        "nsa_w2_k": nrm((L, CMP_HIDDEN, NSA_HEAD_DIM), CMP_HIDDEN ** -0.5),
        "nsa_pe_v": nrm((L, CMP_BLOCK, NSA_HEAD_DIM), 0.5),
        "nsa_w1_v": nrm((L, cmp_in, CMP_HIDDEN), cmp_in ** -0.5),
        "nsa_w2_v": nrm((L, CMP_HIDDEN, NSA_HEAD_DIM), CMP_HIDDEN ** -0.5),
        "conv_dw_w": nrm((L, CONV_KERNEL, CONV_WIDTH), CONV_KERNEL ** -0.5),
        "conv_dw_b": nrm((L, CONV_WIDTH), 0.01),
        "conv_norm": gain(CONV_WIDTH),
        "w_branch": nrm((L, N_BRANCH, BRANCH_WIDTH, D_MODEL), BRANCH_WIDTH ** -0.5),
        "w_out": nrm((L, D_MODEL, D_MODEL), D_MODEL ** -0.5),
        "ffn2_norm": gain(D_MODEL),
        "ffn2_w_in": nrm((L, D_MODEL, 2 * FFN_DIM), D_MODEL ** -0.5),
        "ffn2_w_out": nrm((L, FFN_DIM, D_MODEL), FFN_DIM ** -0.5),
    }


def reference(x, ffn1_norm, ffn1_w_in, ffn1_w_out, mix_norm, w_in,
              sgu_v_norm, sgu_w, sgu_b,
              ssm_conv_w, ssm_conv_b, ssm_dt_bias, ssm_a_log, ssm_d, ssm_norm,
              nsa_q_norm, nsa_k_norm, nsa_pe_k, nsa_w1_k, nsa_w2_k, nsa_pe_v, nsa_w1_v, nsa_w2_v,
              conv_dw_w, conv_dw_b, conv_norm,
              w_branch, w_out, ffn2_norm, ffn2_w_in, ffn2_w_out):
    bsz, s, _ = x.shape
    split_at = [int(v) for v in np.cumsum(SPLIT_SIZES)[:-1]]
    for l in range(DEPTH):
        x = x + 0.5 * swiglu(rms_norm(x, ffn1_norm[l]), ffn1_w_in[l], ffn1_w_out[l])
        h = rms_norm(x, mix_norm[l])
        (a_uv, b_z, b_xbc, b_dt, c_q, c_kc, c_vc, c_ks, c_vs, c_kw, c_vw, c_gate,
         d_ab, merge_logits) = jnp.split(h @ w_in[l], split_at, axis=-1)
        y_a = sgu_mixer(a_uv, sgu_v_norm[l], sgu_w[l], sgu_b[l])
        y_b = mamba2_mixer(b_z, b_xbc, b_dt, ssm_conv_w[l], ssm_conv_b[l], ssm_dt_bias[l],
                           ssm_a_log[l], ssm_d[l], ssm_norm[l])
        y_c = nsa_mixer(c_q, c_kc, c_vc, c_ks, c_vs, c_kw, c_vw, c_gate, nsa_q_norm[l], nsa_k_norm[l],
                        nsa_pe_k[l], nsa_w1_k[l], nsa_w2_k[l], nsa_pe_v[l], nsa_w1_v[l], nsa_w2_v[l])
        y_d = conv_module(d_ab, conv_dw_w[l], conv_dw_b[l], conv_norm[l])
        ys = jnp.stack([y_a, y_b, y_c, y_d], axis=2)
        branch = jnp.einsum("bsie,ied->bsid", ys, w_branch[l])
        gates = jax.nn.sigmoid(merge_logits.reshape(bsz, s, N_BRANCH, D_MODEL))
        x = x + jnp.sum(gates * branch, axis=2) @ w_out[l]
        x = x + 0.5 * swiglu(rms_norm(x, ffn2_norm[l]), ffn2_w_in[l], ffn2_w_out[l])
    return x
```

```python
import math
from contextlib import ExitStack

import numpy as np
import concourse.bass as bass
import concourse.mybir as mybir
from concourse.bass_utils import run_bass_kernel_spmd

F32 = mybir.dt.float32
BF16 = mybir.dt.bfloat16
AF = mybir.ActivationFunctionType
ALU = mybir.AluOpType
AX = mybir.AxisListType

D = 1024
S = 2048
DEPTH = 4
FFN = 2816
NKC = D // 128
NTT = S // 512
EPS = 1e-6
IN_PROJ = 8992


class Buf:
    def __init__(self, ap, name):
        self.ap = ap
        self.name = name
        self.st = {}

    def __getitem__(self, idx):
        return self.ap[idx]


class Eng:
    def __init__(self, name, sem):
        self.name = name
        self.sem = sem
        self.count = 0
        self.waited = {}
        self.ops = []


class Sched:
    def __init__(self, nc, stack):
        self.nc = nc
        self.stack = stack
        self.eng = {}
        for n in ("pe", "dve", "act", "pool", "sp"):
            self.eng[n] = Eng(n, stack.enter_context(nc.semaphore("s_" + n)))
        self.dsems = {}
        self.psum_banks = []
        self.psum_rr = 0
        self.n_ops = 0
        self.marks = []

    def sbuf(self, name, shape, dtype):
        t = self.stack.enter_context(self.nc.sbuf_tensor("sb_" + name, list(shape), dtype))
        return Buf(t[:], name)

    def view(self, ap, name):
        return Buf(ap, name)

    def dsem(self, key):
        if key not in self.dsems:
            self.dsems[key] = [self.stack.enter_context(self.nc.semaphore("d_" + str(key))), 0]
        return self.dsems[key]

    @staticmethod
    def _norm(acc):
        out = []
        for a in acc:
            if isinstance(a, tuple):
                out.append(a)
            else:
                out.append((a, None))
        return out

    @staticmethod
    def _states(buf, key):
        if key is None:
            return list(buf.st.values())
        r = []
        if key in buf.st:
            r.append(buf.st[key])
        if None in buf.st:
            r.append(buf.st[None])
        return r

    def _collect(self, engname, reads, writes):
        deps = []
        for (b, k) in reads:
            for st in self._states(b, k):
                if st[0] is not None:
                    deps.append((st[0], "raw"))
        for (b, k) in writes:
            for st in self._states(b, k):
                if st[0] is not None:
                    deps.append((st[0], "waw"))
                for t in st[1].values():
                    deps.append((t, "war"))
        return deps

    def _emit_waits(self, e, deps):
        for (tok, kind) in deps:
            sem, val, teng = tok[0], tok[1], tok[2]
            if teng is None:
                val = max(val, tok[3][1])
            if teng == e.name:
                if e.name == "pe":
                    continue
                if kind != "raw":
                    continue
            sid = id(sem)
            if e.waited.get(sid, 0) >= val:
                continue
            e.waited[sid] = val
            e.ops.append(("wait", sem, val))

    def _update(self, tok, reads, writes):
        sid = id(tok[0])
        for (b, k) in writes:
            if k is None:
                b.st = {None: [tok, {}]}
            else:
                b.st[k] = [tok, {}]
        for (b, k) in reads:
            st = b.st.setdefault(k, [None, {}])
            old = st[1].get(sid)
            if old is None or old[1] < tok[1]:
                st[1][sid] = tok

    def op(self, engname, fn, reads=(), writes=()):
        e = self.eng[engname]
        reads = self._norm(reads)
        writes = self._norm(writes)
        self._emit_waits(e, self._collect(engname, reads, writes))
        e.count += 1
        tok = (e.sem, e.count, e.name)
        e.ops.append(("op", fn, e.sem, 1))
        self._update(tok, reads, writes)
        self.n_ops += 1
        return tok

    def dma(self, engname, out, in_, reads=(), writes=(), skey=None, slow=False):
        e = self.eng[engname]
        reads = self._norm(reads)
        writes = self._norm(writes)
        self._emit_waits(e, self._collect(engname, reads, writes))
        ds = self.dsem(skey)
        if ds[1] > 0 and e.waited.get(id(ds[0]), 0) < ds[1]:
            e.waited[id(ds[0])] = ds[1]
            e.ops.append(("wait", ds[0], ds[1]))
        ds[1] += 16
        tok = (ds[0], ds[1], None, ds)
        if slow:
            e.ops.append(("op", lambda h: h.dma_start(out=out, in_=in_, allow_slow_non_contiguous=True), ds[0], 16))
        else:
            e.ops.append(("op", lambda h: h.dma_start(out=out, in_=in_), ds[0], 16))
        self._update(tok, reads, writes)
        self.n_ops += 1
        return tok

    def mark(self, name):
        self.marks.append((name, self.eng["pe"].count))

    def barrier(self):
        toks = [(e.sem, e.count, "*") for e in self.eng.values() if e.count > 0]
        toks += [(ds[0], ds[1], None, ds) for ds in self.dsems.values() if ds[1] > 0]
        for e in self.eng.values():
            for tk in toks:
                sem, val = tk[0], tk[1]
                if sem is e.sem and e.name != "sp":
                    pass
                sid = id(sem)
                if e.waited.get(sid, 0) >= val:
                    continue
                e.waited[sid] = val
                e.ops.append(("wait", sem, val))

    def make_psum(self):
        for i in range(8):
            t = self.stack.enter_context(self.nc.psum_tensor("ps%d" % i, [128, 512], F32))
            self.psum_banks.append(Buf(t[:], "ps%d" % i))
        self.psum_rot = list(self.psum_banks)

    def psum(self):
        b = self.psum_rot[self.psum_rr % len(self.psum_rot)]
        self.psum_rr += 1
        return b

    def reserve(self, n):
        got = self.psum_rot[-n:]
        self.psum_rot = self.psum_rot[:-n]
        return got

    def release(self, banks):
        self.psum_rot = self.psum_rot + list(banks)

    def mm(self, out, lhsT, rhs, start, stop, reads, writes):
        return self.op("pe", lambda h: h.matmul(out, lhsT, rhs, start=start, stop=stop), reads, writes)

    def transpose(self, out, in_, ident, reads, writes):
        return self.op("pe", lambda h: h.transpose(out, in_, ident), reads, writes)

    def act(self, out, in_, func, reads, writes, bias=0.0, scale=1.0, eng="act"):
        return self.op(eng, lambda h: h.activation(out, in_, func, bias=bias, scale=scale), reads, writes)

    def tt(self, eng, out, in0, in1, op, reads, writes):
        return self.op(eng, lambda h: h.tensor_tensor(out, in0, in1, op), reads, writes)

    def ts(self, eng, out, in0, s1, s2, op0, op1, reads, writes):
        return self.op(eng, lambda h: h.tensor_scalar(out, in0, s1, s2, op0, op1), reads, writes)

    def stt(self, eng, out, in0, scalar, in1, op0, op1, reads, writes):
        return self.op(eng, lambda h: h.scalar_tensor_tensor(out, in0, scalar, in1, op0, op1), reads, writes)

    def copy(self, eng, out, in_, reads, writes):
        return self.op(eng, lambda h: h.tensor_copy(out, in_), reads, writes)

    def emit(self, final_waits):
        nc = self.nc
        hmap = {"pe": "tensor", "dve": "vector", "act": "scalar", "pool": "gpsimd", "sp": "sync"}
        with nc.Block() as block:
            for n, e in self.eng.items():
                ops = list(e.ops)
                if n == "sp":
                    ops = ops + [("wait", s, v) for (s, v) in final_waits]

                def body(h, ops=ops):
                    for o in ops:
                        if o[0] == "wait":
                            h.wait_ge(o[1], o[2])
                        else:
                            o[1](h).then_inc(o[2], o[3])

                getattr(block, hmap[n])(body)


class WRing:
    def __init__(self, sc, name, nslots, elems):
        self.sc = sc
        self.slots = [sc.sbuf("%s%d" % (name, i), [128, elems], BF16) for i in range(nslots)]
        self.n = nslots
        self.plan = []
        self.issued = 0
        self.name = name

    def add(self, parts):
        self.plan.append(parts)
        return len(self.plan) - 1

    def _issue(self, hi):
        hi = min(len(self.plan), hi)
        while self.issued < hi:
            j = self.issued
            slot = self.slots[j % self.n]
            for pi, (vf, src) in enumerate(self.plan[j]):
                self.sc.dma("pool", vf(slot.ap), src, reads=(), writes=[slot],
                            skey=(self.name, j % self.n))
            self.issued += 1

    def done(self, j):
        self._issue(j + self.n + 1)

    def get(self, i, live=None):
        if live is None:
            self._issue(i + 1)
        else:
            self._issue(max(i + 1, i - live + 1 + self.n))
        return self.slots[i % self.n]


C_A_U, C_A_V = 0, 512
C_B_Z, C_B_X, C_B_B, C_B_C, C_B_DT = 1024, 1536, 2048, 2304, 2560
C_C_Q, C_C_KC, C_C_VC, C_C_KS, C_C_VS, C_C_KW, C_C_VW, C_C_G = 2568, 3080, 3208, 3336, 3464, 3592, 3720, 3848
C_D_A, C_D_B = 3872, 4384
C_MERGE = 4896
NEGM = -30000.0

W_NAMES = [("ffn1_norm", [D]), ("ffn1_w_in", [D, 2 * FFN]), ("ffn1_w_out", [FFN, D]),
           ("mix_norm", [D]), ("w_in", [D, IN_PROJ]),
           ("sgu_v_norm", [512]), ("sgu_w", [4, 128, 128]), ("sgu_b", [4, 128]),
           ("ssm_conv_w", [4, 1024]), ("ssm_conv_b", [1024]), ("ssm_dt_bias", [8]), ("ssm_a_log", [8]),
           ("ssm_d", [8]), ("ssm_norm", [512]),
           ("nsa_q_norm", [64]), ("nsa_k_norm", [64]), ("nsa_pe_k", [32, 64]), ("nsa_w1_k", [2048, 128]),
           ("nsa_w2_k", [128, 64]), ("nsa_pe_v", [32, 64]), ("nsa_w1_v", [2048, 128]), ("nsa_w2_v", [128, 64]),
           ("conv_dw_w", [31, 512]), ("conv_dw_b", [512]), ("conv_norm", [512]),
           ("w_branch", [4, 512, 1024]), ("w_out", [D, D]),
           ("ffn2_norm", [D]), ("ffn2_w_in", [D, 2 * FFN]), ("ffn2_w_out", [FFN, D])]


def host_consts():
    c = np.zeros((128, 6, 128), np.float32)
    r = np.arange(128)
    c[:, 0, :] = np.eye(128)
    c[:, 1, :] = (r[None, :] <= r[:, None])
    c[:, 2, :] = np.where(r[None, :] >= r[:, None], 0.0, NEGM)
    c[:, 3, :] = (r[:, None] <= r[None, :])
    c[:, 4, :] = 1.0
    c[:, 5, :] = 16.0 * r[:, None] - r[None, :]
    return c


def host_nsa_consts():
    t = np.arange(S)
    hi, lo = (t // 64).astype(np.float32), (t % 64).astype(np.float32)
    slopes = np.array([2.0 ** (-(i + 1)) for i in range(8)], np.float32)
    qal = np.zeros((4, 2, 16, 4, 128), np.float32)
    for h in range(8):
        g, r = h // 4, h % 4
        sl = slopes[h]
        qal[0, g, :, r, :] = (-sl * 64.0 * hi).reshape(16, 128)
        qal[1, g, :, r, :] = (-sl * lo).reshape(16, 128)
        qal[2, g, :, r, :] = sl * 64.0
        qal[3, g, :, r, :] = sl
    kal = np.stack([np.ones(S, np.float32), np.ones(S, np.float32), hi, lo], 0)
    cm = np.arange(128) * 16 + 15.5
    kalc = np.stack([np.ones(128), np.ones(128), np.floor(cm / 64.0), cm - 64.0 * np.floor(cm / 64.0)], 0).astype(np.float32)
    onehot = (np.arange(32)[:, None] == (t // 64)[None, :]).astype(np.float32)
    r = np.arange(128)
    mc = np.where(r[None, :] >= r[:, None], 0.0, NEGM).astype(np.float32)
    mf = np.where(r[:, None] > r[None, :], 0.0, NEGM).astype(np.float32)
    c_start = np.arange(128) * 16
    s_start = np.arange(32) * 64
    ov = ((c_start[:, None] <= s_start[None, :] + 63) & (c_start[:, None] + 31 >= s_start[None, :])).astype(np.float32)
    selp = np.zeros((128, 2, 63), np.float32)
    for npr in range(63):
        d = npr - 31
        for q in range(128):
            up = q >= 64
            if d < -1 or (d == -1 and up):
                selp[q, 0, npr], selp[q, 1, npr] = 1.0, 0.0
            elif (d == -1 and not up) or d == 0 or (d == 1 and up):
                selp[q, 0, npr], selp[q, 1, npr] = 0.0, 1e9
            else:
                selp[q, 0, npr], selp[q, 1, npr] = 0.0, -1e9
    return {"nc_qal": qal.reshape(4, 2, 8192), "nc_kal": kal, "nc_kalc": kalc, "nc_onehot": onehot,
            "nc_mc": np.tile(mc, (1, 4)), "nc_mf": np.tile(mf, (1, 4)), "nc_ov": ov, "nc_selp": selp}


_NSAC = host_nsa_consts()


def build_program(cfg):
    depth = cfg.get("depth", DEPTH)
    mixers = cfg.get("mixers", "ABCD")
    nc = bass.Bass("TRN2", target_bir_lowering=False)

    def din(name, shape, dtype=F32):
        return nc.dram_tensor(name, list(shape), dtype, kind="ExternalInput").ap()

    x_d = din("x", [S, D])
    cst_d = din("cst", [128, 6, 128])
    nsa_c = {k: din(k, list(v.shape)) for k, v in _NSAC.items()}
    w_d = {}
    for nm, shp in W_NAMES:
        w_d[nm] = din(nm, [depth] + shp)
    out_d = nc.dram_tensor("out", [S, D], F32, kind="ExternalOutput").ap()
    xspill_d = nc.dram_tensor("xspill", [128, NKC * S], F32, kind="Internal").ap()

    with ExitStack() as stack:
        sc = Sched(nc, stack)
        sc.make_psum()

        xT = sc.sbuf("xT", [128, NKC, S], F32)
        hT = sc.sbuf("hT", [128, NKC, S], BF16)
        mT = sc.sbuf("mT", [128, NKC, S], BF16)
        actT = sc.sbuf("actT", [128, 4, S], BF16)
        yT = actT
        cst = sc.sbuf("cst", [128, 6, 128], F32)
        ident = Buf(cst.ap[:, 0, :], "ident")
        ident_bf = sc.sbuf("ident_bf", [128, 128], BF16)
        ones_bf = sc.sbuf("ones_bf", [128, 128], BF16)
        gains = sc.sbuf("gains", [128, 3 * DEPTH, NKC], F32)
        iobuf = [sc.sbuf("io%d" % i, [128, D], F32) for i in range(2)]
        tmp_a = [sc.sbuf("tmpa%d" % i, [128, 512], F32) for i in range(3)]
        sq_bf = [sc.sbuf("sq%d" % i, [128, 512], BF16) for i in range(2)]
        rstd = sc.sbuf("rstd", [128, 512], F32)
        small = sc.sbuf("small", [128, 64], F32)
        selp_sb = sc.sbuf("selp", [128, 126], F32)
        ring = WRing(sc, "wr", 4, 4096)
        tmp_rr = [0]

        def tmpf():
            tmp_rr[0] += 1
            return tmp_a[tmp_rr[0] % 3]

        xflat = xT.ap.rearrange("p c t -> p (c t)")
        xflat_bf = xflat.bitcast(BF16)

        def scr(name, off, shape, dtype):
            n = int(np.prod(shape[1:]))
            if dtype == F32:
                assert off % 4 == 0
                ap = xflat[0:shape[0], off // 4: off // 4 + n]
            else:
                ap = xflat_bf[0:shape[0], off // 2: off // 2 + n]
            if len(shape) == 3:
                ap = ap.rearrange("p (a b) -> p a b", a=shape[1])
            return Buf(ap, name)

        sc.dma("sp", cst.ap, cst_d, writes=[cst], skey="c0")
        sc.op("dve", lambda h: h.memset(ones_bf.ap, 1.0), writes=[ones_bf])
        sc.dma("sp", selp_sb.ap, nsa_c["nc_selp"].rearrange("p a n -> p (a n)"), writes=[selp_sb], skey="c0")
        sc.copy("dve", ident_bf.ap, ident.ap, reads=[cst], writes=[ident_bf])
        for l in range(depth):
            for wi, nm in enumerate(("ffn1_norm", "ffn2_norm", "mix_norm")):
                sc.dma("sp", gains.ap[:, 3 * l + wi, :], w_d[nm][l].rearrange("(c p) -> p c", p=128),
                       writes=[(gains, (l, wi))], skey="c1", slow=True)

        for t in range(S // 128):
            io = iobuf[t % 2]
            sc.dma("sp", io.ap, x_d[t * 128:(t + 1) * 128, :], writes=[io], skey=("io", t % 2))
            for half in range(2):
                ps = sc.psum()
                for j in range(4):
                    c = half * 4 + j
                    sc.transpose(ps.ap[:, j * 128:(j + 1) * 128], io.ap[:, c * 128:(c + 1) * 128], ident.ap,
                                 reads=[io, cst], writes=[(ps, j)])
                sc.copy("dve", xT.ap[:, half * 4:half * 4 + 4, t * 128:(t + 1) * 128],
                        ps.ap.rearrange("p (c t) -> p c t", c=4),
                        reads=[ps], writes=[(xT, (half * 4 + j, t // 4)) for j in range(4)])

        RS_LN = set(cfg.get("rs_ln", ()))

        def rstd_op(site, o_ap, i_ap, ibuf, obuf, scale):
            if site in RS_LN:
                sc.act(o_ap, i_ap, AF.Ln, reads=[ibuf], writes=[obuf], bias=EPS, scale=scale)
                sc.act(o_ap, o_ap, AF.Exp, reads=[obuf], writes=[obuf], scale=-0.5)
            else:
                sc.act(o_ap, i_ap, AF.Sqrt, reads=[ibuf], writes=[obuf], bias=EPS, scale=scale)
                sc.op("dve", lambda h: h.reciprocal(o_ap, o_ap), reads=[obuf], writes=[obuf])

        def rmsnorm_to_hT(gidx):
            for tt in range(NTT):
                tsl = slice(tt * 512, (tt + 1) * 512)
                ps = sc.psum()
                for c in range(NKC):
                    sq = sq_bf[c % 2]
                    sc.act(sq.ap, xT.ap[:, c, tsl], AF.Square, reads=[(xT, (c, tt))], writes=[sq])
                    sc.mm(ps.ap, ones_bf.ap, sq.ap, c == 0, c == NKC - 1, reads=[ones_bf, sq], writes=[ps])
                rstd_op("hT", rstd.ap, ps.ap, ps, rstd, 1.0 / D)
                for c in range(NKC):
                    sc.stt("dve", hT.ap[:, c, tsl], xT.ap[:, c, tsl],
                           gains.ap[:, gidx, c:c + 1], rstd.ap, ALU.mult, ALU.mult,
                           reads=[(xT, (c, tt)), gains, rstd], writes=[(hT, (c, tt))])

        def wview8(slot, cw):
            return slot.ap[:, 0:8 * cw].rearrange("p (k c) -> p k c", k=8)

        def add_win(l, c0, cw):
            return ring.add([(lambda s, cw=cw: s[:, 0:8 * cw].rearrange("p (k c) -> p k c", k=8),
                              w_d["w_in"][l][:, c0:c0 + cw].rearrange("(k p) c -> p k c", p=128))])

        def inproj_fm(ps_ap, wslot, wv, col0, tok_sl, tt):
            for k in range(NKC):
                sc.mm(ps_ap, wv[:, k, col0:col0 + 128], hT.ap[:, k, tok_sl], k == 0, k == NKC - 1,
                      reads=[wslot, (hT, (k, tt))], writes=[])

        def ffn(l, which, base=None, plan_only=False):
            w_in = w_d["ffn%d_w_in" % which][l]
            w_out = w_d["ffn%d_w_out" % which][l]
            groups = [(f0, min(4, 22 - f0)) for f0 in range(0, 22, 4)]
            if base is not None:
                rmsnorm_to_hT(3 * l + (which - 1))
            for (f0, nf) in (groups if base is None else []):
                cw = nf * 128
                for off in (0, FFN):
                    ring.add([(lambda s, cw=cw: s[:, 0:8 * cw].rearrange("p (k c) -> p k c", k=8),
                               w_in[:, off + f0 * 128: off + f0 * 128 + cw].rearrange("(k p) c -> p k c", p=128))])
                ring.add([(lambda s, nf=nf: s[:, 0:nf * 1024].rearrange("p (f c) -> p f c", f=nf),
                           w_out[f0 * 128:(f0 + nf) * 128, :].rearrange("(f p) c -> p f c", p=128))])
            if base is None:
                base = len(ring.plan) - 3 * len(groups)
                if plan_only:
                    return base
                rmsnorm_to_hT(3 * l + (which - 1))
            for gi, (f0, nf) in enumerate(groups):
                cw = nf * 128
                wg = ring.get(base + 3 * gi, live=1)
                wu = ring.get(base + 3 * gi + 1, live=2)
                wgv = wview8(wg, cw)
                wuv = wview8(wu, cw)
                for tt in range(NTT):
                    tsl = slice(tt * 512, (tt + 1) * 512)
                    for fi in range(nf):
                        pg = sc.psum()
                        pu = sc.psum()
                        for k in range(NKC):
                            sc.mm(pg.ap, wgv[:, k, fi * 128:(fi + 1) * 128], hT.ap[:, k, tsl], k == 0, k == NKC - 1,
                                  reads=[wg, (hT, (k, tt))], writes=[pg])
                        for k in range(NKC):
                            sc.mm(pu.ap, wuv[:, k, fi * 128:(fi + 1) * 128], hT.ap[:, k, tsl], k == 0, k == NKC - 1,
                                  reads=[wu, (hT, (k, tt))], writes=[pu])
                        tm = tmpf()
                        sc.act(tm.ap, pg.ap, AF.Silu, reads=[pg], writes=[tm])
                        sc.tt("dve", actT.ap[:, fi, tsl], tm.ap, pu.ap, ALU.mult,
                              reads=[tm, pu], writes=[(actT, (fi, tt))])
                wo = ring.get(base + 3 * gi + 2, live=1)
                wov = wo.ap[:, 0:nf * 1024].rearrange("p (f c) -> p f c", f=nf)
                for tt in range(NTT):
                    tsl = slice(tt * 512, (tt + 1) * 512)
                    for dc in range(NKC):
                        po = sc.psum()
                        for fi in range(nf):
                            sc.mm(po.ap, wov[:, fi, dc * 128:(dc + 1) * 128], actT.ap[:, fi, tsl], fi == 0, fi == nf - 1,
                                  reads=[wo, (actT, (fi, tt))], writes=[po])
                        sc.stt("dve", xT.ap[:, dc, tsl], po.ap, 0.5, xT.ap[:, dc, tsl], ALU.mult, ALU.add,
                               reads=[po, (xT, (dc, tt))], writes=[(xT, (dc, tt))])

        def load_T(dst_fn, src2d, rows, nchunk, stage):
            sc.dma("sp", stage.ap[0:rows, 0:nchunk * 128], src2d, writes=[stage], skey="ldT")
            ps = sc.psum()
            for c in range(nchunk):
                sc.transpose(ps.ap[:, c * 128:c * 128 + rows], stage.ap[0:rows, c * 128:(c + 1) * 128],
                             ident.ap[0:rows, 0:rows], reads=[stage, cst], writes=[(ps, c)])
            for c in range(nchunk):
                sc.copy("dve", dst_fn(c), ps.ap[:, c * 128:c * 128 + rows], reads=[ps], writes=[])

        def gelu_psum(p_ap, out_ap, pbuf, outbuf_w):
            n = p_ap.shape[-1] if len(p_ap.shape) == 2 else None
            t1 = tmpf()
            t2 = tmpf()
            a1 = t1.ap[:, 0:512] if n is None else t1.ap[:, 0:n]
            a2 = t2.ap[:, 0:512] if n is None else t2.ap[:, 0:n]
            sc.act(a1, p_ap, AF.Square, reads=[pbuf], writes=[t1])
            sc.ts("dve", a1, a1, 0.044715, 1.0, ALU.mult, ALU.add, reads=[t1], writes=[t1])
            sc.tt("dve", a1, a1, p_ap, ALU.mult, reads=[t1, pbuf], writes=[t1])
            sc.act(a2, a1, AF.Sigmoid, reads=[t1], writes=[t2], scale=1.5957691216057308)
            sc.tt("dve", out_ap, a2, p_ap, ALU.mult, reads=[t2, pbuf], writes=outbuf_w)

        def merge(l, i, first, base=None, plan_only=False):
            if base is None:
                base = len(ring.plan)
                wb_d = w_d["w_branch"][l][i]
                ring.add([(lambda s: s[:, 0:4096].rearrange("p (f c) -> p f c", f=4),
                           wb_d.rearrange("(f p) c -> p f c", p=128))])
                for half in range(2):
                    add_win(l, C_MERGE + i * 1024 + half * 512, 512)
                if plan_only:
                    return base
            wb = ring.get(base, live=1)
            wbv = wb.ap[:, 0:4096].rearrange("p (f c) -> p f c", f=4)
            for half in range(2):
                wl = ring.get(base + 1 + half, live=2 + half)
                wlv = wview8(wl, 512)
                for tt in range(NTT):
                    tsl = slice(tt * 512, (tt + 1) * 512)
                    for j in range(4):
                        dc = half * 4 + j
                        pl = sc.psum()
                        pb = sc.psum()
                        for k in range(NKC):
                            sc.mm(pl.ap, wlv[:, k, j * 128:(j + 1) * 128], hT.ap[:, k, tsl], k == 0, k == NKC - 1,
                                  reads=[wl, (hT, (k, tt))], writes=[pl])
                        for k in range(4):
                            sc.mm(pb.ap, wbv[:, k, dc * 128:(dc + 1) * 128], yT.ap[:, k, tsl], k == 0, k == 3,
                                  reads=[wb, (yT, (k, tt))], writes=[pb])
                        tm = tmpf()
                        sc.act(tm.ap, pl.ap, AF.Sigmoid, reads=[pl], writes=[tm])
                        if first:
                            sc.tt("dve", mT.ap[:, dc, tsl], tm.ap, pb.ap, ALU.mult,
                                  reads=[tm, pb], writes=[(mT, (dc, tt))])
                        else:
                            sc.tt("dve", tm.ap, tm.ap, pb.ap, ALU.mult, reads=[tm, pb], writes=[tm])
                            sc.tt("pool", mT.ap[:, dc, tsl], mT.ap[:, dc, tsl], tm.ap, ALU.add,
                                  reads=[tm, (mT, (dc, tt))], writes=[(mT, (dc, tt))])

        def out_proj(l, base=None, plan_only=False):
            if base is None:
                base = len(ring.plan)
                for half in range(2):
                    ring.add([(lambda s: s[:, 0:4096].rearrange("p (k c) -> p k c", k=8),
                               w_d["w_out"][l][:, half * 512:(half + 1) * 512].rearrange("(k p) c -> p k c", p=128))])
                if plan_only:
                    return base
            for half in range(2):
                wo = ring.get(base + half, live=1)
                wov = wview8(wo, 512)
                for tt in range(NTT):
                    tsl = slice(tt * 512, (tt + 1) * 512)
                    for j in range(4):
                        dc = half * 4 + j
                        po = sc.psum()
                        for k in range(NKC):
                            sc.mm(po.ap, wov[:, k, j * 128:(j + 1) * 128], mT.ap[:, k, tsl], k == 0, k == NKC - 1,
                                  reads=[wo, (mT, (k, tt))], writes=[po])
                        sc.tt("dve", xT.ap[:, dc, tsl], po.ap, xT.ap[:, dc, tsl], ALU.add,
                              reads=[po, (xT, (dc, tt))], writes=[(xT, (dc, tt))])

        def mixer_D(l, base=None, plan_only=False):
            if base is None:
                base = len(ring.plan)
                add_win(l, C_D_A, 512)
                add_win(l, C_D_B, 512)
                if plan_only:
                    return base
            u = scr("D_u", 0, [128, 30 + S], BF16)
            dg = scr("D_dg", 4352, [128, 31, 128], BF16)
            wD = scr("D_w", 12288, [128, 4, 32], F32)
            bg = scr("D_bg", 12800, [128, 8], F32)
            stage = scr("D_stage", 13056, [128, 512], F32)
            load_T(lambda c: wD.ap[:, c, 0:31], w_d["conv_dw_w"][l], 31, 4, stage)
            sc.dma("sp", bg.ap[:, 0:4], w_d["conv_dw_b"][l].rearrange("(c p) -> p c", p=128), writes=[(bg, 0)],
                   skey="c1", slow=True)
            sc.dma("sp", bg.ap[:, 4:8], w_d["conv_norm"][l].rearrange("(c p) -> p c", p=128), writes=[(bg, 1)],
                   skey="c1", slow=True)
            sc.op("dve", lambda h: h.memset(u.ap[:, 0:30], 0.0), writes=[(u, "pad")])
            ssb = sc.reserve(4)
            wa = ring.get(base, live=1)
            wb = ring.get(base + 1, live=2)
            wav, wbv = wview8(wa, 512), wview8(wb, 512)
            for c in range(4):
                sc.tt("dve", dg.ap, ident_bf.ap.unsqueeze(1).to_broadcast([128, 31, 128]),
                      wD.ap[:, c, 0:31].unsqueeze(2).to_broadcast([128, 31, 128]), ALU.mult,
                      reads=[ident_bf, wD], writes=[dg])
                for tt in range(NTT):
                    tsl = slice(tt * 512, (tt + 1) * 512)
                    pa, pb = sc.psum(), sc.psum()
                    for k in range(NKC):
                        sc.mm(pa.ap, wav[:, k, c * 128:(c + 1) * 128], hT.ap[:, k, tsl], k == 0, k == NKC - 1,
                              reads=[wa, (hT, (k, tt))], writes=[pa])
                    for k in range(NKC):
                        sc.mm(pb.ap, wbv[:, k, c * 128:(c + 1) * 128], hT.ap[:, k, tsl], k == 0, k == NKC - 1,
                              reads=[wb, (hT, (k, tt))], writes=[pb])
                    tm = tmpf()
                    sc.act(tm.ap, pb.ap, AF.Sigmoid, reads=[pb], writes=[tm])
                    sc.tt("dve", u.ap[:, 30 + tt * 512: 30 + (tt + 1) * 512], tm.ap, pa.ap, ALU.mult,
                          reads=[tm, pa], writes=[(u, tt)])
                for tt in range(NTT):
                    tsl = slice(tt * 512, (tt + 1) * 512)
                    pcv = sc.psum()
                    for k in range(31):
                        sc.mm(pcv.ap, dg.ap[:, k, :], u.ap[:, tt * 512 + k: tt * 512 + k + 512], k == 0, k == 30,
                              reads=[dg, u], writes=[pcv])
                    sq = sq_bf[tt % 2]
                    sc.act(sq.ap, pcv.ap, AF.Square, reads=[pcv, bg], writes=[sq], bias=bg.ap[:, c:c + 1])
                    sc.mm(ssb[tt].ap, ones_bf.ap, sq.ap, c == 0, c == 3, reads=[ones_bf, sq], writes=[ssb[tt]])
                    sc.act(yT.ap[:, c, tsl], pcv.ap, AF.Identity, reads=[pcv, bg], writes=[(yT, (c, tt))],
                           bias=bg.ap[:, c:c + 1])
            for tt in range(NTT):
                tsl = slice(tt * 512, (tt + 1) * 512)
                rstd_op("D", rstd.ap, ssb[tt].ap, ssb[tt], rstd, 1.0 / 512)
                for c in range(4):
                    tm = tmpf()
                    sc.stt("dve", tm.ap, yT.ap[:, c, tsl], bg.ap[:, 4 + c:5 + c], rstd.ap, ALU.mult, ALU.mult,
                           reads=[(yT, (c, tt)), bg, rstd], writes=[tm])
                    sc.act(yT.ap[:, c, tsl], tm.ap, AF.Silu, reads=[tm], writes=[(yT, (c, tt))])
            sc.release(ssb)

        def mixer_A(l, base=None, plan_only=False):
            if base is None:
                base = len(ring.plan)
                add_win(l, C_A_U, 512)
                add_win(l, C_A_V, 512)
                if plan_only:
                    return base
            stage = scr("A_stage", 0, [128, 512], F32)
            wTs = scr("A_wT", 2048, [128, 4, 128], BF16)
            gbc = scr("A_gbc", 3072, [128, 512], F32)
            brow_f = scr("A_browf", 5120, [1, 512], F32)
            brow = scr("A_brow", 7168, [1, 512], BF16)
            ug = scr("A_ug", 8192, [128, 4, 512], BF16)
            vg2 = [scr("A_vg%d" % i, 12288 + i * 2048, [128, 512], F32) for i in range(2)]
            vn2 = [scr("A_vn%d" % i, 16384 + i * 1024, [128, 512], BF16) for i in range(2)]
            ssv = scr("A_ssv", 18432, [128, 8], F32)
            for g in range(4):
                sc.dma("sp", stage.ap[:, g * 128:(g + 1) * 128], w_d["sgu_w"][l][g], writes=[(stage, g)], skey="ldA")
            sc.tt("dve", stage.ap.rearrange("p (g s) -> p g s", g=4), stage.ap.rearrange("p (g s) -> p g s", g=4),
                  cst.ap[:, 1:2, :].to_broadcast([128, 4, 128]), ALU.mult, reads=[stage, cst], writes=[stage])
            ps = sc.psum()
            for g in range(4):
                sc.transpose(ps.ap[:, g * 128:(g + 1) * 128], stage.ap[:, g * 128:(g + 1) * 128], ident.ap,
                             reads=[stage, cst], writes=[(ps, g)])
            sc.copy("dve", wTs.ap, ps.ap.rearrange("p (g s) -> p g s", g=4), reads=[ps], writes=[wTs])
            sc.dma("sp", gbc.ap, w_d["sgu_v_norm"][l].partition_broadcast(128), writes=[gbc], skey="ldA")
            sc.dma("sp", brow_f.ap, w_d["sgu_b"][l].rearrange("g t -> (g t)")[None, :], writes=[brow_f], skey="ldA")
            sc.copy("dve", brow.ap, brow_f.ap, reads=[brow_f], writes=[brow])
            wu = ring.get(base, live=1)
            wv = ring.get(base + 1, live=2)
            wuv, wvv = wview8(wu, 512), wview8(wv, 512)
            def a_u(tt, g):
                tsl = slice(tt * 512, (tt + 1) * 512)
                pu = sc.psum()
                for k in range(NKC):
                    sc.mm(pu.ap, wuv[:, k, g * 128:(g + 1) * 128], hT.ap[:, k, tsl], k == 0, k == NKC - 1,
                          reads=[wu, (hT, (k, tt))], writes=[pu])
                gelu_psum(pu.ap, ug.ap[:, g, :], pu, [(ug, g)])

            def a_vproj(tt, sub):
                t0 = tt * 512 + sub * 128
                pv = sc.psum()
                for k in range(NKC):
                    sc.mm(pv.ap, hT.ap[:, k, t0:t0 + 128], wvv[:, k, :], k == 0, k == NKC - 1,
                          reads=[wv, (hT, (k, tt))], writes=[pv])
                return pv

            def a_vrest(tt, sub, pv):
                t0 = tt * 512 + sub * 128
                par = sub % 2
                vgp, vnp = vg2[par], vn2[par]
                gelu_psum(pv.ap, vgp.ap, pv, [vgp])
                tj = tmpf()
                so = par * 4
                sc.op("act", lambda h, o=tj.ap, i=vgp.ap, a=ssv.ap[:, so:so + 1]: h.activation(o, i, AF.Square, accum_out=a),
                      reads=[vgp], writes=[tj, (ssv, so)])
                sc.act(ssv.ap[:, so + 1:so + 2], ssv.ap[:, so:so + 1], AF.Sqrt, reads=[(ssv, so)], writes=[(ssv, so + 1)],
                       bias=EPS, scale=1.0 / 512)
                sc.op("dve", lambda h, o=ssv.ap[:, so + 2:so + 3], i=ssv.ap[:, so + 1:so + 2]: h.reciprocal(o, i),
                      reads=[(ssv, so + 1)], writes=[(ssv, so + 2)])
                sc.stt("dve", vnp.ap, vgp.ap, ssv.ap[:, so + 2:so + 3], gbc.ap, ALU.mult, ALU.mult,
                       reads=[vgp, (ssv, so + 2), gbc], writes=[vnp])
                pm = sc.psum()
                for g in range(4):
                    sc.mm(pm.ap[:, g * 128:(g + 1) * 128], vnp.ap[:, g * 128:(g + 1) * 128], wTs.ap[:, g, :], True, False,
                          reads=[vnp, wTs], writes=[(pm, g)])
                    sc.mm(pm.ap[:, g * 128:(g + 1) * 128], ones_bf.ap[0:1, :], brow.ap[0:1, g * 128:(g + 1) * 128],
                          False, True, reads=[ones_bf, brow], writes=[(pm, g)])
                sc.tt("dve", yT.ap[:, :, t0:t0 + 128], ug.ap[:, :, sub * 128:(sub + 1) * 128],
                      pm.ap.rearrange("p (g t) -> p g t", g=4), ALU.mult,
                      reads=[ug, pm], writes=[(yT, (g, tt)) for g in range(4)])

            for tt in range(NTT):
                for g in range(4):
                    a_u(tt, g)
                pend = a_vproj(tt, 0)
                for sub in range(4):
                    nxt = a_vproj(tt, sub + 1) if sub + 1 < 4 else None
                    a_vrest(tt, sub, pend)
                    pend = nxt

        def mixer_B(l, base=None, plan_only=False):
            if base is None:
                base = len(ring.plan)
                add_win(l, C_B_X, 512)
                add_win(l, C_B_B, 512)
                add_win(l, C_B_Z, 512)
                add_win(l, C_B_DT, 8)
                if plan_only:
                    return base
            raw = scr("B_raw", 0, [128, 3 + S], F32)
            xsT = scr("B_xsT", 8448, [128, 4, S], BF16)
            bcT = scr("B_bcT", 24832, [128, 4, S], BF16)
            zsT = scr("B_zsT", 41216, [128, 4, S], BF16)
            o = 57600
            cw = scr("B_cw", o, [128, 8, 4], F32); o += 128
            cb = scr("B_cb", o, [128, 8], F32); o += 32
            dtT = scr("B_dt", o, [128, 16, 8], F32); o += 512
            aT = scr("B_a", o, [128, 16, 8], F32); o += 512
            dtb_bc = scr("B_dtb", o, [128, 8], F32); o += 32
            A_bc = scr("B_A", o, [128, 8], F32); o += 32
            Dp = scr("B_Dp", o, [128, 4], F32); o += 16
            gn = scr("B_gn", o, [128, 4], F32); o += 16
            sm = scr("B_sm", o, [128, 64], F32); o += 256
            xdt = scr("B_xdt", o, [128, 512], BF16); o += 1024
            xdd = scr("B_xdd", o, [128, 512], BF16); o += 1024
            btok = scr("B_btok", o, [128, 256], BF16); o += 512
            Hs = scr("B_H", o, [128, 512], F32); o += 2048
            Hbf = scr("B_Hbf", o, [128, 512], BF16); o += 1024
            assert o <= 65536
            ST = scr("B_ST", 0, [128, 8, 128], BF16)
            CTs = scr("B_CTs", 2048, [128, 8, 128], BF16)
            yz = scr("B_yz", 4096, [128, 512], F32)
            sqb = scr("B_sqb", 6144, [128, 512], BF16)
            rsb = scr("B_rs", 7168, [128, 128], F32)
            rsb2 = [rsb, scr("B_rs1", 7680, [128, 128], F32)]
            stage = iobuf[0]
            atri = Buf(iobuf[0].ap.rearrange("p (h l) -> p h l", h=8), "B_atri")
            argm = Buf(iobuf[1].ap.rearrange("p (h l) -> p h l", h=8), "B_argm")

            for hf in range(2):
                load_T(lambda c, hf=hf: cw.ap[:, hf * 4 + c, 0:4], w_d["ssm_conv_w"][l][:, hf * 512:(hf + 1) * 512], 4, 4,
                       stage)
            sc.dma("sp", cb.ap, w_d["ssm_conv_b"][l].rearrange("(c p) -> p c", p=128), writes=[cb], skey="c1", slow=True)
            sc.dma("sp", gn.ap, w_d["ssm_norm"][l].rearrange("(c p) -> p c", p=128), writes=[gn], skey="c1", slow=True)
            sc.dma("sp", dtb_bc.ap, w_d["ssm_dt_bias"][l].partition_broadcast(128), writes=[dtb_bc], skey="c1")
            sc.dma("sp", A_bc.ap, w_d["ssm_a_log"][l].partition_broadcast(128), writes=[A_bc], skey="c1")
            for c in range(4):
                for hf in range(2):
                    sc.dma("sp", Dp.ap[hf * 64:(hf + 1) * 64, c:c + 1],
                           w_d["ssm_d"][l][2 * c + hf:2 * c + hf + 1].partition_broadcast(64), writes=[(Dp, (c, hf))],
                           skey="c1")
            sc.act(A_bc.ap, A_bc.ap, AF.Exp, reads=[A_bc], writes=[A_bc])
            sc.ts("dve", A_bc.ap, A_bc.ap, -1.0, None, ALU.mult, ALU.bypass, reads=[A_bc], writes=[A_bc])
            sc.op("dve", lambda h: h.memset(raw.ap[:, 0:3], 0.0), writes=[(raw, "pad")])

            for blk in range(2):
                wx = ring.get(base + blk, live=1)
                wxv = wview8(wx, 512)
                for cc in range(4):
                    c = blk * 4 + cc
                    dst = xsT if blk == 0 else bcT
                    for tt in range(NTT):
                        tsl = slice(tt * 512, (tt + 1) * 512)
                        px = sc.psum()
                        for k in range(NKC):
                            sc.mm(px.ap, wxv[:, k, cc * 128:(cc + 1) * 128], hT.ap[:, k, tsl], k == 0, k == NKC - 1,
                                  reads=[wx, (hT, (k, tt))], writes=[px])
                        sc.op("act", lambda h, o=raw.ap[:, 3 + tt * 512:3 + (tt + 1) * 512], i=px.ap: h.copy(o, i),
                              reads=[px], writes=[(raw, tt)])
                        tm = tmpf()
                        sc.ts("dve", tm.ap, raw.ap[:, tt * 512:tt * 512 + 512], cw.ap[:, c, 0:1], cb.ap[:, c:c + 1],
                              ALU.mult, ALU.add, reads=[raw, cw, cb], writes=[tm])
                        for k in range(1, 4):
                            sc.stt("dve", tm.ap, raw.ap[:, tt * 512 + k:tt * 512 + k + 512], cw.ap[:, c, k:k + 1], tm.ap,
                                   ALU.mult, ALU.add, reads=[raw, cw, tm], writes=[tm])
                        sc.act(dst.ap[:, cc, tsl], tm.ap, AF.Silu, reads=[tm], writes=[(dst, (cc, tt))])
            wz = ring.get(base + 2, live=1)
            wzv = wview8(wz, 512)
            for c in range(4):
                for tt in range(NTT):
                    tsl = slice(tt * 512, (tt + 1) * 512)
                    pz = sc.psum()
                    for k in range(NKC):
                        sc.mm(pz.ap, wzv[:, k, c * 128:(c + 1) * 128], hT.ap[:, k, tsl], k == 0, k == NKC - 1,
                              reads=[wz, (hT, (k, tt))], writes=[pz])
                    sc.act(zsT.ap[:, c, tsl], pz.ap, AF.Silu, reads=[pz], writes=[(zsT, (c, tt))])
            wdt = ring.get(base + 3, live=1)
            wdtv = wview8(wdt, 8)
            pdt = sc.psum()
            for ci in range(16):
                for k in range(NKC):
                    sc.mm(pdt.ap[:, ci * 8:(ci + 1) * 8], hT.ap[:, k, ci * 128:(ci + 1) * 128], wdtv[:, k, :], k == 0,
                          k == NKC - 1, reads=[wdt, (hT, (k, ci // 4))], writes=[(pdt, ci)])
            sc.tt("dve", dtT.ap, pdt.ap[:, 0:128].rearrange("p (c h) -> p c h", c=16),
                  dtb_bc.ap.unsqueeze(1).to_broadcast([128, 16, 8]), ALU.add, reads=[pdt, dtb_bc], writes=[dtT])
            sc.act(dtT.ap, dtT.ap, AF.Exp, reads=[dtT], writes=[dtT])
            sc.act(dtT.ap, dtT.ap, AF.Ln, reads=[dtT], writes=[dtT], bias=1.0)
            sc.tt("dve", aT.ap, dtT.ap, A_bc.ap.unsqueeze(1).to_broadcast([128, 16, 8]), ALU.mult,
                  reads=[dtT, A_bc], writes=[aT])

            triU = cst.ap[:, 3, :]
            maskT = cst.ap[:, 2, :]
            ones_f = cst.ap[:, 4, :]
            ST2 = [(ST, ST.ap), (tmp_a[0], tmp_a[0].ap.bitcast(BF16).rearrange("p (h l) -> p h l", h=8))]
            CT2 = [(CTs, CTs.ap), (tmp_a[1], tmp_a[1].ap.bitcast(BF16).rearrange("p (h l) -> p h l", h=8))]
            XDT2 = [(xdt, xdt.ap), (sq_bf[0], sq_bf[0].ap)]
            XDD2 = [(xdd, xdd.ap), (sq_bf[1], sq_bf[1].ap)]
            BTK2 = [(btok, btok.ap), (tmp_a[2], tmp_a[2].ap.bitcast(BF16)[:, 0:256])]

            def front(ci):
                p = ci % 2
                so = p * 32
                tok = slice(ci * 128, (ci + 1) * 128)
                tt = ci // 4
                STb, STa = ST2[p]
                CTb, CTa = CT2[p]
                XTb, XTa = XDT2[p]
                XDb, XDa = XDD2[p]
                BKb, BKa = BTK2[p]
                a_c = aT.ap[:, ci, :]
                pc = sc.psum()
                sc.mm(pc.ap[:, 0:8], triU, a_c, True, True, reads=[cst, aT], writes=[pc])
                negacs = sm.ap[:, so:so + 8]
                sc.ts("dve", negacs, pc.ap[:, 0:8], -1.0, None, ALU.mult, ALU.bypass, reads=[pc],
                      writes=[(sm, ("neg", p))])
                sc.tt("dve", atri.ap, triU.unsqueeze(1).to_broadcast([128, 8, 128]),
                      a_c.unsqueeze(2).to_broadcast([128, 8, 128]), ALU.mult, reads=[cst, aT], writes=[atri])
                pr = [sc.psum(), sc.psum()]
                for hb in range(2):
                    sc.mm(pr[hb].ap, ones_f, atri.ap[:, hb * 4:(hb + 1) * 4, :], True, True,
                          reads=[cst, atri], writes=[pr[hb]])
                for hb in range(2):
                    sc.tt("dve", argm.ap[:, hb * 4:(hb + 1) * 4, :], pr[hb].ap.rearrange("p (h l) -> p h l", h=4),
                          maskT.unsqueeze(1).to_broadcast([128, 4, 128]), ALU.add, reads=[pr[hb], cst],
                          writes=[(argm, hb)])
                for h in range(8):
                    sc.act(argm.ap[:, h, :], argm.ap[:, h, :], AF.Exp, reads=[(argm, h // 4), (sm, ("neg", p))],
                           writes=[(argm, h // 4)], bias=sm.ap[:, so + h:so + h + 1])
                pg = sc.psum()
                for g in range(2):
                    sc.mm(pg.ap[:, g * 128:(g + 1) * 128], bcT.ap[:, g, tok], bcT.ap[:, 2 + g, tok], True, True,
                          reads=[(bcT, (g, tt)), (bcT, (2 + g, tt))], writes=[(pg, g)])
                for g in range(2):
                    sc.tt("dve", STa[:, g * 4:(g + 1) * 4, :], argm.ap[:, g * 4:(g + 1) * 4, :],
                          pg.ap[:, g * 128:(g + 1) * 128].unsqueeze(1).to_broadcast([128, 4, 128]), ALU.mult,
                          reads=[(argm, g), pg], writes=[(STb, g)])
                for hb in range(2):
                    sc.act(atri.ap[:, hb * 4:(hb + 1) * 4, :], pr[hb].ap.rearrange("p (h l) -> p h l", h=4), AF.Exp,
                           reads=[pr[hb]], writes=[atri])
                for g in range(2):
                    sc.tt("dve", CTa[:, g * 4:(g + 1) * 4, :], atri.ap[:, g * 4:(g + 1) * 4, :],
                          bcT.ap[:, 2 + g, tok].unsqueeze(1).to_broadcast([128, 4, 128]), ALU.mult,
                          reads=[atri, (bcT, (2 + g, tt))], writes=[(CTb, g)])
                ptb = sc.psum()
                ptv = ptb.ap.bitcast(BF16)
                for c in range(4):
                    sc.transpose(ptv[:, c * 128:(c + 1) * 128], xsT.ap[:, c, tok], ident_bf.ap,
                                 reads=[(xsT, (c, tt)), ident_bf], writes=[(ptb, c)])
                for g in range(2):
                    sc.transpose(ptv[:, 512 + g * 128:512 + (g + 1) * 128], bcT.ap[:, g, tok], ident_bf.ap,
                                 reads=[(bcT, (g, tt)), ident_bf], writes=[(ptb, 4 + g)])
                dec = sm.ap[:, so + 8:so + 16]
                for hb in range(2):
                    last = pr[hb].ap.rearrange("p (h l) -> p h l", h=4)[:, :, 127]
                    sc.tt("dve", sm.ap[:, so + 8 + hb * 4:so + 12 + hb * 4], last, sm.ap[:, so + hb * 4:so + hb * 4 + 4],
                          ALU.add, reads=[pr[hb], (sm, ("neg", p))], writes=[(sm, ("dec", p))])
                    sc.act(sm.ap[:, so + 16 + hb * 4:so + 20 + hb * 4], last, AF.Exp, reads=[pr[hb]],
                           writes=[(sm, ("eA", p))])
                sc.act(dec, dec, AF.Exp, reads=[(sm, ("dec", p))], writes=[(sm, ("dec", p))])
                sc.tt("dve", sm.ap[:, so + 24:so + 32], dec, dtT.ap[:, ci, :], ALU.mult, reads=[(sm, ("dec", p)), dtT],
                      writes=[(sm, ("dtdec", p))])
                xtok = ptv[:, 0:512].rearrange("p (h e) -> p h e", h=8)
                sc.tt("dve", XTa.rearrange("p (h e) -> p h e", h=8), xtok,
                      dtT.ap[:, ci, :].unsqueeze(2).to_broadcast([128, 8, 64]), ALU.mult, reads=[ptb, dtT], writes=[XTb])
                sc.tt("dve", XDa.rearrange("p (h e) -> p h e", h=8), xtok,
                      sm.ap[:, so + 24:so + 32].unsqueeze(2).to_broadcast([128, 8, 64]), ALU.mult,
                      reads=[ptb, (sm, ("dtdec", p))], writes=[XDb])
                sc.op("act", lambda h, o_=BKa, i_=ptv[:, 512:768]: h.copy(o_, i_), reads=[ptb], writes=[BKb])

            def back(ci):
                p = ci % 2
                so = p * 32
                tok = slice(ci * 128, (ci + 1) * 128)
                tt = ci // 4
                STb, STa = ST2[p]
                CTb, CTa = CT2[p]
                XTb, XTa = XDT2[p]
                XDb, XDa = XDD2[p]
                BKb, BKa = BTK2[p]
                eA = sm.ap[:, so + 16:so + 24]
                py = sc.psum()
                for h in range(8):
                    outp = py.ap[(h % 2) * 64:(h % 2) * 64 + 64, (h // 2) * 128:(h // 2) * 128 + 128]
                    sc.mm(outp, XTa[:, h * 64:(h + 1) * 64], STa[:, h, :], True, ci == 0,
                          reads=[XTb, (STb, h // 4)], writes=[(py, h)])
                    if ci > 0:
                        sc.mm(outp, Hbf.ap[:, h * 64:(h + 1) * 64], CTa[:, h, :], False, True,
                              reads=[Hbf, (CTb, h // 4)], writes=[(py, h)])
                pS = sc.psum()
                for g in range(2):
                    sc.mm(pS.ap[:, g * 256:(g + 1) * 256], BKa[:, g * 128:(g + 1) * 128], XDa[:, g * 256:(g + 1) * 256],
                          True, True, reads=[BKb, XDb], writes=[(pS, g)])
                if ci == 0:
                    sc.copy("dve", Hs.ap, pS.ap, reads=[pS], writes=[Hs])
                else:
                    sc.tt("dve", Hs.ap.rearrange("p (h e) -> p h e", h=8), Hs.ap.rearrange("p (h e) -> p h e", h=8),
                          eA.unsqueeze(2).to_broadcast([128, 8, 64]), ALU.mult, reads=[Hs, (sm, ("eA", p))], writes=[Hs])
                    sc.tt("dve", Hs.ap, Hs.ap, pS.ap, ALU.add, reads=[Hs, pS], writes=[Hs])
                if ci < 15:
                    sc.copy("pool", Hbf.ap, Hs.ap, reads=[Hs], writes=[Hbf])
                for c in range(4):
                    sc.stt("dve", yz.ap[:, c * 128:(c + 1) * 128], xsT.ap[:, c, tok], Dp.ap[:, c:c + 1],
                           py.ap[:, c * 128:(c + 1) * 128], ALU.mult, ALU.add,
                           reads=[(xsT, (c, tt)), Dp, py], writes=[(yz, c)])
                sc.tt("dve", yz.ap.rearrange("p (c t) -> p c t", c=4), yz.ap.rearrange("p (c t) -> p c t", c=4),
                      zsT.ap[:, :, tok], ALU.mult, reads=[yz] + [(zsT, (c, tt)) for c in range(4)], writes=[yz])
                sc.act(sqb.ap, yz.ap, AF.Square, reads=[yz], writes=[sqb])
                sc.copy("pool", yT.ap[:, :, tok], yz.ap.rearrange("p (c t) -> p c t", c=4), reads=[yz],
                        writes=[(yT, (c, tt)) for c in range(4)])
                pss = sc.psum()
                for c in range(4):
                    sc.mm(pss.ap[:, 0:128], ones_bf.ap, sqb.ap[:, c * 128:(c + 1) * 128], c == 0, c == 3,
                          reads=[ones_bf, sqb], writes=[pss])
                rs = rsb2[p]
                sc.act(rs.ap, pss.ap[:, 0:128], AF.Sqrt, reads=[pss], writes=[rs], bias=EPS, scale=1.0 / 512)

            def back_b(ci):
                p = ci % 2
                tok = slice(ci * 128, (ci + 1) * 128)
                tt = ci // 4
                rs = rsb2[p]
                sc.op("dve", lambda h, a=rs.ap: h.reciprocal(a, a), reads=[rs], writes=[rs])
                for c in range(4):
                    sc.stt("dve", yT.ap[:, c, tok], yT.ap[:, c, tok], gn.ap[:, c:c + 1], rs.ap,
                           ALU.mult, ALU.mult, reads=[(yT, (c, tt)), gn, rs], writes=[(yT, (c, tt))])

            front(0)
            for ci in range(16):
                if ci + 1 < 16:
                    front(ci + 1)
                back(ci)
                if ci > 0:
                    back_b(ci - 1)
            back_b(15)

        def mixer_C(l, base=None, plan_only=False):
            if base is None:
                base = len(ring.plan)
                add_win(l, C_C_KC, 512)
                add_win(l, C_C_KW, 280)
                for kv in range(2):
                    ring.add([(lambda s_: s_[0:64, 0:4096].rearrange("p (j h) -> p j h", j=32),
                               w_d["nsa_w1_" + "kv"[kv]][l].rearrange("(j e) h -> e j h", e=64))])
                add_win(l, C_C_Q, 512)
                if plan_only:
                    return base
            Qa = [scr("C_Qa%d" % g, g * 16384, [128, 8192], BF16) for g in range(2)]
            Ks = [scr("C_Ks%d" % g, 32768 + g * 4096, [128, 2048], BF16) for g in range(2)]
            Kw = [scr("C_Kw%d" % g, 40960 + g * 4096, [128, 2048], BF16) for g in range(2)]
            Vs = scr("C_Vs", 49152, [128, 16 * 2 * 66], BF16)
            Vw = scr("C_Vw", 53376, [128, 16 * 2 * 66], BF16)
            Kc = scr("C_Kc", 57600, [128, 2 * 128], BF16)
            Vc = scr("C_Vc", 58112, [128, 2 * 98], BF16)
            gate = scr("C_gate", 58624, [128, 16 * 24], F32)
            PT = [scr("C_PT%d" % i, 60160 + i * 1024, [128, 512], BF16) for i in range(2)]
            o = 62208
            gqk = scr("C_gqk", o, [128, 2], F32); o += 8
            cbias = scr("C_cb", o, [128, 2], F32); o += 8
            w2 = scr("C_w2", o, [128, 2 * 64], BF16); o += 256
            peT = scr("C_peT", o, [64, 64], F32); o += 256
            peTb = scr("C_peTb", o, [64, 64], BF16); o += 128
            mcb = scr("C_mc", o, [128, 512], BF16); o += 1024
            mfb = scr("C_mf", o, [128, 512], BF16); o += 1024
            assert o <= 65536, o
            Vs4 = Vs.ap.rearrange("p (k g e) -> p k g e", k=16, g=2)
            Vw4 = Vw.ap.rearrange("p (k g e) -> p k g e", k=16, g=2)
            Vc3 = Vc.ap.rearrange("p (g e) -> p g e", g=2)
            Kc3 = Kc.ap.rearrange("p (g c) -> p g c", g=2)
            gate3 = gate.ap.rearrange("p (j c) -> p j c", j=16)
            kcvc = scr("C_kcvc", 0, [64, 4 * 2048], BF16)
            kcvc3 = kcvc.ap.rearrange("p (a t) -> p a t", a=4)
            yq = iobuf[0]
            sel = iobuf[1]
            Dcq = cst.ap[:, 5, :]

            sc.dma("sp", gqk.ap[0:64, 0:1], w_d["nsa_q_norm"][l].rearrange("(e o) -> e o", o=1), writes=[(gqk, 0)],
                   skey="ldC")
            sc.dma("sp", gqk.ap[0:64, 1:2], w_d["nsa_k_norm"][l].rearrange("(e o) -> e o", o=1), writes=[(gqk, 1)],
                   skey="ldC")
            sc.ts("dve", gqk.ap[0:64, 0:1], gqk.ap[0:64, 0:1], 0.125, None, ALU.mult, ALU.bypass,
                  reads=[(gqk, 0)], writes=[(gqk, 0)])
            for kv, nm in enumerate(("nsa_pe_k", "nsa_pe_v")):
                sc.dma("sp", peT.ap[:, kv * 32:(kv + 1) * 32], w_d[nm][l].rearrange("j e -> e j"), writes=[(peT, kv)],
                       skey="ldC", slow=True)
                sc.dma("pool", w2.ap[:, kv * 64:(kv + 1) * 64], w_d["nsa_w2_" + "kv"[kv]][l], writes=[(w2, kv)], skey="ldC2")
            sc.copy("dve", peTb.ap, peT.ap, reads=[peT], writes=[peTb])
            sc.dma("pool", mcb.ap, nsa_c["nc_mc"], writes=[mcb], skey="ldC2")
            sc.dma("pool", mfb.ap, nsa_c["nc_mf"], writes=[mfb], skey="ldC2")
            for g in range(2):
                sc.dma("pool", Ks[g].ap[64:68, :], nsa_c["nc_kal"], writes=[(Ks[g], "al")], skey="ldC2")
                sc.dma("pool", Kw[g].ap[64:68, :], nsa_c["nc_kal"], writes=[(Kw[g], "al")], skey="ldC2")
                sc.dma("pool", Ks[g].ap[96:128, :], nsa_c["nc_onehot"], writes=[(Ks[g], "oh")], skey="ldC2")
                sc.op("dve", lambda h, a=Ks[g].ap[64:96, :]: h.memset(a, 0.0), writes=[(Ks[g], "z")])
                sc.dma("pool", Ks[g].ap[64:68, :], nsa_c["nc_kal"], reads=[(Ks[g], "z")], writes=[(Ks[g], "al")], skey="ldC2")
                sc.dma("pool", Kc3[64:68, g, :], nsa_c["nc_kalc"], writes=[(Kc, ("al", g))], skey="ldC2")
                sc.dma("pool", Vc3[:, g, 65:97], nsa_c["nc_ov"], writes=[(Vc, ("ov", g))], skey="ldC2")
                sc.op("dve", lambda h, a=Vc3[:, g, 64:65]: h.memset(a, 1.0), writes=[(Vc, ("one", g))])
            sc.op("dve", lambda h, a=Vs4[:, :, :, 64:65]: h.memset(a, 1.0), writes=[(Vs, "one")])
            sc.op("dve", lambda h, a=Vw4[:, :, :, 64:65]: h.memset(a, 1.0), writes=[(Vw, "one")])


            n64 = [0]

            def norm64(p_ap, pbuf, gcol, out_ap, out_w, n):
                n64[0] += 1
                sq = sq_bf[n64[0] % 2]
                sc.act(sq.ap[0:64, 0:n], p_ap, AF.Square, reads=[pbuf], writes=[sq])
                pss = sc.psum()
                sc.mm(pss.ap[0:64, 0:n], ones_bf.ap[0:64, 0:64], sq.ap[0:64, 0:n], True, True, reads=[ones_bf, sq],
                      writes=[pss])
                rs = tmpf()
                rstd_op("Ck", rs.ap[0:64, 0:n], pss.ap[0:64, 0:n], pss, rs, 1.0 / 64)
                sc.stt("dve", out_ap, p_ap, gqk.ap[0:64, gcol:gcol + 1], rs.ap[0:64, 0:n], ALU.mult, ALU.mult,
                       reads=[pbuf, gqk, rs], writes=out_w)

            w1 = ring.get(base, live=1)
            w1v = wview8(w1, 512)
            for a in range(4):
                for tt in range(NTT):
                    tsl = slice(tt * 512, (tt + 1) * 512)
                    pk = sc.psum()
                    for k in range(NKC):
                        sc.mm(pk.ap[0:64, :], w1v[:, k, a * 64:(a + 1) * 64], hT.ap[:, k, tsl], k == 0, k == NKC - 1,
                              reads=[w1, (hT, (k, tt))], writes=[pk])
                    sc.op("act", lambda h, o_=kcvc3[:, a, tsl], i_=pk.ap[0:64, :]: h.copy(o_, i_), reads=[pk],
                          writes=[(kcvc, (a, tt))])
            def pipe(tasks):
                pend = tasks[0][0]()
                for i, (_, cons) in enumerate(tasks):
                    nxt = tasks[i + 1][0]() if i + 1 < len(tasks) else None
                    cons(pend)
                    pend = nxt

            def kproj(wslot, wv, col0, tt):
                def f():
                    tsl = slice(tt * 512, (tt + 1) * 512)
                    pk = sc.psum()
                    for k in range(NKC):
                        sc.mm(pk.ap[0:64, :], wv[:, k, col0:col0 + 64], hT.ap[:, k, tsl], k == 0, k == NKC - 1,
                              reads=[wslot, (hT, (k, tt))], writes=[pk])
                    return pk
                return f

            tasks = []
            for g in range(2):
                for tt in range(NTT):
                    tsl = slice(tt * 512, (tt + 1) * 512)
                    tasks.append((kproj(w1, w1v, 256 + g * 64, tt),
                                  lambda pk, g=g, tt=tt, tsl=tsl: norm64(pk.ap[0:64, :], pk, 1, Ks[g].ap[0:64, tsl],
                                                                          [(Ks[g], ("k", tt))], 512)))
            pipe(tasks)
            for kb in range(16):
                pv = sc.psum()
                for k in range(NKC):
                    sc.mm(pv.ap[:, 0:128], hT.ap[:, k, kb * 128:(kb + 1) * 128], w1v[:, k, 384:512], k == 0, k == NKC - 1,
                          reads=[w1, (hT, (k, kb // 4))], writes=[pv])
                sc.copy("dve", Vs4[:, kb, :, 0:64], pv.ap[:, 0:128].rearrange("p (g e) -> p g e", g=2), reads=[pv],
                        writes=[(Vs, ("v", kb))])
            w2s = ring.get(base + 1, live=1)
            w2v = wview8(w2s, 280)
            tasks = []
            for g in range(2):
                for tt in range(NTT):
                    tsl = slice(tt * 512, (tt + 1) * 512)
                    tasks.append((kproj(w2s, w2v, g * 64, tt),
                                  lambda pk, g=g, tt=tt, tsl=tsl: norm64(pk.ap[0:64, :], pk, 1, Kw[g].ap[0:64, tsl],
                                                                          [(Kw[g], ("k", tt))], 512)))
            pipe(tasks)
            for kb in range(16):
                pv = sc.psum()
                for k in range(NKC):
                    sc.mm(pv.ap[:, 0:152], hT.ap[:, k, kb * 128:(kb + 1) * 128], w2v[:, k, 128:280], k == 0, k == NKC - 1,
                          reads=[w2s, (hT, (k, kb // 4))], writes=[pv])
                sc.copy("dve", Vw4[:, kb, :, 0:64], pv.ap[:, 0:128].rearrange("p (g e) -> p g e", g=2), reads=[pv],
                        writes=[(Vw, ("v", kb))])
                sc.act(gate3[:, kb, :], pv.ap[:, 128:152], AF.Sigmoid, reads=[pv], writes=[(gate, kb)])
            for kv in range(2):
                ww = ring.get(base + 2 + kv, live=1)
                wwv = ww.ap[0:64, 0:4096].rearrange("p (j h) -> p j h", j=32)
                pb_ = sc.psum()
                for j in range(32):
                    sc.mm(pb_.ap[:, 0:1], wwv[:, j, :], peTb.ap[:, kv * 32 + j:kv * 32 + j + 1], j == 0, j == 31,
                          reads=[ww, peTb], writes=[pb_])
                sc.copy("dve", cbias.ap[:, kv:kv + 1], pb_.ap[:, 0:1], reads=[pb_], writes=[(cbias, kv)])
                for g in range(2):
                    ph = sc.psum()
                    for j in range(32):
                        sc.mm(ph.ap[:, 0:127], wwv[:, j, :], kcvc3[:, kv * 2 + g, j:j + 2017:16], j == 0, j == 31,
                              reads=[ww, kcvc], writes=[ph])
                    hb_ = tmpf()
                    sc.ts("dve", hb_.ap[:, 0:127], ph.ap[:, 0:127], cbias.ap[:, kv:kv + 1], None, ALU.add, ALU.bypass,
                          reads=[ph, (cbias, kv)], writes=[hb_])
                    hid = sq_bf[1]
                    gelu_psum(hb_.ap[:, 0:127], hid.ap[:, 0:127], hb_, [hid])
                    if kv == 0:
                        pc_ = sc.psum()
                        sc.mm(pc_.ap[0:64, 0:127], w2.ap[:, 0:64], hid.ap[:, 0:127], True, True, reads=[(w2, 0), hid],
                              writes=[pc_])
                        norm64(pc_.ap[0:64, 0:127], pc_, 1, Kc3[0:64, g, 0:127], [(Kc, ("k", g))], 127)
                    else:
                        pc_ = sc.psum()
                        sc.mm(pc_.ap[0:127, 0:64], hid.ap[:, 0:127], w2.ap[:, 64:128], True, True, reads=[(w2, 1), hid],
                              writes=[pc_])
                        sc.copy("dve", Vc3[0:127, g, 0:64], pc_.ap[0:127, 0:64], reads=[pc_], writes=[(Vc, ("v", g))])
            sc.barrier()
            for g in range(2):
                sc.dma("pool", Qa[g].ap[64:68, :], nsa_c["nc_qal"][:, g, :], writes=[(Qa[g], "al")], skey="ldC2")
                sc.op("dve", lambda h, a=Qa[g].ap[64:128, :]: h.memset(a, 0.0), writes=[(Qa[g], "z")])
                sc.dma("pool", Qa[g].ap[64:68, :], nsa_c["nc_qal"][:, g, :], reads=[(Qa[g], "z")], writes=[(Qa[g], "al")],
                       skey="ldC2")
            wq = ring.get(base + 4, live=1)
            wqv = wview8(wq, 512)
            def qnorm(pq, hh, tt):
                g, r = hh // 4, hh % 4
                Qv = Qa[g].ap[0:64, :].rearrange("p (j r q) -> p j r q", j=16, r=4)
                sq = sq_bf[(hh * 4 + tt) % 2]
                sc.act(sq.ap[0:64, :], pq.ap[0:64, :], AF.Square, reads=[pq], writes=[sq])
                pss = sc.psum()
                sc.mm(pss.ap[0:64, :], ones_bf.ap[0:64, 0:64], sq.ap[0:64, :], True, True, reads=[ones_bf, sq],
                      writes=[pss])
                rs = tmpf()
                rstd_op("Cq", rs.ap[0:64, :], pss.ap[0:64, :], pss, rs, 1.0 / 64)
                sc.stt("dve", Qv[:, 4 * tt:4 * tt + 4, r, :], pq.ap[0:64, :].rearrange("p (j q) -> p j q", j=4),
                       gqk.ap[0:64, 0:1], rs.ap[0:64, :].rearrange("p (j q) -> p j q", j=4), ALU.mult, ALU.mult,
                       reads=[pq, gqk, rs], writes=[(Qa[g], ("q", hh, tt))])

            tasks = []
            for hh in range(8):
                for tt in range(NTT):
                    tasks.append((kproj(wq, wqv, hh * 64, tt), lambda pq, hh=hh, tt=tt: qnorm(pq, hh, tt)))
            pipe(tasks)

            if cfg.get("c_stop", 9) < 5:
                return
            sc.barrier()
            sc.mark("L%d C attn" % l)
            NB = cfg.get("c_nj", 16)
            acc_c = sc.reserve(2)
            acc_s = sc.reserve(1)[0]
            acc_w = sc.reserve(1)[0]
            order = [(j, g) for j in range(NB) for g in range(2)]
            pt_rr = [0]
            selp_v = selp_sb.ap.rearrange("p (a n) -> p a n", a=2)

            def cmpsel(idx):
                j, g = order[idx]
                par = idx % 2
                pc_ = acc_c[par]
                so = par * 16
                ps = sc.psum()
                sc.mm(ps.ap[0:127, :], Kc3[0:68, g, 0:127], Qa[g].ap[0:68, j * 512:(j + 1) * 512], True, True,
                      reads=[(Qa[g], ("blk", j))], writes=[ps])
                mk = tmpf()
                sc.ts("dve", mk.ap[0:127, 0:128], Dcq[0:127, :], float(128 * j - 31), NEGM, ALU.is_gt, ALU.mult,
                      reads=[cst], writes=[mk])
                sc.tt("dve", sel.ap[0:127, 0:512].rearrange("p (r q) -> p r q", r=4),
                      ps.ap[0:127, :].rearrange("p (r q) -> p r q", r=4),
                      mk.ap[0:127, 0:128].unsqueeze(1).to_broadcast([127, 4, 128]), ALU.add, reads=[ps, mk],
                      writes=[(sel, "s")])
                pt_rr[0] += 1
                pt = PT[pt_rr[0] % 2]
                sc.act(pt.ap[0:127, :], sel.ap[0:127, 0:512], AF.Exp, reads=[(sel, "s")], writes=[pt])
                for r in range(4):
                    sc.mm(pc_.ap[:, r * 128:r * 128 + 97], pt.ap[0:127, r * 128:(r + 1) * 128], Vc3[0:127, g, 0:97],
                          True, True, reads=[pt], writes=[(pc_, r)])
                pc4 = pc_.ap.rearrange("p (r c) -> p r c", r=4)
                rc = small.ap[:, so:so + 4]
                sc.ts("dve", rc, pc4[:, :, 64], 1e-30, None, ALU.max, ALU.bypass, reads=[pc_], writes=[(small, ("rc", par))])
                sc.op("dve", lambda h, a=rc: h.reciprocal(a, a), reads=[(small, ("rc", par))], writes=[(small, ("rc", par))])
                imp = sel.ap[:, 512:544]
                sc.ts("dve", imp, pc4[:, 0, 65:97], small.ap[:, so:so + 1], None, ALU.mult, ALU.bypass,
                      reads=[pc_, (small, ("rc", par))], writes=[(sel, "imp")])
                for r in range(1, 4):
                    sc.stt("dve", imp, pc4[:, r, 65:97], small.ap[:, so + r:so + r + 1], imp, ALU.mult, ALU.add,
                           reads=[pc_, (small, ("rc", par)), (sel, "imp")], writes=[(sel, "imp")])
                sc.tt("dve", small.ap[:, so + 4:so + 8], rc, gate3[:, j, g * 12:g * 12 + 12:3], ALU.mult,
                      reads=[(small, ("rc", par)), (gate, j)], writes=[(small, ("cfc", par))])
                sc.tt("dve", imp, imp, selp_v[:, 0, 31 - 2 * j:63 - 2 * j], ALU.mult, reads=[(sel, "imp"), selp_sb],
                      writes=[(sel, "imp")])
                sc.tt("dve", imp, imp, selp_v[:, 1, 31 - 2 * j:63 - 2 * j], ALU.add, reads=[(sel, "imp"), selp_sb],
                      writes=[(sel, "imp")])
                sc.op("dve", lambda h, a=sel.ap[:, 512:513]: h.memset(a, 1e9), reads=[(sel, "imp")], writes=[(sel, "imp")])
                top8 = sel.ap[:, 544:552]
                sc.op("dve", lambda h, o_=top8, i_=imp: h.max(o_, i_), reads=[(sel, "imp")], writes=[(sel, "top")])
                sneg = sel.ap[:, 640:768]
                sc.op("dve", lambda h, a=sel.ap[:, 640:736]: h.memset(a, 0.0), writes=[(sel, "sneg")])
                sc.ts("dve", sel.ap[:, 736:768], imp, sel.ap[:, 551:552], -32768.0, ALU.is_lt, ALU.mult,
                      reads=[(sel, "imp"), (sel, "top")], writes=[(sel, "sneg")])

            def cmpsel_b(idx):
                j, g = order[idx]
                sneg = sel.ap[:, 640:768]
                ptr = sc.psum()
                sc.transpose(ptr.ap[:, 0:128], sneg, ident.ap, reads=[(sel, "sneg"), cst], writes=[ptr])
                Qv = Qa[g].ap[96:128, j * 512:(j + 1) * 512].rearrange("p (r q) -> p r q", r=4)
                sc.copy("dve", Qv, ptr.ap[96:128, 0:128].unsqueeze(1).to_broadcast([32, 4, 128]), reads=[ptr],
                        writes=[(Qa[g], ("blk", j))])

            def attends(idx):
                j, g = order[idx]
                items = []
                for kb in range(j + 1):
                    items.append((Ks[g].ap[:, kb * 128:(kb + 1) * 128], 128, mcb if kb == j else None, acc_s,
                                  Vs4[:, kb, g, 0:65], kb == 0))
                kbs = [kb for kb in (j - 2, j - 1, j) if kb >= 0]
                for kb in kbs:
                    mb = mcb if kb == j else (mfb if kb == j - 2 else None)
                    items.append((Kw[g].ap[0:68, kb * 128:(kb + 1) * 128], 68, mb, acc_w, Vw4[:, kb, g, 0:65],
                                  kb == kbs[0]))

                def qk(it):
                    Kt, krows, maskb, po, Vap, first = it
                    ps = sc.psum()
                    sc.mm(ps.ap, Kt, Qa[g].ap[0:krows, j * 512:(j + 1) * 512], True, maskb is None,
                          reads=[(Qa[g], ("blk", j))], writes=[ps])
                    if maskb is not None:
                        sc.mm(ps.ap, ident_bf.ap, maskb.ap, False, True, reads=[], writes=[ps])
                    return ps

                def pv(it, ps):
                    Kt, krows, maskb, po, Vap, first = it
                    pt_rr[0] += 1
                    pt = PT[pt_rr[0] % 2]
                    sc.act(pt.ap, ps.ap, AF.Exp, reads=[ps], writes=[pt])
                    for r in range(4):
                        sc.op("pe", lambda h, o_=po.ap[:, r * 128:r * 128 + 65], a_=pt.ap[:, r * 128:(r + 1) * 128],
                              b_=Vap, st=(first and r == 0): h.matmul(o_, a_, b_, start=st, stop=True,
                                                                       skip_group_check=True),
                              reads=[pt], writes=[po])

                pend = qk(items[0])
                for i, it in enumerate(items):
                    nxt = qk(items[i + 1]) if i + 1 < len(items) else None
                    pv(it, pend)
                    pend = nxt

            def combine(idx):
                j, g = order[idx]
                par = idx % 2
                so = par * 16
                yv = yq.ap[:, g * 256:(g + 1) * 256].rearrange("p (r e) -> p r e", r=4)
                for bi, pob in enumerate((acc_c[par], acc_s, acc_w)):
                    p4 = pob.ap.rearrange("p (r c) -> p r c", r=4)
                    if bi == 0:
                        cf = small.ap[:, so + 4:so + 8]
                        cfk = (small, ("cfc", par))
                    else:
                        cf = small.ap[:, 32 + bi * 4:36 + bi * 4]
                        cfk = (small, ("cf", bi))
                        sc.op("dve", lambda h, o_=cf, i_=p4[:, :, 64]: h.reciprocal(o_, i_), reads=[pob], writes=[cfk])
                        sc.tt("dve", cf, cf, gate3[:, j, g * 12 + bi:g * 12 + 12:3], ALU.mult,
                              reads=[cfk, (gate, j)], writes=[cfk])
                    cfb = cf.unsqueeze(2).to_broadcast([128, 4, 64])
                    if bi == 0:
                        sc.tt("dve", yv, p4[:, :, 0:64], cfb, ALU.mult, reads=[pob, cfk], writes=[(yq, g)])
                    else:
                        tm = tmpf()
                        tv = tm.ap[:, 0:256].rearrange("p (r e) -> p r e", r=4)
                        sc.tt("dve", tv, p4[:, :, 0:64], cfb, ALU.mult, reads=[pob, cfk], writes=[tm])
                        sc.tt("pool", yv, yv, tv, ALU.add, reads=[tm, (yq, g)], writes=[(yq, g)])
                if g == 1:
                    tok = slice(j * 128, (j + 1) * 128)
                    pty = sc.psum()
                    for c in range(4):
                        sc.transpose(pty.ap[:, c * 128:(c + 1) * 128], yq.ap[:, c * 128:(c + 1) * 128], ident.ap,
                                     reads=[yq, cst], writes=[(pty, c)])
                    sc.copy("dve", yT.ap[:, :, tok], pty.ap.rearrange("p (c t) -> p c t", c=4), reads=[pty],
                            writes=[(yT, (c, j // 4)) for c in range(4)])

            cmpsel(0)
            cmpsel_b(0)
            for idx in range(len(order)):
                if idx + 1 < len(order):
                    cmpsel(idx + 1)
                attends(idx)
                if idx + 1 < len(order):
                    cmpsel_b(idx + 1)
                combine(idx)
            sc.release(acc_c + [acc_s, acc_w])

        spill_v = xspill_d.rearrange("p (c t) -> p c t", c=NKC)
        mixfn = {"A": mixer_A, "B": mixer_B, "C": mixer_C, "D": mixer_D}
        stages = []
        for l in range(depth):
            if cfg.get("ffn1", True):
                stages.append(("L%d ffn1" % l, lambda l=l: ffn(l, 1, plan_only=True), lambda b, l=l: ffn(l, 1, base=b)))
            if cfg.get("mix", True):
                stages.append(("L%d mixpre" % l, None, lambda b, l=l: mix_pre(l)))
                first = True
                for mi, mname in enumerate("ABCD"):
                    if mname not in mixers:
                        continue
                    stages.append(("L%d mixer %s" % (l, mname), lambda l=l, f=mixfn[mname]: f(l, plan_only=True),
                                   lambda b, l=l, f=mixfn[mname]: f(l, base=b)))
                    stages.append(("L%d merge %s" % (l, mname), lambda l=l, mi=mi: merge(l, mi, False, plan_only=True),
                                   lambda b, l=l, mi=mi, first=first: (merge(l, mi, first, base=b), sc.barrier())))
                    first = False
                stages.append(("L%d outproj" % l, lambda l=l: out_proj(l, plan_only=True),
                               lambda b, l=l: (sc.dma("sp", xT.ap, spill_v, writes=[xT], skey="spill"), out_proj(l, base=b))))
            if cfg.get("ffn2", True):
                stages.append(("L%d ffn2" % l, lambda l=l: ffn(l, 2, plan_only=True), lambda b, l=l: ffn(l, 2, base=b)))

        def mix_pre(l):
            rmsnorm_to_hT(3 * l + 2)
            sc.dma("sp", spill_v, xT.ap, reads=[xT], skey="spill")
            sc.barrier()

        bases = {}

        def plan(i):
            if i < len(stages) and i not in bases:
                bases[i] = stages[i][1]() if stages[i][1] is not None else None

        for i, (nm, pf, rf) in enumerate(stages):
            plan(i)
            k = i + 1
            plan(k)
            while k < len(stages) and stages[k][1] is None:
                k += 1
                plan(k)
            sc.mark(nm)
            rf(bases[i])

        finals = []
        for t in range(S // 128):
            io = iobuf[t % 2]
            for half in range(2):
                ps = sc.psum()
                for j in range(4):
                    c = half * 4 + j
                    sc.transpose(ps.ap[:, j * 128:(j + 1) * 128], xT.ap[:, c, t * 128:(t + 1) * 128], ident.ap,
                                 reads=[(xT, (c, t // 4)), cst], writes=[(ps, j)])
                sc.copy("dve", io.ap[:, half * 512:(half + 1) * 512], ps.ap, reads=[ps], writes=[(io, half)])
            sc.dma("sp", out_d[t * 128:(t + 1) * 128, :], io.ap, reads=[io], skey=("io", t % 2))
        for key in (("io", 0), ("io", 1)):
            ds = sc.dsems[key]
            finals.append((ds[0], ds[1]))
        sc.mark("end")
        print("ENGINE OPS", {n: (e.count, len(e.ops)) for n, e in sc.eng.items()}, flush=True)
        print("MARKS", sc.marks, flush=True)
        sc.emit(finals)
    return nc


_CST = host_consts()


def kernel(**inputs):
    cfg = inputs.pop("_cfg", {})
    depth = cfg.get("depth", DEPTH)
    nc = build_program(cfg)
    x = np.ascontiguousarray(inputs["x"], dtype=np.float32)
    shared = {"cst": _CST}
    shared.update(_NSAC)
    for nm, _ in W_NAMES:
        shared[nm] = np.ascontiguousarray(np.asarray(inputs[nm], dtype=np.float32)[:depth])
    in_maps = []
    for b in range(8):
        m = dict(shared)
        m["x"] = x[b]
        in_maps.append(m)
    res = run_bass_kernel_spmd(nc, in_maps, core_ids=list(range(8)))
    return np.stack([np.asarray(r["out"], dtype=np.float32) for r in res.results], axis=0)
```

```python
import math
from contextlib import ExitStack

import numpy as np
import concourse.bass as bass
import concourse.mybir as mybir
from concourse.bass_utils import run_bass_kernel_spmd

F32 = mybir.dt.float32
BF16 = mybir.dt.bfloat16
AF = mybir.ActivationFunctionType
ALU = mybir.AluOpType
AX = mybir.AxisListType

D = 1024
S = 2048
DEPTH = 4
FFN = 2816
NKC = D // 128
NTT = S // 512
EPS = 1e-6
IN_PROJ = 8992


class Buf:
    def __init__(self, ap, name):
        self.ap = ap
        self.name = name
        self.st = {}

    def __getitem__(self, idx):
        return self.ap[idx]


class Eng:
    def __init__(self, name, sem):
        self.name = name
        self.sem = sem
        self.count = 0
        self.waited = {}
        self.ops = []


class Sched:
    def __init__(self, nc, stack):
        self.nc = nc
        self.stack = stack
        self.eng = {}
        for n in ("pe", "dve", "act", "pool", "sp"):
            self.eng[n] = Eng(n, stack.enter_context(nc.semaphore("s_" + n)))
        self.dsems = {}
        self.psum_banks = []
        self.psum_rr = 0
        self.n_ops = 0
        self.marks = []

    def sbuf(self, name, shape, dtype):
        t = self.stack.enter_context(self.nc.sbuf_tensor("sb_" + name, list(shape), dtype))
        return Buf(t[:], name)

    def view(self, ap, name):
        return Buf(ap, name)

    def dsem(self, key):
        if key not in self.dsems:
            self.dsems[key] = [self.stack.enter_context(self.nc.semaphore("d_" + str(key))), 0]
        return self.dsems[key]

    @staticmethod
    def _norm(acc):
        out = []
        for a in acc:
            if isinstance(a, tuple):
                out.append(a)
            else:
                out.append((a, None))
        return out

    @staticmethod
    def _states(buf, key):
        if key is None:
            return list(buf.st.values())
        r = []
        if key in buf.st:
            r.append(buf.st[key])
        if None in buf.st:
            r.append(buf.st[None])
        return r

    def _collect(self, engname, reads, writes):
        deps = []
        for (b, k) in reads:
            for st in self._states(b, k):
                if st[0] is not None:
                    deps.append((st[0], "raw"))
        for (b, k) in writes:
            for st in self._states(b, k):
                if st[0] is not None:
                    deps.append((st[0], "waw"))
                for t in st[1].values():
                    deps.append((t, "war"))
        return deps

    def _emit_waits(self, e, deps):
        for (tok, kind) in deps:
            sem, val, teng = tok[0], tok[1], tok[2]
            if teng is None:
                val = max(val, tok[3][1])
            if teng == e.name:
                if e.name == "pe":
                    continue
                if kind != "raw":
                    continue
            sid = id(sem)
            if e.waited.get(sid, 0) >= val:
                continue
            e.waited[sid] = val
            e.ops.append(("wait", sem, val))

    def _update(self, tok, reads, writes):
        sid = id(tok[0])
        for (b, k) in writes:
            if k is None:
                b.st = {None: [tok, {}]}
            else:
                b.st[k] = [tok, {}]
        for (b, k) in reads:
            st = b.st.setdefault(k, [None, {}])
            old = st[1].get(sid)
            if old is None or old[1] < tok[1]:
                st[1][sid] = tok

    def op(self, engname, fn, reads=(), writes=()):
        e = self.eng[engname]
        reads = self._norm(reads)
        writes = self._norm(writes)
        self._emit_waits(e, self._collect(engname, reads, writes))
        e.count += 1
        tok = (e.sem, e.count, e.name)
        e.ops.append(("op", fn, e.sem, 1))
        self._update(tok, reads, writes)
        self.n_ops += 1
        return tok

    def dma(self, engname, out, in_, reads=(), writes=(), skey=None, slow=False):
        e = self.eng[engname]
        reads = self._norm(reads)
        writes = self._norm(writes)
        self._emit_waits(e, self._collect(engname, reads, writes))
        ds = self.dsem(skey)
        if ds[1] > 0 and e.waited.get(id(ds[0]), 0) < ds[1]:
            e.waited[id(ds[0])] = ds[1]
            e.ops.append(("wait", ds[0], ds[1]))
        ds[1] += 16
        tok = (ds[0], ds[1], None, ds)
        if slow:
            e.ops.append(("op", lambda h: h.dma_start(out=out, in_=in_, allow_slow_non_contiguous=True), ds[0], 16))
        else:
            e.ops.append(("op", lambda h: h.dma_start(out=out, in_=in_), ds[0], 16))
        self._update(tok, reads, writes)
        self.n_ops += 1
        return tok

    def mark(self, name):
        self.marks.append((name, self.eng["pe"].count))

    def barrier(self):
        toks = [(e.sem, e.count, "*") for e in self.eng.values() if e.count > 0]
        toks += [(ds[0], ds[1], None, ds) for ds in self.dsems.values() if ds[1] > 0]
        for e in self.eng.values():
            for tk in toks:
                sem, val = tk[0], tk[1]
                if sem is e.sem and e.name != "sp":
                    pass
                sid = id(sem)
                if e.waited.get(sid, 0) >= val:
                    continue
                e.waited[sid] = val
                e.ops.append(("wait", sem, val))

    def make_psum(self):
        for i in range(8):
            t = self.stack.enter_context(self.nc.psum_tensor("ps%d" % i, [128, 512], F32))
            self.psum_banks.append(Buf(t[:], "ps%d" % i))
        self.psum_rot = list(self.psum_banks)

    def psum(self):
        b = self.psum_rot[self.psum_rr % len(self.psum_rot)]
        self.psum_rr += 1
        return b

    def reserve(self, n):
        got = self.psum_rot[-n:]
        self.psum_rot = self.psum_rot[:-n]
        return got

    def release(self, banks):
        self.psum_rot = self.psum_rot + list(banks)

    def mm(self, out, lhsT, rhs, start, stop, reads, writes):
        return self.op("pe", lambda h: h.matmul(out, lhsT, rhs, start=start, stop=stop), reads, writes)

    def transpose(self, out, in_, ident, reads, writes):
        return self.op("pe", lambda h: h.transpose(out, in_, ident), reads, writes)

    def act(self, out, in_, func, reads, writes, bias=0.0, scale=1.0, eng="act"):
        return self.op(eng, lambda h: h.activation(out, in_, func, bias=bias, scale=scale), reads, writes)

    def tt(self, eng, out, in0, in1, op, reads, writes):
        return self.op(eng, lambda h: h.tensor_tensor(out, in0, in1, op), reads, writes)

    def ts(self, eng, out, in0, s1, s2, op0, op1, reads, writes):
        return self.op(eng, lambda h: h.tensor_scalar(out, in0, s1, s2, op0, op1), reads, writes)

    def stt(self, eng, out, in0, scalar, in1, op0, op1, reads, writes):
        return self.op(eng, lambda h: h.scalar_tensor_tensor(out, in0, scalar, in1, op0, op1), reads, writes)

    def copy(self, eng, out, in_, reads, writes):
        return self.op(eng, lambda h: h.tensor_copy(out, in_), reads, writes)

    def emit(self, final_waits):
        nc = self.nc
        hmap = {"pe": "tensor", "dve": "vector", "act": "scalar", "pool": "gpsimd", "sp": "sync"}
        with nc.Block() as block:
            for n, e in self.eng.items():
                ops = list(e.ops)
                if n == "sp":
                    ops = ops + [("wait", s, v) for (s, v) in final_waits]

                def body(h, ops=ops):
                    for o in ops:
                        if o[0] == "wait":
                            h.wait_ge(o[1], o[2])
                        else:
                            o[1](h).then_inc(o[2], o[3])

                getattr(block, hmap[n])(body)


class WRing:
    def __init__(self, sc, name, nslots, elems):
        self.sc = sc
        self.slots = [sc.sbuf("%s%d" % (name, i), [128, elems], BF16) for i in range(nslots)]
        self.n = nslots
        self.plan = []
        self.issued = 0
        self.name = name

    def add(self, parts):
        self.plan.append(parts)
        return len(self.plan) - 1

    def _issue(self, hi):
        hi = min(len(self.plan), hi)
        while self.issued < hi:
            j = self.issued
            slot = self.slots[j % self.n]
            for pi, (vf, src) in enumerate(self.plan[j]):
                self.sc.dma("pool", vf(slot.ap), src, reads=(), writes=[slot],
                            skey=(self.name, j % self.n))
            self.issued += 1

    def done(self, j):
        self._issue(j + self.n + 1)

    def get(self, i, live=None):
        if live is None:
            self._issue(i + 1)
        else:
            self._issue(max(i + 1, i - live + 1 + self.n))
        return self.slots[i % self.n]


C_A_U, C_A_V = 0, 512
C_B_Z, C_B_X, C_B_B, C_B_C, C_B_DT = 1024, 1536, 2048, 2304, 2560
C_C_Q, C_C_KC, C_C_VC, C_C_KS, C_C_VS, C_C_KW, C_C_VW, C_C_G = 2568, 3080, 3208, 3336, 3464, 3592, 3720, 3848
C_D_A, C_D_B = 3872, 4384
C_MERGE = 4896
NEGM = -30000.0

W_NAMES = [("ffn1_norm", [D]), ("ffn1_w_in", [D, 2 * FFN]), ("ffn1_w_out", [FFN, D]),
           ("mix_norm", [D]), ("w_in", [D, IN_PROJ]),
           ("sgu_v_norm", [512]), ("sgu_w", [4, 128, 128]), ("sgu_b", [4, 128]),
           ("ssm_conv_w", [4, 1024]), ("ssm_conv_b", [1024]), ("ssm_dt_bias", [8]), ("ssm_a_log", [8]),
           ("ssm_d", [8]), ("ssm_norm", [512]),
           ("nsa_q_norm", [64]), ("nsa_k_norm", [64]), ("nsa_pe_k", [32, 64]), ("nsa_w1_k", [2048, 128]),
           ("nsa_w2_k", [128, 64]), ("nsa_pe_v", [32, 64]), ("nsa_w1_v", [2048, 128]), ("nsa_w2_v", [128, 64]),
           ("conv_dw_w", [31, 512]), ("conv_dw_b", [512]), ("conv_norm", [512]),
           ("w_branch", [4, 512, 1024]), ("w_out", [D, D]),
           ("ffn2_norm", [D]), ("ffn2_w_in", [D, 2 * FFN]), ("ffn2_w_out", [FFN, D])]


def host_consts():
    c = np.zeros((128, 6, 128), np.float32)
    r = np.arange(128)
    c[:, 0, :] = np.eye(128)
    c[:, 1, :] = (r[None, :] <= r[:, None])
    c[:, 2, :] = np.where(r[None, :] >= r[:, None], 0.0, NEGM)
    c[:, 3, :] = (r[:, None] <= r[None, :])
    c[:, 4, :] = 1.0
    c[:, 5, :] = 16.0 * r[:, None] - r[None, :]
    return c


def host_nsa_consts():
    t = np.arange(S)
    hi, lo = (t // 64).astype(np.float32), (t % 64).astype(np.float32)
    slopes = np.array([2.0 ** (-(i + 1)) for i in range(8)], np.float32)
    qal = np.zeros((4, 2, 16, 4, 128), np.float32)
    for h in range(8):
        g, r = h // 4, h % 4
        sl = slopes[h]
        qal[0, g, :, r, :] = (-sl * 64.0 * hi).reshape(16, 128)
        qal[1, g, :, r, :] = (-sl * lo).reshape(16, 128)
        qal[2, g, :, r, :] = sl * 64.0
        qal[3, g, :, r, :] = sl
    kal = np.stack([np.ones(S, np.float32), np.ones(S, np.float32), hi, lo], 0)
    cm = np.arange(128) * 16 + 15.5
    kalc = np.stack([np.ones(128), np.ones(128), np.floor(cm / 64.0), cm - 64.0 * np.floor(cm / 64.0)], 0).astype(np.float32)
    onehot = (np.arange(32)[:, None] == (t // 64)[None, :]).astype(np.float32)
    r = np.arange(128)
    mc = np.where(r[None, :] >= r[:, None], 0.0, NEGM).astype(np.float32)
    mf = np.where(r[:, None] > r[None, :], 0.0, NEGM).astype(np.float32)
    c_start = np.arange(128) * 16
    s_start = np.arange(32) * 64
    ov = ((c_start[:, None] <= s_start[None, :] + 63) & (c_start[:, None] + 31 >= s_start[None, :])).astype(np.float32)
    selp = np.zeros((128, 2, 63), np.float32)
    for npr in range(63):
        d = npr - 31
        for q in range(128):
            up = q >= 64
            if d < -1 or (d == -1 and up):
                selp[q, 0, npr], selp[q, 1, npr] = 1.0, 0.0
            elif (d == -1 and not up) or d == 0 or (d == 1 and up):
                selp[q, 0, npr], selp[q, 1, npr] = 0.0, 1e9
            else:
                selp[q, 0, npr], selp[q, 1, npr] = 0.0, -1e9
    return {"nc_qal": qal.reshape(4, 2, 8192), "nc_kal": kal, "nc_kalc": kalc, "nc_onehot": onehot,
            "nc_mc": np.tile(mc, (1, 4)), "nc_mf": np.tile(mf, (1, 4)), "nc_ov": ov, "nc_selp": selp}


_NSAC = host_nsa_consts()


def build_program(cfg):
    depth = cfg.get("depth", DEPTH)
    mixers = cfg.get("mixers", "ABCD")
    nc = bass.Bass("TRN2", target_bir_lowering=False)

    def din(name, shape, dtype=F32):
        return nc.dram_tensor(name, list(shape), dtype, kind="ExternalInput").ap()

    x_d = din("x", [S, D])
    cst_d = din("cst", [128, 6, 128])
    nsa_c = {k: din(k, list(v.shape)) for k, v in _NSAC.items()}
    w_d = {}
    for nm, shp in W_NAMES:
        w_d[nm] = din(nm, [depth] + shp)
    out_d = nc.dram_tensor("out", [S, D], F32, kind="ExternalOutput").ap()
    xspill_d = nc.dram_tensor("xspill", [128, NKC * S], F32, kind="Internal").ap()

    with ExitStack() as stack:
        sc = Sched(nc, stack)
        sc.make_psum()

        xT = sc.sbuf("xT", [128, NKC, S], F32)
        hT = sc.sbuf("hT", [128, NKC, S], BF16)
        mT = sc.sbuf("mT", [128, NKC, S], BF16)
        actT = sc.sbuf("actT", [128, 4, S], BF16)
        yT = actT
        cst = sc.sbuf("cst", [128, 6, 128], F32)
        ident = Buf(cst.ap[:, 0, :], "ident")
        ident_bf = sc.sbuf("ident_bf", [128, 128], BF16)
        ones_bf = sc.sbuf("ones_bf", [128, 128], BF16)
        gains = sc.sbuf("gains", [128, 3 * DEPTH, NKC], F32)
        iobuf = [sc.sbuf("io%d" % i, [128, D], F32) for i in range(2)]
        tmp_a = [sc.sbuf("tmpa%d" % i, [128, 512], F32) for i in range(3)]
        sq_bf = [sc.sbuf("sq%d" % i, [128, 512], BF16) for i in range(2)]
        rstd = sc.sbuf("rstd", [128, 512], F32)
        small = sc.sbuf("small", [128, 64], F32)
        selp_sb = sc.sbuf("selp", [128, 126], F32)
        ring = WRing(sc, "wr", 4, 4096)
        tmp_rr = [0]

        def tmpf():
            tmp_rr[0] += 1
            return tmp_a[tmp_rr[0] % 3]

        xflat = xT.ap.rearrange("p c t -> p (c t)")
        xflat_bf = xflat.bitcast(BF16)

        def scr(name, off, shape, dtype):
            n = int(np.prod(shape[1:]))
            if dtype == F32:
                assert off % 4 == 0
                ap = xflat[0:shape[0], off // 4: off // 4 + n]
            else:
                ap = xflat_bf[0:shape[0], off // 2: off // 2 + n]
            if len(shape) == 3:
                ap = ap.rearrange("p (a b) -> p a b", a=shape[1])
            return Buf(ap, name)

        sc.dma("sp", cst.ap, cst_d, writes=[cst], skey="c0")
        sc.op("dve", lambda h: h.memset(ones_bf.ap, 1.0), writes=[ones_bf])
        sc.dma("sp", selp_sb.ap, nsa_c["nc_selp"].rearrange("p a n -> p (a n)"), writes=[selp_sb], skey="c0")
        sc.copy("dve", ident_bf.ap, ident.ap, reads=[cst], writes=[ident_bf])
        for l in range(depth):
            for wi, nm in enumerate(("ffn1_norm", "ffn2_norm", "mix_norm")):
                sc.dma("sp", gains.ap[:, 3 * l + wi, :], w_d[nm][l].rearrange("(c p) -> p c", p=128),
                       writes=[(gains, (l, wi))], skey="c1", slow=True)

        for t in range(S // 128):
            io = iobuf[t % 2]
            sc.dma("sp", io.ap, x_d[t * 128:(t + 1) * 128, :], writes=[io], skey=("io", t % 2))
            for half in range(2):
                ps = sc.psum()
                for j in range(4):
                    c = half * 4 + j
                    sc.transpose(ps.ap[:, j * 128:(j + 1) * 128], io.ap[:, c * 128:(c + 1) * 128], ident.ap,
                                 reads=[io, cst], writes=[(ps, j)])
                sc.copy("dve", xT.ap[:, half * 4:half * 4 + 4, t * 128:(t + 1) * 128],
                        ps.ap.rearrange("p (c t) -> p c t", c=4),
                        reads=[ps], writes=[(xT, (half * 4 + j, t // 4)) for j in range(4)])

        RS_LN = set(cfg.get("rs_ln", ()))

        def rstd_op(site, o_ap, i_ap, ibuf, obuf, scale):
            if site in RS_LN:
                sc.act(o_ap, i_ap, AF.Ln, reads=[ibuf], writes=[obuf], bias=EPS, scale=scale)
                sc.act(o_ap, o_ap, AF.Exp, reads=[obuf], writes=[obuf], scale=-0.5)
            else:
                sc.act(o_ap, i_ap, AF.Sqrt, reads=[ibuf], writes=[obuf], bias=EPS, scale=scale)
                sc.op("dve", lambda h: h.reciprocal(o_ap, o_ap), reads=[obuf], writes=[obuf])

        def rmsnorm_to_hT(gidx):
            for tt in range(NTT):
                tsl = slice(tt * 512, (tt + 1) * 512)
                ps = sc.psum()
                for c in range(NKC):
                    sq = sq_bf[c % 2]
                    sc.act(sq.ap, xT.ap[:, c, tsl], AF.Square, reads=[(xT, (c, tt))], writes=[sq])
                    sc.mm(ps.ap, ones_bf.ap, sq.ap, c == 0, c == NKC - 1, reads=[ones_bf, sq], writes=[ps])
                rstd_op("hT", rstd.ap, ps.ap, ps, rstd, 1.0 / D)
                for c in range(NKC):
                    sc.stt("dve", hT.ap[:, c, tsl], xT.ap[:, c, tsl],
                           gains.ap[:, gidx, c:c + 1], rstd.ap, ALU.mult, ALU.mult,
                           reads=[(xT, (c, tt)), gains, rstd], writes=[(hT, (c, tt))])

        def wview8(slot, cw):
            return slot.ap[:, 0:8 * cw].rearrange("p (k c) -> p k c", k=8)

        def add_win(l, c0, cw):
            return ring.add([(lambda s, cw=cw: s[:, 0:8 * cw].rearrange("p (k c) -> p k c", k=8),
                              w_d["w_in"][l][:, c0:c0 + cw].rearrange("(k p) c -> p k c", p=128))])

        def inproj_fm(ps_ap, wslot, wv, col0, tok_sl, tt):
            for k in range(NKC):
                sc.mm(ps_ap, wv[:, k, col0:col0 + 128], hT.ap[:, k, tok_sl], k == 0, k == NKC - 1,
                      reads=[wslot, (hT, (k, tt))], writes=[])

        def ffn(l, which, base=None, plan_only=False):
            w_in = w_d["ffn%d_w_in" % which][l]
            w_out = w_d["ffn%d_w_out" % which][l]
            groups = [(f0, min(4, 22 - f0)) for f0 in range(0, 22, 4)]
            if base is not None:
                rmsnorm_to_hT(3 * l + (which - 1))
            for (f0, nf) in (groups if base is None else []):
                cw = nf * 128
                for off in (0, FFN):
                    ring.add([(lambda s, cw=cw: s[:, 0:8 * cw].rearrange("p (k c) -> p k c", k=8),
                               w_in[:, off + f0 * 128: off + f0 * 128 + cw].rearrange("(k p) c -> p k c", p=128))])
                ring.add([(lambda s, nf=nf: s[:, 0:nf * 1024].rearrange("p (f c) -> p f c", f=nf),
                           w_out[f0 * 128:(f0 + nf) * 128, :].rearrange("(f p) c -> p f c", p=128))])
            if base is None:
                base = len(ring.plan) - 3 * len(groups)
                if plan_only:
                    return base
                rmsnorm_to_hT(3 * l + (which - 1))
            for gi, (f0, nf) in enumerate(groups):
                cw = nf * 128
                wg = ring.get(base + 3 * gi, live=1)
                wu = ring.get(base + 3 * gi + 1, live=2)
                wgv = wview8(wg, cw)
                wuv = wview8(wu, cw)
                for tt in range(NTT):
                    tsl = slice(tt * 512, (tt + 1) * 512)
                    for fi in range(nf):
                        pg = sc.psum()
                        pu = sc.psum()
                        for k in range(NKC):
                            sc.mm(pg.ap, wgv[:, k, fi * 128:(fi + 1) * 128], hT.ap[:, k, tsl], k == 0, k == NKC - 1,
                                  reads=[wg, (hT, (k, tt))], writes=[pg])
                        for k in range(NKC):
                            sc.mm(pu.ap, wuv[:, k, fi * 128:(fi + 1) * 128], hT.ap[:, k, tsl], k == 0, k == NKC - 1,
                                  reads=[wu, (hT, (k, tt))], writes=[pu])
                        tm = tmpf()
                        sc.act(tm.ap, pg.ap, AF.Silu, reads=[pg], writes=[tm])
                        sc.tt("dve", actT.ap[:, fi, tsl], tm.ap, pu.ap, ALU.mult,
                              reads=[tm, pu], writes=[(actT, (fi, tt))])
                wo = ring.get(base + 3 * gi + 2, live=1)
                wov = wo.ap[:, 0:nf * 1024].rearrange("p (f c) -> p f c", f=nf)
                for tt in range(NTT):
                    tsl = slice(tt * 512, (tt + 1) * 512)
                    for dc in range(NKC):
                        po = sc.psum()
                        for fi in range(nf):
                            sc.mm(po.ap, wov[:, fi, dc * 128:(dc + 1) * 128], actT.ap[:, fi, tsl], fi == 0, fi == nf - 1,
                                  reads=[wo, (actT, (fi, tt))], writes=[po])
                        sc.stt("dve", xT.ap[:, dc, tsl], po.ap, 0.5, xT.ap[:, dc, tsl], ALU.mult, ALU.add,
                               reads=[po, (xT, (dc, tt))], writes=[(xT, (dc, tt))])

        def load_T(dst_fn, src2d, rows, nchunk, stage):
            sc.dma("sp", stage.ap[0:rows, 0:nchunk * 128], src2d, writes=[stage], skey="ldT")
            ps = sc.psum()
            for c in range(nchunk):
                sc.transpose(ps.ap[:, c * 128:c * 128 + rows], stage.ap[0:rows, c * 128:(c + 1) * 128],
                             ident.ap[0:rows, 0:rows], reads=[stage, cst], writes=[(ps, c)])
            for c in range(nchunk):
                sc.copy("dve", dst_fn(c), ps.ap[:, c * 128:c * 128 + rows], reads=[ps], writes=[])

        def gelu_psum(p_ap, out_ap, pbuf, outbuf_w):
            n = p_ap.shape[-1] if len(p_ap.shape) == 2 else None
            t1 = tmpf()
            t2 = tmpf()
            a1 = t1.ap[:, 0:512] if n is None else t1.ap[:, 0:n]
            a2 = t2.ap[:, 0:512] if n is None else t2.ap[:, 0:n]
            sc.act(a1, p_ap, AF.Square, reads=[pbuf], writes=[t1])
            sc.ts("dve", a1, a1, 0.044715, 1.0, ALU.mult, ALU.add, reads=[t1], writes=[t1])
            sc.tt("dve", a1, a1, p_ap, ALU.mult, reads=[t1, pbuf], writes=[t1])
            sc.act(a2, a1, AF.Sigmoid, reads=[t1], writes=[t2], scale=1.5957691216057308)
            sc.tt("dve", out_ap, a2, p_ap, ALU.mult, reads=[t2, pbuf], writes=outbuf_w)

        def merge(l, i, first, base=None, plan_only=False):
            if base is None:
                base = len(ring.plan)
                wb_d = w_d["w_branch"][l][i]
                ring.add([(lambda s: s[:, 0:4096].rearrange("p (f c) -> p f c", f=4),
                           wb_d.rearrange("(f p) c -> p f c", p=128))])
                for half in range(2):
                    add_win(l, C_MERGE + i * 1024 + half * 512, 512)
                if plan_only:
                    return base
            wb = ring.get(base, live=1)
            wbv = wb.ap[:, 0:4096].rearrange("p (f c) -> p f c", f=4)
            for half in range(2):
                wl = ring.get(base + 1 + half, live=2 + half)
                wlv = wview8(wl, 512)
                for tt in range(NTT):
                    tsl = slice(tt * 512, (tt + 1) * 512)
                    for j in range(4):
                        dc = half * 4 + j
                        pl = sc.psum()
                        pb = sc.psum()
                        for k in range(NKC):
                            sc.mm(pl.ap, wlv[:, k, j * 128:(j + 1) * 128], hT.ap[:, k, tsl], k == 0, k == NKC - 1,
                                  reads=[wl, (hT, (k, tt))], writes=[pl])
                        for k in range(4):
                            sc.mm(pb.ap, wbv[:, k, dc * 128:(dc + 1) * 128], yT.ap[:, k, tsl], k == 0, k == 3,
                                  reads=[wb, (yT, (k, tt))], writes=[pb])
                        tm = tmpf()
                        sc.act(tm.ap, pl.ap, AF.Sigmoid, reads=[pl], writes=[tm])
                        if first:
                            sc.tt("dve", mT.ap[:, dc, tsl], tm.ap, pb.ap, ALU.mult,
                                  reads=[tm, pb], writes=[(mT, (dc, tt))])
                        else:
                            sc.tt("dve", tm.ap, tm.ap, pb.ap, ALU.mult, reads=[tm, pb], writes=[tm])
                            sc.tt("pool", mT.ap[:, dc, tsl], mT.ap[:, dc, tsl], tm.ap, ALU.add,
                                  reads=[tm, (mT, (dc, tt))], writes=[(mT, (dc, tt))])

        def out_proj(l, base=None, plan_only=False):
            if base is None:
                base = len(ring.plan)
                for half in range(2):
                    ring.add([(lambda s: s[:, 0:4096].rearrange("p (k c) -> p k c", k=8),
                               w_d["w_out"][l][:, half * 512:(half + 1) * 512].rearrange("(k p) c -> p k c", p=128))])
                if plan_only:
                    return base
            for half in range(2):
                wo = ring.get(base + half, live=1)
                wov = wview8(wo, 512)
                for tt in range(NTT):
                    tsl = slice(tt * 512, (tt + 1) * 512)
                    for j in range(4):
                        dc = half * 4 + j
                        po = sc.psum()
                        for k in range(NKC):
                            sc.mm(po.ap, wov[:, k, j * 128:(j + 1) * 128], mT.ap[:, k, tsl], k == 0, k == NKC - 1,
                                  reads=[wo, (mT, (k, tt))], writes=[po])
                        sc.tt("dve", xT.ap[:, dc, tsl], po.ap, xT.ap[:, dc, tsl], ALU.add,
                              reads=[po, (xT, (dc, tt))], writes=[(xT, (dc, tt))])

        def mixer_D(l, base=None, plan_only=False):
            if base is None:
                base = len(ring.plan)
                add_win(l, C_D_A, 512)
                add_win(l, C_D_B, 512)
                if plan_only:
                    return base
            u = scr("D_u", 0, [128, 30 + S], BF16)
            dg = scr("D_dg", 4352, [128, 31, 128], BF16)
            wD = scr("D_w", 12288, [128, 4, 32], F32)
            bg = scr("D_bg", 12800, [128, 8], F32)
            stage = scr("D_stage", 13056, [128, 512], F32)
            load_T(lambda c: wD.ap[:, c, 0:31], w_d["conv_dw_w"][l], 31, 4, stage)
            sc.dma("sp", bg.ap[:, 0:4], w_d["conv_dw_b"][l].rearrange("(c p) -> p c", p=128), writes=[(bg, 0)],
                   skey="c1", slow=True)
            sc.dma("sp", bg.ap[:, 4:8], w_d["conv_norm"][l].rearrange("(c p) -> p c", p=128), writes=[(bg, 1)],
                   skey="c1", slow=True)
            sc.op("dve", lambda h: h.memset(u.ap[:, 0:30], 0.0), writes=[(u, "pad")])
            ssb = sc.reserve(4)
            wa = ring.get(base, live=1)
            wb = ring.get(base + 1, live=2)
            wav, wbv = wview8(wa, 512), wview8(wb, 512)
            for c in range(4):
                sc.tt("dve", dg.ap, ident_bf.ap.unsqueeze(1).to_broadcast([128, 31, 128]),
                      wD.ap[:, c, 0:31].unsqueeze(2).to_broadcast([128, 31, 128]), ALU.mult,
                      reads=[ident_bf, wD], writes=[dg])
                for tt in range(NTT):
                    tsl = slice(tt * 512, (tt + 1) * 512)
                    pa, pb = sc.psum(), sc.psum()
                    for k in range(NKC):
                        sc.mm(pa.ap, wav[:, k, c * 128:(c + 1) * 128], hT.ap[:, k, tsl], k == 0, k == NKC - 1,
                              reads=[wa, (hT, (k, tt))], writes=[pa])
                    for k in range(NKC):
                        sc.mm(pb.ap, wbv[:, k, c * 128:(c + 1) * 128], hT.ap[:, k, tsl], k == 0, k == NKC - 1,
                              reads=[wb, (hT, (k, tt))], writes=[pb])
                    tm = tmpf()
                    sc.act(tm.ap, pb.ap, AF.Sigmoid, reads=[pb], writes=[tm])
                    sc.tt("dve", u.ap[:, 30 + tt * 512: 30 + (tt + 1) * 512], tm.ap, pa.ap, ALU.mult,
                          reads=[tm, pa], writes=[(u, tt)])
                for tt in range(NTT):
                    tsl = slice(tt * 512, (tt + 1) * 512)
                    pcv = sc.psum()
                    for k in range(31):
                        sc.mm(pcv.ap, dg.ap[:, k, :], u.ap[:, tt * 512 + k: tt * 512 + k + 512], k == 0, k == 30,
                              reads=[dg, u], writes=[pcv])
                    sq = sq_bf[tt % 2]
                    sc.act(sq.ap, pcv.ap, AF.Square, reads=[pcv, bg], writes=[sq], bias=bg.ap[:, c:c + 1])
                    sc.mm(ssb[tt].ap, ones_bf.ap, sq.ap, c == 0, c == 3, reads=[ones_bf, sq], writes=[ssb[tt]])
                    sc.act(yT.ap[:, c, tsl], pcv.ap, AF.Identity, reads=[pcv, bg], writes=[(yT, (c, tt))],
                           bias=bg.ap[:, c:c + 1])
            for tt in range(NTT):
                tsl = slice(tt * 512, (tt + 1) * 512)
                rstd_op("D", rstd.ap, ssb[tt].ap, ssb[tt], rstd, 1.0 / 512)
                for c in range(4):
                    tm = tmpf()
                    sc.stt("dve", tm.ap, yT.ap[:, c, tsl], bg.ap[:, 4 + c:5 + c], rstd.ap, ALU.mult, ALU.mult,
                           reads=[(yT, (c, tt)), bg, rstd], writes=[tm])
                    sc.act(yT.ap[:, c, tsl], tm.ap, AF.Silu, reads=[tm], writes=[(yT, (c, tt))])
            sc.release(ssb)

        def mixer_A(l, base=None, plan_only=False):
            if base is None:
                base = len(ring.plan)
                add_win(l, C_A_U, 512)
                add_win(l, C_A_V, 512)
                if plan_only:
                    return base
            stage = scr("A_stage", 0, [128, 512], F32)
            wTs = scr("A_wT", 2048, [128, 4, 128], BF16)
            gbc = scr("A_gbc", 3072, [128, 512], F32)
            brow_f = scr("A_browf", 5120, [1, 512], F32)
            brow = scr("A_brow", 7168, [1, 512], BF16)
            ug = scr("A_ug", 8192, [128, 4, 512], BF16)
            vg2 = [scr("A_vg%d" % i, 12288 + i * 2048, [128, 512], F32) for i in range(2)]
            vn2 = [scr("A_vn%d" % i, 16384 + i * 1024, [128, 512], BF16) for i in range(2)]
            ssv = scr("A_ssv", 18432, [128, 8], F32)
            for g in range(4):
                sc.dma("sp", stage.ap[:, g * 128:(g + 1) * 128], w_d["sgu_w"][l][g], writes=[(stage, g)], skey="ldA")
            sc.tt("dve", stage.ap.rearrange("p (g s) -> p g s", g=4), stage.ap.rearrange("p (g s) -> p g s", g=4),
                  cst.ap[:, 1:2, :].to_broadcast([128, 4, 128]), ALU.mult, reads=[stage, cst], writes=[stage])
            ps = sc.psum()
            for g in range(4):
                sc.transpose(ps.ap[:, g * 128:(g + 1) * 128], stage.ap[:, g * 128:(g + 1) * 128], ident.ap,
                             reads=[stage, cst], writes=[(ps, g)])
            sc.copy("dve", wTs.ap, ps.ap.rearrange("p (g s) -> p g s", g=4), reads=[ps], writes=[wTs])
            sc.dma("sp", gbc.ap, w_d["sgu_v_norm"][l].partition_broadcast(128), writes=[gbc], skey="ldA")
            sc.dma("sp", brow_f.ap, w_d["sgu_b"][l].rearrange("g t -> (g t)")[None, :], writes=[brow_f], skey="ldA")
            sc.copy("dve", brow.ap, brow_f.ap, reads=[brow_f], writes=[brow])
            wu = ring.get(base, live=1)
            wv = ring.get(base + 1, live=2)
            wuv, wvv = wview8(wu, 512), wview8(wv, 512)
            def a_u(tt, g):
                tsl = slice(tt * 512, (tt + 1) * 512)
                pu = sc.psum()
                for k in range(NKC):
                    sc.mm(pu.ap, wuv[:, k, g * 128:(g + 1) * 128], hT.ap[:, k, tsl], k == 0, k == NKC - 1,
                          reads=[wu, (hT, (k, tt))], writes=[pu])
                gelu_psum(pu.ap, ug.ap[:, g, :], pu, [(ug, g)])

            def a_vproj(tt, sub):
                t0 = tt * 512 + sub * 128
                pv = sc.psum()
                for k in range(NKC):
                    sc.mm(pv.ap, hT.ap[:, k, t0:t0 + 128], wvv[:, k, :], k == 0, k == NKC - 1,
                          reads=[wv, (hT, (k, tt))], writes=[pv])
                return pv

            def a_vrest(tt, sub, pv):
                t0 = tt * 512 + sub * 128
                par = sub % 2
                vgp, vnp = vg2[par], vn2[par]
                gelu_psum(pv.ap, vgp.ap, pv, [vgp])
                tj = tmpf()
                so = par * 4
                sc.op("act", lambda h, o=tj.ap, i=vgp.ap, a=ssv.ap[:, so:so + 1]: h.activation(o, i, AF.Square, accum_out=a),
                      reads=[vgp], writes=[tj, (ssv, so)])
                sc.act(ssv.ap[:, so + 1:so + 2], ssv.ap[:, so:so + 1], AF.Sqrt, reads=[(ssv, so)], writes=[(ssv, so + 1)],
                       bias=EPS, scale=1.0 / 512)
                sc.op("dve", lambda h, o=ssv.ap[:, so + 2:so + 3], i=ssv.ap[:, so + 1:so + 2]: h.reciprocal(o, i),
                      reads=[(ssv, so + 1)], writes=[(ssv, so + 2)])
                sc.stt("dve", vnp.ap, vgp.ap, ssv.ap[:, so + 2:so + 3], gbc.ap, ALU.mult, ALU.mult,
                       reads=[vgp, (ssv, so + 2), gbc], writes=[vnp])
                pm = sc.psum()
                for g in range(4):
                    sc.mm(pm.ap[:, g * 128:(g + 1) * 128], vnp.ap[:, g * 128:(g + 1) * 128], wTs.ap[:, g, :], True, False,
                          reads=[vnp, wTs], writes=[(pm, g)])
                    sc.mm(pm.ap[:, g * 128:(g + 1) * 128], ones_bf.ap[0:1, :], brow.ap[0:1, g * 128:(g + 1) * 128],
                          False, True, reads=[ones_bf, brow], writes=[(pm, g)])
                sc.tt("dve", yT.ap[:, :, t0:t0 + 128], ug.ap[:, :, sub * 128:(sub + 1) * 128],
                      pm.ap.rearrange("p (g t) -> p g t", g=4), ALU.mult,
                      reads=[ug, pm], writes=[(yT, (g, tt)) for g in range(4)])

            for tt in range(NTT):
                for g in range(4):
                    a_u(tt, g)
                pend = a_vproj(tt, 0)
                for sub in range(4):
                    nxt = a_vproj(tt, sub + 1) if sub + 1 < 4 else None
                    a_vrest(tt, sub, pend)
                    pend = nxt

        def mixer_B(l, base=None, plan_only=False):
            if base is None:
                base = len(ring.plan)
                add_win(l, C_B_X, 512)
                add_win(l, C_B_B, 512)
                add_win(l, C_B_Z, 512)
                add_win(l, C_B_DT, 8)
                if plan_only:
                    return base
            raw = scr("B_raw", 0, [128, 3 + S], F32)
            xsT = scr("B_xsT", 8448, [128, 4, S], BF16)
            bcT = scr("B_bcT", 24832, [128, 4, S], BF16)
            zsT = scr("B_zsT", 41216, [128, 4, S], BF16)
            o = 57600
            cw = scr("B_cw", o, [128, 8, 4], F32); o += 128
            cb = scr("B_cb", o, [128, 8], F32); o += 32
            dtT = scr("B_dt", o, [128, 16, 8], F32); o += 512
            aT = scr("B_a", o, [128, 16, 8], F32); o += 512
            dtb_bc = scr("B_dtb", o, [128, 8], F32); o += 32
            A_bc = scr("B_A", o, [128, 8], F32); o += 32
            Dp = scr("B_Dp", o, [128, 4], F32); o += 16
            gn = scr("B_gn", o, [128, 4], F32); o += 16
            sm = scr("B_sm", o, [128, 64], F32); o += 256
            xdt = scr("B_xdt", o, [128, 512], BF16); o += 1024
            xdd = scr("B_xdd", o, [128, 512], BF16); o += 1024
            btok = scr("B_btok", o, [128, 256], BF16); o += 512
            Hs = scr("B_H", o, [128, 512], F32); o += 2048
            Hbf = scr("B_Hbf", o, [128, 512], BF16); o += 1024
            assert o <= 65536
            ST = scr("B_ST", 0, [128, 8, 128], BF16)
            CTs = scr("B_CTs", 2048, [128, 8, 128], BF16)
            yz = scr("B_yz", 4096, [128, 512], F32)
            sqb = scr("B_sqb", 6144, [128, 512], BF16)
            rsb = scr("B_rs", 7168, [128, 128], F32)
            rsb2 = [rsb, scr("B_rs1", 7680, [128, 128], F32)]
            stage = iobuf[0]
            atri = Buf(iobuf[0].ap.rearrange("p (h l) -> p h l", h=8), "B_atri")
            argm = Buf(iobuf[1].ap.rearrange("p (h l) -> p h l", h=8), "B_argm")

            for hf in range(2):
                load_T(lambda c, hf=hf: cw.ap[:, hf * 4 + c, 0:4], w_d["ssm_conv_w"][l][:, hf * 512:(hf + 1) * 512], 4, 4,
                       stage)
            sc.dma("sp", cb.ap, w_d["ssm_conv_b"][l].rearrange("(c p) -> p c", p=128), writes=[cb], skey="c1", slow=True)
            sc.dma("sp", gn.ap, w_d["ssm_norm"][l].rearrange("(c p) -> p c", p=128), writes=[gn], skey="c1", slow=True)
            sc.dma("sp", dtb_bc.ap, w_d["ssm_dt_bias"][l].partition_broadcast(128), writes=[dtb_bc], skey="c1")
            sc.dma("sp", A_bc.ap, w_d["ssm_a_log"][l].partition_broadcast(128), writes=[A_bc], skey="c1")
            for c in range(4):
                for hf in range(2):
                    sc.dma("sp", Dp.ap[hf * 64:(hf + 1) * 64, c:c + 1],
                           w_d["ssm_d"][l][2 * c + hf:2 * c + hf + 1].partition_broadcast(64), writes=[(Dp, (c, hf))],
                           skey="c1")
            sc.act(A_bc.ap, A_bc.ap, AF.Exp, reads=[A_bc], writes=[A_bc])
            sc.ts("dve", A_bc.ap, A_bc.ap, -1.0, None, ALU.mult, ALU.bypass, reads=[A_bc], writes=[A_bc])
            sc.op("dve", lambda h: h.memset(raw.ap[:, 0:3], 0.0), writes=[(raw, "pad")])

            for blk in range(2):
                wx = ring.get(base + blk, live=1)
                wxv = wview8(wx, 512)
                for cc in range(4):
                    c = blk * 4 + cc
                    dst = xsT if blk == 0 else bcT
                    for tt in range(NTT):
                        tsl = slice(tt * 512, (tt + 1) * 512)
                        px = sc.psum()
                        for k in range(NKC):
                            sc.mm(px.ap, wxv[:, k, cc * 128:(cc + 1) * 128], hT.ap[:, k, tsl], k == 0, k == NKC - 1,
                                  reads=[wx, (hT, (k, tt))], writes=[px])
                        sc.op("act", lambda h, o=raw.ap[:, 3 + tt * 512:3 + (tt + 1) * 512], i=px.ap: h.copy(o, i),
                              reads=[px], writes=[(raw, tt)])
                        tm = tmpf()
                        sc.ts("dve", tm.ap, raw.ap[:, tt * 512:tt * 512 + 512], cw.ap[:, c, 0:1], cb.ap[:, c:c + 1],
                              ALU.mult, ALU.add, reads=[raw, cw, cb], writes=[tm])
                        for k in range(1, 4):
                            sc.stt("dve", tm.ap, raw.ap[:, tt * 512 + k:tt * 512 + k + 512], cw.ap[:, c, k:k + 1], tm.ap,
                                   ALU.mult, ALU.add, reads=[raw, cw, tm], writes=[tm])
                        sc.act(dst.ap[:, cc, tsl], tm.ap, AF.Silu, reads=[tm], writes=[(dst, (cc, tt))])
            wz = ring.get(base + 2, live=1)
            wzv = wview8(wz, 512)
            for c in range(4):
                for tt in range(NTT):
                    tsl = slice(tt * 512, (tt + 1) * 512)
                    pz = sc.psum()
                    for k in range(NKC):
                        sc.mm(pz.ap, wzv[:, k, c * 128:(c + 1) * 128], hT.ap[:, k, tsl], k == 0, k == NKC - 1,
                              reads=[wz, (hT, (k, tt))], writes=[pz])
                    sc.act(zsT.ap[:, c, tsl], pz.ap, AF.Silu, reads=[pz], writes=[(zsT, (c, tt))])
            wdt = ring.get(base + 3, live=1)
            wdtv = wview8(wdt, 8)
            pdt = sc.psum()
            for ci in range(16):
                for k in range(NKC):
                    sc.mm(pdt.ap[:, ci * 8:(ci + 1) * 8], hT.ap[:, k, ci * 128:(ci + 1) * 128], wdtv[:, k, :], k == 0,
                          k == NKC - 1, reads=[wdt, (hT, (k, ci // 4))], writes=[(pdt, ci)])
            sc.tt("dve", dtT.ap, pdt.ap[:, 0:128].rearrange("p (c h) -> p c h", c=16),
                  dtb_bc.ap.unsqueeze(1).to_broadcast([128, 16, 8]), ALU.add, reads=[pdt, dtb_bc], writes=[dtT])
            sc.act(dtT.ap, dtT.ap, AF.Exp, reads=[dtT], writes=[dtT])
            sc.act(dtT.ap, dtT.ap, AF.Ln, reads=[dtT], writes=[dtT], bias=1.0)
            sc.tt("dve", aT.ap, dtT.ap, A_bc.ap.unsqueeze(1).to_broadcast([128, 16, 8]), ALU.mult,
                  reads=[dtT, A_bc], writes=[aT])

            triU = cst.ap[:, 3, :]
            maskT = cst.ap[:, 2, :]
            ones_f = cst.ap[:, 4, :]
            ST2 = [(ST, ST.ap), (tmp_a[0], tmp_a[0].ap.bitcast(BF16).rearrange("p (h l) -> p h l", h=8))]
            CT2 = [(CTs, CTs.ap), (tmp_a[1], tmp_a[1].ap.bitcast(BF16).rearrange("p (h l) -> p h l", h=8))]
            XDT2 = [(xdt, xdt.ap), (sq_bf[0], sq_bf[0].ap)]
            XDD2 = [(xdd, xdd.ap), (sq_bf[1], sq_bf[1].ap)]
            BTK2 = [(btok, btok.ap), (tmp_a[2], tmp_a[2].ap.bitcast(BF16)[:, 0:256])]

            def front(ci):
                p = ci % 2
                so = p * 32
                tok = slice(ci * 128, (ci + 1) * 128)
                tt = ci // 4
                STb, STa = ST2[p]
                CTb, CTa = CT2[p]
                XTb, XTa = XDT2[p]
                XDb, XDa = XDD2[p]
                BKb, BKa = BTK2[p]
                a_c = aT.ap[:, ci, :]
                pc = sc.psum()
                sc.mm(pc.ap[:, 0:8], triU, a_c, True, True, reads=[cst, aT], writes=[pc])
                negacs = sm.ap[:, so:so + 8]
                sc.ts("dve", negacs, pc.ap[:, 0:8], -1.0, None, ALU.mult, ALU.bypass, reads=[pc],
                      writes=[(sm, ("neg", p))])
                sc.tt("dve", atri.ap, triU.unsqueeze(1).to_broadcast([128, 8, 128]),
                      a_c.unsqueeze(2).to_broadcast([128, 8, 128]), ALU.mult, reads=[cst, aT], writes=[atri])
                pr = [sc.psum(), sc.psum()]
                for hb in range(2):
                    sc.mm(pr[hb].ap, ones_f, atri.ap[:, hb * 4:(hb + 1) * 4, :], True, True,
                          reads=[cst, atri], writes=[pr[hb]])
                for hb in range(2):
                    sc.tt("dve", argm.ap[:, hb * 4:(hb + 1) * 4, :], pr[hb].ap.rearrange("p (h l) -> p h l", h=4),
                          maskT.unsqueeze(1).to_broadcast([128, 4, 128]), ALU.add, reads=[pr[hb], cst],
                          writes=[(argm, hb)])
                for h in range(8):
                    sc.act(argm.ap[:, h, :], argm.ap[:, h, :], AF.Exp, reads=[(argm, h // 4), (sm, ("neg", p))],
                           writes=[(argm, h // 4)], bias=sm.ap[:, so + h:so + h + 1])
                pg = sc.psum()
                for g in range(2):
                    sc.mm(pg.ap[:, g * 128:(g + 1) * 128], bcT.ap[:, g, tok], bcT.ap[:, 2 + g, tok], True, True,
                          reads=[(bcT, (g, tt)), (bcT, (2 + g, tt))], writes=[(pg, g)])
                for g in range(2):
                    sc.tt("dve", STa[:, g * 4:(g + 1) * 4, :], argm.ap[:, g * 4:(g + 1) * 4, :],
                          pg.ap[:, g * 128:(g + 1) * 128].unsqueeze(1).to_broadcast([128, 4, 128]), ALU.mult,
                          reads=[(argm, g), pg], writes=[(STb, g)])
                for hb in range(2):
                    sc.act(atri.ap[:, hb * 4:(hb + 1) * 4, :], pr[hb].ap.rearrange("p (h l) -> p h l", h=4), AF.Exp,
                           reads=[pr[hb]], writes=[atri])
                for g in range(2):
                    sc.tt("dve", CTa[:, g * 4:(g + 1) * 4, :], atri.ap[:, g * 4:(g + 1) * 4, :],
                          bcT.ap[:, 2 + g, tok].unsqueeze(1).to_broadcast([128, 4, 128]), ALU.mult,
                          reads=[atri, (bcT, (2 + g, tt))], writes=[(CTb, g)])
                ptb = sc.psum()
                ptv = ptb.ap.bitcast(BF16)
                for c in range(4):
                    sc.transpose(ptv[:, c * 128:(c + 1) * 128], xsT.ap[:, c, tok], ident_bf.ap,
                                 reads=[(xsT, (c, tt)), ident_bf], writes=[(ptb, c)])
                for g in range(2):
                    sc.transpose(ptv[:, 512 + g * 128:512 + (g + 1) * 128], bcT.ap[:, g, tok], ident_bf.ap,
                                 reads=[(bcT, (g, tt)), ident_bf], writes=[(ptb, 4 + g)])
                dec = sm.ap[:, so + 8:so + 16]
                for hb in range(2):
                    last = pr[hb].ap.rearrange("p (h l) -> p h l", h=4)[:, :, 127]
                    sc.tt("dve", sm.ap[:, so + 8 + hb * 4:so + 12 + hb * 4], last, sm.ap[:, so + hb * 4:so + hb * 4 + 4],
                          ALU.add, reads=[pr[hb], (sm, ("neg", p))], writes=[(sm, ("dec", p))])
                    sc.act(sm.ap[:, so + 16 + hb * 4:so + 20 + hb * 4], last, AF.Exp, reads=[pr[hb]],
                           writes=[(sm, ("eA", p))])
                sc.act(dec, dec, AF.Exp, reads=[(sm, ("dec", p))], writes=[(sm, ("dec", p))])
                sc.tt("dve", sm.ap[:, so + 24:so + 32], dec, dtT.ap[:, ci, :], ALU.mult, reads=[(sm, ("dec", p)), dtT],
                      writes=[(sm, ("dtdec", p))])
                xtok = ptv[:, 0:512].rearrange("p (h e) -> p h e", h=8)
                sc.tt("dve", XTa.rearrange("p (h e) -> p h e", h=8), xtok,
                      dtT.ap[:, ci, :].unsqueeze(2).to_broadcast([128, 8, 64]), ALU.mult, reads=[ptb, dtT], writes=[XTb])
                sc.tt("dve", XDa.rearrange("p (h e) -> p h e", h=8), xtok,
                      sm.ap[:, so + 24:so + 32].unsqueeze(2).to_broadcast([128, 8, 64]), ALU.mult,
                      reads=[ptb, (sm, ("dtdec", p))], writes=[XDb])
                sc.op("act", lambda h, o_=BKa, i_=ptv[:, 512:768]: h.copy(o_, i_), reads=[ptb], writes=[BKb])

            def back(ci):
                p = ci % 2
                so = p * 32
                tok = slice(ci * 128, (ci + 1) * 128)
                tt = ci // 4
                STb, STa = ST2[p]
                CTb, CTa = CT2[p]
                XTb, XTa = XDT2[p]
                XDb, XDa = XDD2[p]
                BKb, BKa = BTK2[p]
                eA = sm.ap[:, so + 16:so + 24]
                py = sc.psum()
                for h in range(8):
                    outp = py.ap[(h % 2) * 64:(h % 2) * 64 + 64, (h // 2) * 128:(h // 2) * 128 + 128]
                    sc.mm(outp, XTa[:, h * 64:(h + 1) * 64], STa[:, h, :], True, ci == 0,
                          reads=[XTb, (STb, h // 4)], writes=[(py, h)])
                    if ci > 0:
                        sc.mm(outp, Hbf.ap[:, h * 64:(h + 1) * 64], CTa[:, h, :], False, True,
                              reads=[Hbf, (CTb, h // 4)], writes=[(py, h)])
                pS = sc.psum()
                for g in range(2):
                    sc.mm(pS.ap[:, g * 256:(g + 1) * 256], BKa[:, g * 128:(g + 1) * 128], XDa[:, g * 256:(g + 1) * 256],
                          True, True, reads=[BKb, XDb], writes=[(pS, g)])
                if ci == 0:
                    sc.copy("dve", Hs.ap, pS.ap, reads=[pS], writes=[Hs])
                else:
                    sc.tt("dve", Hs.ap.rearrange("p (h e) -> p h e", h=8), Hs.ap.rearrange("p (h e) -> p h e", h=8),
                          eA.unsqueeze(2).to_broadcast([128, 8, 64]), ALU.mult, reads=[Hs, (sm, ("eA", p))], writes=[Hs])
                    sc.tt("dve", Hs.ap, Hs.ap, pS.ap, ALU.add, reads=[Hs, pS], writes=[Hs])
                if ci < 15:
                    sc.copy("pool", Hbf.ap, Hs.ap, reads=[Hs], writes=[Hbf])
                for c in range(4):
                    sc.stt("dve", yz.ap[:, c * 128:(c + 1) * 128], xsT.ap[:, c, tok], Dp.ap[:, c:c + 1],
                           py.ap[:, c * 128:(c + 1) * 128], ALU.mult, ALU.add,
                           reads=[(xsT, (c, tt)), Dp, py], writes=[(yz, c)])
                sc.tt("dve", yz.ap.rearrange("p (c t) -> p c t", c=4), yz.ap.rearrange("p (c t) -> p c t", c=4),
                      zsT.ap[:, :, tok], ALU.mult, reads=[yz] + [(zsT, (c, tt)) for c in range(4)], writes=[yz])
                sc.act(sqb.ap, yz.ap, AF.Square, reads=[yz], writes=[sqb])
                sc.copy("pool", yT.ap[:, :, tok], yz.ap.rearrange("p (c t) -> p c t", c=4), reads=[yz],
                        writes=[(yT, (c, tt)) for c in range(4)])
                pss = sc.psum()
                for c in range(4):
                    sc.mm(pss.ap[:, 0:128], ones_bf.ap, sqb.ap[:, c * 128:(c + 1) * 128], c == 0, c == 3,
                          reads=[ones_bf, sqb], writes=[pss])
                rs = rsb2[p]
                sc.act(rs.ap, pss.ap[:, 0:128], AF.Sqrt, reads=[pss], writes=[rs], bias=EPS, scale=1.0 / 512)

            def back_b(ci):
                p = ci % 2
                tok = slice(ci * 128, (ci + 1) * 128)
                tt = ci // 4
                rs = rsb2[p]
                sc.op("dve", lambda h, a=rs.ap: h.reciprocal(a, a), reads=[rs], writes=[rs])
                for c in range(4):
                    sc.stt("dve", yT.ap[:, c, tok], yT.ap[:, c, tok], gn.ap[:, c:c + 1], rs.ap,
                           ALU.mult, ALU.mult, reads=[(yT, (c, tt)), gn, rs], writes=[(yT, (c, tt))])

            front(0)
            for ci in range(16):
                if ci + 1 < 16:
                    front(ci + 1)
                back(ci)
                if ci > 0:
                    back_b(ci - 1)
            back_b(15)

        def mixer_C(l, base=None, plan_only=False):
            if base is None:
                base = len(ring.plan)
                add_win(l, C_C_KC, 512)
                add_win(l, C_C_KW, 280)
                for kv in range(2):
                    ring.add([(lambda s_: s_[0:64, 0:4096].rearrange("p (j h) -> p j h", j=32),
                               w_d["nsa_w1_" + "kv"[kv]][l].rearrange("(j e) h -> e j h", e=64))])
                add_win(l, C_C_Q, 512)
                if plan_only:
                    return base
            Qa = [scr("C_Qa%d" % g, g * 16384, [128, 8192], BF16) for g in range(2)]
            Ks = [scr("C_Ks%d" % g, 32768 + g * 4096, [128, 2048], BF16) for g in range(2)]
            Kw = [scr("C_Kw%d" % g, 40960 + g * 4096, [128, 2048], BF16) for g in range(2)]
            Vs = scr("C_Vs", 49152, [128, 16 * 2 * 66], BF16)
            Vw = scr("C_Vw", 53376, [128, 16 * 2 * 66], BF16)
            Kc = scr("C_Kc", 57600, [128, 2 * 128], BF16)
            Vc = scr("C_Vc", 58112, [128, 2 * 98], BF16)
            gate = scr("C_gate", 58624, [128, 16 * 24], F32)
            PT = [scr("C_PT%d" % i, 60160 + i * 1024, [128, 512], BF16) for i in range(2)]
            o = 62208
            gqk = scr("C_gqk", o, [128, 2], F32); o += 8
            cbias = scr("C_cb", o, [128, 2], F32); o += 8
            w2 = scr("C_w2", o, [128, 2 * 64], BF16); o += 256
            peT = scr("C_peT", o, [64, 64], F32); o += 256
            peTb = scr("C_peTb", o, [64, 64], BF16); o += 128
            mcb = scr("C_mc", o, [128, 512], BF16); o += 1024
            mfb = scr("C_mf", o, [128, 512], BF16); o += 1024
            assert o <= 65536, o
            Vs4 = Vs.ap.rearrange("p (k g e) -> p k g e", k=16, g=2)
            Vw4 = Vw.ap.rearrange("p (k g e) -> p k g e", k=16, g=2)
            Vc3 = Vc.ap.rearrange("p (g e) -> p g e", g=2)
            Kc3 = Kc.ap.rearrange("p (g c) -> p g c", g=2)
            gate3 = gate.ap.rearrange("p (j c) -> p j c", j=16)
            kcvc = scr("C_kcvc", 0, [64, 4 * 2048], BF16)
            kcvc3 = kcvc.ap.rearrange("p (a t) -> p a t", a=4)
            yq = iobuf[0]
            sel = iobuf[1]
            Dcq = cst.ap[:, 5, :]

            sc.dma("sp", gqk.ap[0:64, 0:1], w_d["nsa_q_norm"][l].rearrange("(e o) -> e o", o=1), writes=[(gqk, 0)],
                   skey="ldC")
            sc.dma("sp", gqk.ap[0:64, 1:2], w_d["nsa_k_norm"][l].rearrange("(e o) -> e o", o=1), writes=[(gqk, 1)],
                   skey="ldC")
            sc.ts("dve", gqk.ap[0:64, 0:1], gqk.ap[0:64, 0:1], 0.125, None, ALU.mult, ALU.bypass,
                  reads=[(gqk, 0)], writes=[(gqk, 0)])
            for kv, nm in enumerate(("nsa_pe_k", "nsa_pe_v")):
                sc.dma("sp", peT.ap[:, kv * 32:(kv + 1) * 32], w_d[nm][l].rearrange("j e -> e j"), writes=[(peT, kv)],
                       skey="ldC", slow=True)
                sc.dma("pool", w2.ap[:, kv * 64:(kv + 1) * 64], w_d["nsa_w2_" + "kv"[kv]][l], writes=[(w2, kv)], skey="ldC2")
            sc.copy("dve", peTb.ap, peT.ap, reads=[peT], writes=[peTb])
            sc.dma("pool", mcb.ap, nsa_c["nc_mc"], writes=[mcb], skey="ldC2")
            sc.dma("pool", mfb.ap, nsa_c["nc_mf"], writes=[mfb], skey="ldC2")
            for g in range(2):
                sc.dma("pool", Ks[g].ap[64:68, :], nsa_c["nc_kal"], writes=[(Ks[g], "al")], skey="ldC2")
                sc.dma("pool", Kw[g].ap[64:68, :], nsa_c["nc_kal"], writes=[(Kw[g], "al")], skey="ldC2")
                sc.dma("pool", Ks[g].ap[96:128, :], nsa_c["nc_onehot"], writes=[(Ks[g], "oh")], skey="ldC2")
                sc.op("dve", lambda h, a=Ks[g].ap[64:96, :]: h.memset(a, 0.0), writes=[(Ks[g], "z")])
                sc.dma("pool", Ks[g].ap[64:68, :], nsa_c["nc_kal"], reads=[(Ks[g], "z")], writes=[(Ks[g], "al")], skey="ldC2")
                sc.dma("pool", Kc3[64:68, g, :], nsa_c["nc_kalc"], writes=[(Kc, ("al", g))], skey="ldC2")
                sc.dma("pool", Vc3[:, g, 65:97], nsa_c["nc_ov"], writes=[(Vc, ("ov", g))], skey="ldC2")
                sc.op("dve", lambda h, a=Vc3[:, g, 64:65]: h.memset(a, 1.0), writes=[(Vc, ("one", g))])
            sc.op("dve", lambda h, a=Vs4[:, :, :, 64:65]: h.memset(a, 1.0), writes=[(Vs, "one")])
            sc.op("dve", lambda h, a=Vw4[:, :, :, 64:65]: h.memset(a, 1.0), writes=[(Vw, "one")])


            n64 = [0]

            def norm64(p_ap, pbuf, gcol, out_ap, out_w, n):
                n64[0] += 1
                sq = sq_bf[n64[0] % 2]
                sc.act(sq.ap[0:64, 0:n], p_ap, AF.Square, reads=[pbuf], writes=[sq])
                pss = sc.psum()
                sc.mm(pss.ap[0:64, 0:n], ones_bf.ap[0:64, 0:64], sq.ap[0:64, 0:n], True, True, reads=[ones_bf, sq],
                      writes=[pss])
                rs = tmpf()
                rstd_op("Ck", rs.ap[0:64, 0:n], pss.ap[0:64, 0:n], pss, rs, 1.0 / 64)
                sc.stt("dve", out_ap, p_ap, gqk.ap[0:64, gcol:gcol + 1], rs.ap[0:64, 0:n], ALU.mult, ALU.mult,
                       reads=[pbuf, gqk, rs], writes=out_w)

            w1 = ring.get(base, live=1)
            w1v = wview8(w1, 512)
            for a in range(4):
                for tt in range(NTT):
                    tsl = slice(tt * 512, (tt + 1) * 512)
                    pk = sc.psum()
                    for k in range(NKC):
                        sc.mm(pk.ap[0:64, :], w1v[:, k, a * 64:(a + 1) * 64], hT.ap[:, k, tsl], k == 0, k == NKC - 1,
                              reads=[w1, (hT, (k, tt))], writes=[pk])
                    sc.op("act", lambda h, o_=kcvc3[:, a, tsl], i_=pk.ap[0:64, :]: h.copy(o_, i_), reads=[pk],
                          writes=[(kcvc, (a, tt))])
            def pipe(tasks):
                depth_ = cfg.get("c_pipe", 2)
                pend = [t[0]() for t in tasks[0:depth_]]
                for i, (_, cons) in enumerate(tasks):
                    if i + depth_ < len(tasks):
                        pend.append(tasks[i + depth_][0]())
                    cons(pend.pop(0))

            def kproj(wslot, wv, col0, tt):
                def f():
                    tsl = slice(tt * 512, (tt + 1) * 512)
                    pk = sc.psum()
                    for k in range(NKC):
                        sc.mm(pk.ap[0:64, :], wv[:, k, col0:col0 + 64], hT.ap[:, k, tsl], k == 0, k == NKC - 1,
                              reads=[wslot, (hT, (k, tt))], writes=[pk])
                    return pk
                return f

            tasks = []
            for g in range(2):
                for tt in range(NTT):
                    tsl = slice(tt * 512, (tt + 1) * 512)
                    tasks.append((kproj(w1, w1v, 256 + g * 64, tt),
                                  lambda pk, g=g, tt=tt, tsl=tsl: norm64(pk.ap[0:64, :], pk, 1, Ks[g].ap[0:64, tsl],
                                                                          [(Ks[g], ("k", tt))], 512)))
            pipe(tasks)
            for kb in range(16):
                pv = sc.psum()
                for k in range(NKC):
                    sc.mm(pv.ap[:, 0:128], hT.ap[:, k, kb * 128:(kb + 1) * 128], w1v[:, k, 384:512], k == 0, k == NKC - 1,
                          reads=[w1, (hT, (k, kb // 4))], writes=[pv])
                sc.copy("dve", Vs4[:, kb, :, 0:64], pv.ap[:, 0:128].rearrange("p (g e) -> p g e", g=2), reads=[pv],
                        writes=[(Vs, ("v", kb))])
            w2s = ring.get(base + 1, live=1)
            w2v = wview8(w2s, 280)
            tasks = []
            for g in range(2):
                for tt in range(NTT):
                    tsl = slice(tt * 512, (tt + 1) * 512)
                    tasks.append((kproj(w2s, w2v, g * 64, tt),
                                  lambda pk, g=g, tt=tt, tsl=tsl: norm64(pk.ap[0:64, :], pk, 1, Kw[g].ap[0:64, tsl],
                                                                          [(Kw[g], ("k", tt))], 512)))
            pipe(tasks)
            for kb in range(16):
                pv = sc.psum()
                for k in range(NKC):
                    sc.mm(pv.ap[:, 0:152], hT.ap[:, k, kb * 128:(kb + 1) * 128], w2v[:, k, 128:280], k == 0, k == NKC - 1,
                          reads=[w2s, (hT, (k, kb // 4))], writes=[pv])
                sc.copy("dve", Vw4[:, kb, :, 0:64], pv.ap[:, 0:128].rearrange("p (g e) -> p g e", g=2), reads=[pv],
                        writes=[(Vw, ("v", kb))])
                sc.act(gate3[:, kb, :], pv.ap[:, 128:152], AF.Sigmoid, reads=[pv], writes=[(gate, kb)])
            for kv in range(2):
                ww = ring.get(base + 2 + kv, live=1)
                wwv = ww.ap[0:64, 0:4096].rearrange("p (j h) -> p j h", j=32)
                pb_ = sc.psum()
                for j in range(32):
                    sc.mm(pb_.ap[:, 0:1], wwv[:, j, :], peTb.ap[:, kv * 32 + j:kv * 32 + j + 1], j == 0, j == 31,
                          reads=[ww, peTb], writes=[pb_])
                sc.copy("dve", cbias.ap[:, kv:kv + 1], pb_.ap[:, 0:1], reads=[pb_], writes=[(cbias, kv)])
                for g in range(2):
                    ph = sc.psum()
                    for j in range(32):
                        sc.mm(ph.ap[:, 0:127], wwv[:, j, :], kcvc3[:, kv * 2 + g, j:j + 2017:16], j == 0, j == 31,
                              reads=[ww, kcvc], writes=[ph])
                    hb_ = tmpf()
                    sc.ts("dve", hb_.ap[:, 0:127], ph.ap[:, 0:127], cbias.ap[:, kv:kv + 1], None, ALU.add, ALU.bypass,
                          reads=[ph, (cbias, kv)], writes=[hb_])
                    hid = sq_bf[1]
                    gelu_psum(hb_.ap[:, 0:127], hid.ap[:, 0:127], hb_, [hid])
                    if kv == 0:
                        pc_ = sc.psum()
                        sc.mm(pc_.ap[0:64, 0:127], w2.ap[:, 0:64], hid.ap[:, 0:127], True, True, reads=[(w2, 0), hid],
                              writes=[pc_])
                        norm64(pc_.ap[0:64, 0:127], pc_, 1, Kc3[0:64, g, 0:127], [(Kc, ("k", g))], 127)
                    else:
                        pc_ = sc.psum()
                        sc.mm(pc_.ap[0:127, 0:64], hid.ap[:, 0:127], w2.ap[:, 64:128], True, True, reads=[(w2, 1), hid],
                              writes=[pc_])
                        sc.copy("dve", Vc3[0:127, g, 0:64], pc_.ap[0:127, 0:64], reads=[pc_], writes=[(Vc, ("v", g))])
            sc.barrier()
            for g in range(2):
                sc.dma("pool", Qa[g].ap[64:68, :], nsa_c["nc_qal"][:, g, :], writes=[(Qa[g], "al")], skey="ldC2")
                sc.op("dve", lambda h, a=Qa[g].ap[64:128, :]: h.memset(a, 0.0), writes=[(Qa[g], "z")])
                sc.dma("pool", Qa[g].ap[64:68, :], nsa_c["nc_qal"][:, g, :], reads=[(Qa[g], "z")], writes=[(Qa[g], "al")],
                       skey="ldC2")
            wq = ring.get(base + 4, live=1)
            wqv = wview8(wq, 512)
            def qnorm(pq, hh, tt):
                g, r = hh // 4, hh % 4
                Qv = Qa[g].ap[0:64, :].rearrange("p (j r q) -> p j r q", j=16, r=4)
                sq = sq_bf[(hh * 4 + tt) % 2]
                sc.act(sq.ap[0:64, :], pq.ap[0:64, :], AF.Square, reads=[pq], writes=[sq])
                pss = sc.psum()
                sc.mm(pss.ap[0:64, :], ones_bf.ap[0:64, 0:64], sq.ap[0:64, :], True, True, reads=[ones_bf, sq],
                      writes=[pss])
                rs = tmpf()
                rstd_op("Cq", rs.ap[0:64, :], pss.ap[0:64, :], pss, rs, 1.0 / 64)
                sc.stt("dve", Qv[:, 4 * tt:4 * tt + 4, r, :], pq.ap[0:64, :].rearrange("p (j q) -> p j q", j=4),
                       gqk.ap[0:64, 0:1], rs.ap[0:64, :].rearrange("p (j q) -> p j q", j=4), ALU.mult, ALU.mult,
                       reads=[pq, gqk, rs], writes=[(Qa[g], ("q", hh, tt))])

            tasks = []
            for hh in range(8):
                for tt in range(NTT):
                    tasks.append((kproj(wq, wqv, hh * 64, tt), lambda pq, hh=hh, tt=tt: qnorm(pq, hh, tt)))
            pipe(tasks)

            if cfg.get("c_stop", 9) < 5:
                return
            sc.barrier()
            sc.mark("L%d C attn" % l)
            NB = cfg.get("c_nj", 16)
            acc_c = sc.reserve(2)
            acc_s = sc.reserve(1)[0]
            acc_w = sc.reserve(1)[0]
            order = [(j, g) for j in range(NB) for g in range(2)]
            pt_rr = [0]
            selp_v = selp_sb.ap.rearrange("p (a n) -> p a n", a=2)

            def cmpsel(idx):
                j, g = order[idx]
                par = idx % 2
                pc_ = acc_c[par]
                so = par * 16
                ps = sc.psum()
                sc.mm(ps.ap[0:127, :], Kc3[0:68, g, 0:127], Qa[g].ap[0:68, j * 512:(j + 1) * 512], True, True,
                      reads=[(Qa[g], ("blk", j))], writes=[ps])
                mk = tmpf()
                sc.ts("dve", mk.ap[0:127, 0:128], Dcq[0:127, :], float(128 * j - 31), NEGM, ALU.is_gt, ALU.mult,
                      reads=[cst], writes=[mk])
                sc.tt("dve", sel.ap[0:127, 0:512].rearrange("p (r q) -> p r q", r=4),
                      ps.ap[0:127, :].rearrange("p (r q) -> p r q", r=4),
                      mk.ap[0:127, 0:128].unsqueeze(1).to_broadcast([127, 4, 128]), ALU.add, reads=[ps, mk],
                      writes=[(sel, "s")])
                pt_rr[0] += 1
                pt = PT[pt_rr[0] % 2]
                sc.act(pt.ap[0:127, :], sel.ap[0:127, 0:512], AF.Exp, reads=[(sel, "s")], writes=[pt])
                for r in range(4):
                    sc.mm(pc_.ap[:, r * 128:r * 128 + 97], pt.ap[0:127, r * 128:(r + 1) * 128], Vc3[0:127, g, 0:97],
                          True, True, reads=[pt], writes=[(pc_, r)])
                pc4 = pc_.ap.rearrange("p (r c) -> p r c", r=4)
                rc = small.ap[:, so:so + 4]
                sc.ts("dve", rc, pc4[:, :, 64], 1e-30, None, ALU.max, ALU.bypass, reads=[pc_], writes=[(small, ("rc", par))])
                sc.op("dve", lambda h, a=rc: h.reciprocal(a, a), reads=[(small, ("rc", par))], writes=[(small, ("rc", par))])
                imp = sel.ap[:, 512:544]
                sc.ts("dve", imp, pc4[:, 0, 65:97], small.ap[:, so:so + 1], None, ALU.mult, ALU.bypass,
                      reads=[pc_, (small, ("rc", par))], writes=[(sel, "imp")])
                for r in range(1, 4):
                    sc.stt("dve", imp, pc4[:, r, 65:97], small.ap[:, so + r:so + r + 1], imp, ALU.mult, ALU.add,
                           reads=[pc_, (small, ("rc", par)), (sel, "imp")], writes=[(sel, "imp")])
                sc.tt("dve", small.ap[:, so + 4:so + 8], rc, gate3[:, j, g * 12:g * 12 + 12:3], ALU.mult,
                      reads=[(small, ("rc", par)), (gate, j)], writes=[(small, ("cfc", par))])
                sc.tt("dve", imp, imp, selp_v[:, 0, 31 - 2 * j:63 - 2 * j], ALU.mult, reads=[(sel, "imp"), selp_sb],
                      writes=[(sel, "imp")])
                sc.tt("dve", imp, imp, selp_v[:, 1, 31 - 2 * j:63 - 2 * j], ALU.add, reads=[(sel, "imp"), selp_sb],
                      writes=[(sel, "imp")])
                sc.op("dve", lambda h, a=sel.ap[:, 512:513]: h.memset(a, 1e9), reads=[(sel, "imp")], writes=[(sel, "imp")])
                top8 = sel.ap[:, 544:552]
                sc.op("dve", lambda h, o_=top8, i_=imp: h.max(o_, i_), reads=[(sel, "imp")], writes=[(sel, "top")])
                sneg = sel.ap[:, 640:768]
                sc.op("dve", lambda h, a=sel.ap[:, 640:736]: h.memset(a, 0.0), writes=[(sel, "sneg")])
                sc.ts("dve", sel.ap[:, 736:768], imp, sel.ap[:, 551:552], -32768.0, ALU.is_lt, ALU.mult,
                      reads=[(sel, "imp"), (sel, "top")], writes=[(sel, "sneg")])

            def cmpsel_b(idx):
                j, g = order[idx]
                sneg = sel.ap[:, 640:768]
                ptr = sc.psum()
                sc.transpose(ptr.ap[:, 0:128], sneg, ident.ap, reads=[(sel, "sneg"), cst], writes=[ptr])
                Qv = Qa[g].ap[96:128, j * 512:(j + 1) * 512].rearrange("p (r q) -> p r q", r=4)
                sc.copy("dve", Qv, ptr.ap[96:128, 0:128].unsqueeze(1).to_broadcast([32, 4, 128]), reads=[ptr],
                        writes=[(Qa[g], ("blk", j))])

            def attends(idx):
                j, g = order[idx]
                items = []
                for kb in range(j + 1):
                    items.append((Ks[g].ap[:, kb * 128:(kb + 1) * 128], 128, mcb if kb == j else None, acc_s,
                                  Vs4[:, kb, g, 0:65], kb == 0))
                kbs = [kb for kb in (j - 2, j - 1, j) if kb >= 0]
                for kb in kbs:
                    mb = mcb if kb == j else (mfb if kb == j - 2 else None)
                    items.append((Kw[g].ap[0:68, kb * 128:(kb + 1) * 128], 68, mb, acc_w, Vw4[:, kb, g, 0:65],
                                  kb == kbs[0]))

                def qk(it):
                    Kt, krows, maskb, po, Vap, first = it
                    ps = sc.psum()
                    sc.mm(ps.ap, Kt, Qa[g].ap[0:krows, j * 512:(j + 1) * 512], True, maskb is None,
                          reads=[(Qa[g], ("blk", j))], writes=[ps])
                    if maskb is not None:
                        sc.mm(ps.ap, ident_bf.ap, maskb.ap, False, True, reads=[], writes=[ps])
                    return ps

                def pv(it, ps):
                    Kt, krows, maskb, po, Vap, first = it
                    pt_rr[0] += 1
                    pt = PT[pt_rr[0] % 2]
                    sc.act(pt.ap, ps.ap, AF.Exp, reads=[ps], writes=[pt])
                    for r in range(4):
                        sc.op("pe", lambda h, o_=po.ap[:, r * 128:r * 128 + 65], a_=pt.ap[:, r * 128:(r + 1) * 128],
                              b_=Vap, st=(first and r == 0): h.matmul(o_, a_, b_, start=st, stop=True,
                                                                       skip_group_check=True),
                              reads=[pt], writes=[po])

                pend = qk(items[0])
                for i, it in enumerate(items):
                    nxt = qk(items[i + 1]) if i + 1 < len(items) else None
                    pv(it, pend)
                    pend = nxt

            def combine(idx):
                j, g = order[idx]
                par = idx % 2
                so = par * 16
                yv = yq.ap[:, g * 256:(g + 1) * 256].rearrange("p (r e) -> p r e", r=4)
                for bi, pob in enumerate((acc_c[par], acc_s, acc_w)):
                    p4 = pob.ap.rearrange("p (r c) -> p r c", r=4)
                    if bi == 0:
                        cf = small.ap[:, so + 4:so + 8]
                        cfk = (small, ("cfc", par))
                    else:
                        cf = small.ap[:, 32 + bi * 4:36 + bi * 4]
                        cfk = (small, ("cf", bi))
                        sc.op("dve", lambda h, o_=cf, i_=p4[:, :, 64]: h.reciprocal(o_, i_), reads=[pob], writes=[cfk])
                        sc.tt("dve", cf, cf, gate3[:, j, g * 12 + bi:g * 12 + 12:3], ALU.mult,
                              reads=[cfk, (gate, j)], writes=[cfk])
                    cfb = cf.unsqueeze(2).to_broadcast([128, 4, 64])
                    if bi == 0:
                        sc.tt("dve", yv, p4[:, :, 0:64], cfb, ALU.mult, reads=[pob, cfk], writes=[(yq, g)])
                    else:
                        tm = tmpf()
                        tv = tm.ap[:, 0:256].rearrange("p (r e) -> p r e", r=4)
                        sc.tt("dve", tv, p4[:, :, 0:64], cfb, ALU.mult, reads=[pob, cfk], writes=[tm])
                        sc.tt("pool", yv, yv, tv, ALU.add, reads=[tm, (yq, g)], writes=[(yq, g)])
                if g == 1:
                    tok = slice(j * 128, (j + 1) * 128)
                    pty = sc.psum()
                    for c in range(4):
                        sc.transpose(pty.ap[:, c * 128:(c + 1) * 128], yq.ap[:, c * 128:(c + 1) * 128], ident.ap,
                                     reads=[yq, cst], writes=[(pty, c)])
                    sc.copy("dve", yT.ap[:, :, tok], pty.ap.rearrange("p (c t) -> p c t", c=4), reads=[pty],
                            writes=[(yT, (c, j // 4)) for c in range(4)])

            cmpsel(0)
            cmpsel_b(0)
            for idx in range(len(order)):
                if idx + 1 < len(order):
                    cmpsel(idx + 1)
                attends(idx)
                if idx + 1 < len(order):
                    cmpsel_b(idx + 1)
                combine(idx)
            sc.release(acc_c + [acc_s, acc_w])

        spill_v = xspill_d.rearrange("p (c t) -> p c t", c=NKC)
        mixfn = {"A": mixer_A, "B": mixer_B, "C": mixer_C, "D": mixer_D}
        stages = []
        for l in range(depth):
            if cfg.get("ffn1", True):
                stages.append(("L%d ffn1" % l, lambda l=l: ffn(l, 1, plan_only=True), lambda b, l=l: ffn(l, 1, base=b)))
            if cfg.get("mix", True):
                stages.append(("L%d mixpre" % l, None, lambda b, l=l: mix_pre(l)))
                first = True
                for mi, mname in enumerate("ABCD"):
                    if mname not in mixers:
                        continue
                    stages.append(("L%d mixer %s" % (l, mname), lambda l=l, f=mixfn[mname]: f(l, plan_only=True),
                                   lambda b, l=l, f=mixfn[mname]: f(l, base=b)))
                    stages.append(("L%d merge %s" % (l, mname), lambda l=l, mi=mi: merge(l, mi, False, plan_only=True),
                                   lambda b, l=l, mi=mi, first=first: (merge(l, mi, first, base=b), sc.barrier())))
                    first = False
                stages.append(("L%d outproj" % l, lambda l=l: out_proj(l, plan_only=True),
                               lambda b, l=l: (sc.dma("sp", xT.ap, spill_v, writes=[xT], skey="spill"), out_proj(l, base=b))))
            if cfg.get("ffn2", True):
                stages.append(("L%d ffn2" % l, lambda l=l: ffn(l, 2, plan_only=True), lambda b, l=l: ffn(l, 2, base=b)))

        def mix_pre(l):
            rmsnorm_to_hT(3 * l + 2)
            sc.dma("sp", spill_v, xT.ap, reads=[xT], skey="spill")
            sc.barrier()

        bases = {}

        def plan(i):
            if i < len(stages) and i not in bases:
                bases[i] = stages[i][1]() if stages[i][1] is not None else None

        for i, (nm, pf, rf) in enumerate(stages):
            plan(i)
            k = i + 1
            plan(k)
            while k < len(stages) and stages[k][1] is None:
                k += 1
                plan(k)
            sc.mark(nm)
            rf(bases[i])

        finals = []
        for t in range(S // 128):
            io = iobuf[t % 2]
            for half in range(2):
                ps = sc.psum()
                for j in range(4):
                    c = half * 4 + j
                    sc.transpose(ps.ap[:, j * 128:(j + 1) * 128], xT.ap[:, c, t * 128:(t + 1) * 128], ident.ap,
                                 reads=[(xT, (c, t // 4)), cst], writes=[(ps, j)])
                sc.copy("dve", io.ap[:, half * 512:(half + 1) * 512], ps.ap, reads=[ps], writes=[(io, half)])
            sc.dma("sp", out_d[t * 128:(t + 1) * 128, :], io.ap, reads=[io], skey=("io", t % 2))
        for key in (("io", 0), ("io", 1)):
            ds = sc.dsems[key]
            finals.append((ds[0], ds[1]))
        sc.mark("end")
        print("ENGINE OPS", {n: (e.count, len(e.ops)) for n, e in sc.eng.items()}, flush=True)
        print("MARKS", sc.marks, flush=True)
        sc.emit(finals)
    return nc


_CST = host_consts()


def kernel(**inputs):
    cfg = inputs.pop("_cfg", {})
    depth = cfg.get("depth", DEPTH)
    nc = build_program(cfg)
    x = np.ascontiguousarray(inputs["x"], dtype=np.float32)
    shared = {"cst": _CST}
    shared.update(_NSAC)
    for nm, _ in W_NAMES:
        shared[nm] = np.ascontiguousarray(np.asarray(inputs[nm], dtype=np.float32)[:depth])
    in_maps = []
    for b in range(8):
        m = dict(shared)
        m["x"] = x[b]
        in_maps.append(m)
    res = run_bass_kernel_spmd(nc, in_maps, core_ids=list(range(8)))
    return np.stack([np.asarray(r["out"], dtype=np.float32) for r in res.results], axis=0)
```

```python
import math
from contextlib import ExitStack

import numpy as np
import concourse.bass as bass
import concourse.mybir as mybir
from concourse.bass_utils import run_bass_kernel_spmd

F32 = mybir.dt.float32
BF16 = mybir.dt.bfloat16
AF = mybir.ActivationFunctionType
ALU = mybir.AluOpType
AX = mybir.AxisListType

D = 1024
S = 2048
DEPTH = 4
FFN = 2816
NKC = D // 128
NTT = S // 512
EPS = 1e-6
IN_PROJ = 8992


class Buf:
    def __init__(self, ap, name):
        self.ap = ap
        self.name = name
        self.st = {}

    def __getitem__(self, idx):
        return self.ap[idx]


class Eng:
    def __init__(self, name, sem):
        self.name = name
        self.sem = sem
        self.count = 0
        self.waited = {}
        self.ops = []


class Sched:
    def __init__(self, nc, stack):
        self.nc = nc
        self.stack = stack
        self.eng = {}
        for n in ("pe", "dve", "act", "pool", "sp"):
            self.eng[n] = Eng(n, stack.enter_context(nc.semaphore("s_" + n)))
        self.dsems = {}
        self.psum_banks = []
        self.psum_rr = 0
        self.n_ops = 0
        self.marks = []

    def sbuf(self, name, shape, dtype):
        t = self.stack.enter_context(self.nc.sbuf_tensor("sb_" + name, list(shape), dtype))
        return Buf(t[:], name)

    def view(self, ap, name):
        return Buf(ap, name)

    def dsem(self, key):
        if key not in self.dsems:
            self.dsems[key] = [self.stack.enter_context(self.nc.semaphore("d_" + str(key))), 0]
        return self.dsems[key]

    @staticmethod
    def _norm(acc):
        out = []
        for a in acc:
            if isinstance(a, tuple):
                out.append(a)
            else:
                out.append((a, None))
        return out

    @staticmethod
    def _states(buf, key):
        if key is None:
            return list(buf.st.values())
        r = []
        if key in buf.st:
            r.append(buf.st[key])
        if None in buf.st:
            r.append(buf.st[None])
        return r

    def _collect(self, engname, reads, writes):
        deps = []
        for (b, k) in reads:
            for st in self._states(b, k):
                if st[0] is not None:
                    deps.append((st[0], "raw"))
        for (b, k) in writes:
            for st in self._states(b, k):
                if st[0] is not None:
                    deps.append((st[0], "waw"))
                for t in st[1].values():
                    deps.append((t, "war"))
        return deps

    def _emit_waits(self, e, deps):
        for (tok, kind) in deps:
            sem, val, teng = tok[0], tok[1], tok[2]
            if teng is None:
                val = max(val, tok[3][1])
            if teng == e.name:
                if e.name == "pe":
                    continue
                if kind != "raw":
                    continue
            sid = id(sem)
            if e.waited.get(sid, 0) >= val:
                continue
            e.waited[sid] = val
            e.ops.append(("wait", sem, val))

    def _update(self, tok, reads, writes):
        sid = id(tok[0])
        for (b, k) in writes:
            if k is None:
                b.st = {None: [tok, {}]}
            else:
                b.st[k] = [tok, {}]
        for (b, k) in reads:
            st = b.st.setdefault(k, [None, {}])
            old = st[1].get(sid)
            if old is None or old[1] < tok[1]:
                st[1][sid] = tok

    def op(self, engname, fn, reads=(), writes=()):
        e = self.eng[engname]
        reads = self._norm(reads)
        writes = self._norm(writes)
        self._emit_waits(e, self._collect(engname, reads, writes))
        e.count += 1
        tok = (e.sem, e.count, e.name)
        e.ops.append(("op", fn, e.sem, 1))
        self._update(tok, reads, writes)
        self.n_ops += 1
        return tok

    def dma(self, engname, out, in_, reads=(), writes=(), skey=None, slow=False):
        e = self.eng[engname]
        reads = self._norm(reads)
        writes = self._norm(writes)
        self._emit_waits(e, self._collect(engname, reads, writes))
        ds = self.dsem(skey)
        if ds[1] > 0 and e.waited.get(id(ds[0]), 0) < ds[1]:
            e.waited[id(ds[0])] = ds[1]
            e.ops.append(("wait", ds[0], ds[1]))
        ds[1] += 16
        tok = (ds[0], ds[1], None, ds)
        if slow:
            e.ops.append(("op", lambda h: h.dma_start(out=out, in_=in_, allow_slow_non_contiguous=True), ds[0], 16))
        else:
            e.ops.append(("op", lambda h: h.dma_start(out=out, in_=in_), ds[0], 16))
        self._update(tok, reads, writes)
        self.n_ops += 1
        return tok

    def mark(self, name):
        self.marks.append((name, self.eng["pe"].count))

    def barrier(self):
        toks = [(e.sem, e.count, "*") for e in self.eng.values() if e.count > 0]
        toks += [(ds[0], ds[1], None, ds) for ds in self.dsems.values() if ds[1] > 0]
        for e in self.eng.values():
            for tk in toks:
                sem, val = tk[0], tk[1]
                if sem is e.sem and e.name != "sp":
                    pass
                sid = id(sem)
                if e.waited.get(sid, 0) >= val:
                    continue
                e.waited[sid] = val
                e.ops.append(("wait", sem, val))

    def make_psum(self):
        for i in range(8):
            t = self.stack.enter_context(self.nc.psum_tensor("ps%d" % i, [128, 512], F32))
            self.psum_banks.append(Buf(t[:], "ps%d" % i))
        self.psum_rot = list(self.psum_banks)

    def psum(self):
        b = self.psum_rot[self.psum_rr % len(self.psum_rot)]
        self.psum_rr += 1
        return b

    def reserve(self, n):
        got = self.psum_rot[-n:]
        self.psum_rot = self.psum_rot[:-n]
        return got

    def release(self, banks):
        self.psum_rot = self.psum_rot + list(banks)

    def mm(self, out, lhsT, rhs, start, stop, reads, writes):
        return self.op("pe", lambda h: h.matmul(out, lhsT, rhs, start=start, stop=stop), reads, writes)

    def transpose(self, out, in_, ident, reads, writes):
        return self.op("pe", lambda h: h.transpose(out, in_, ident), reads, writes)

    def act(self, out, in_, func, reads, writes, bias=0.0, scale=1.0, eng="act"):
        return self.op(eng, lambda h: h.activation(out, in_, func, bias=bias, scale=scale), reads, writes)

    def tt(self, eng, out, in0, in1, op, reads, writes):
        return self.op(eng, lambda h: h.tensor_tensor(out, in0, in1, op), reads, writes)

    def ts(self, eng, out, in0, s1, s2, op0, op1, reads, writes):
        return self.op(eng, lambda h: h.tensor_scalar(out, in0, s1, s2, op0, op1), reads, writes)

    def stt(self, eng, out, in0, scalar, in1, op0, op1, reads, writes):
        return self.op(eng, lambda h: h.scalar_tensor_tensor(out, in0, scalar, in1, op0, op1), reads, writes)

    def copy(self, eng, out, in_, reads, writes):
        return self.op(eng, lambda h: h.tensor_copy(out, in_), reads, writes)

    def emit(self, final_waits):
        nc = self.nc
        hmap = {"pe": "tensor", "dve": "vector", "act": "scalar", "pool": "gpsimd", "sp": "sync"}
        with nc.Block() as block:
            for n, e in self.eng.items():
                ops = list(e.ops)
                if n == "sp":
                    ops = ops + [("wait", s, v) for (s, v) in final_waits]

                def body(h, ops=ops):
                    for o in ops:
                        if o[0] == "wait":
                            h.wait_ge(o[1], o[2])
                        else:
                            o[1](h).then_inc(o[2], o[3])

                getattr(block, hmap[n])(body)


class WRing:
    def __init__(self, sc, name, nslots, elems):
        self.sc = sc
        self.slots = [sc.sbuf("%s%d" % (name, i), [128, elems], BF16) for i in range(nslots)]
        self.n = nslots
        self.plan = []
        self.issued = 0
        self.name = name

    def add(self, parts):
        self.plan.append(parts)
        return len(self.plan) - 1

    def _issue(self, hi):
        hi = min(len(self.plan), hi)
        while self.issued < hi:
            j = self.issued
            slot = self.slots[j % self.n]
            for pi, (vf, src) in enumerate(self.plan[j]):
                self.sc.dma("pool", vf(slot.ap), src, reads=(), writes=[slot],
                            skey=(self.name, j % self.n))
            self.issued += 1

    def done(self, j):
        self._issue(j + self.n + 1)

    def get(self, i, live=None):
        if live is None:
            self._issue(i + 1)
        else:
            self._issue(max(i + 1, i - live + 1 + self.n))
        return self.slots[i % self.n]


C_A_U, C_A_V = 0, 512
C_B_Z, C_B_X, C_B_B, C_B_C, C_B_DT = 1024, 1536, 2048, 2304, 2560
C_C_Q, C_C_KC, C_C_VC, C_C_KS, C_C_VS, C_C_KW, C_C_VW, C_C_G = 2568, 3080, 3208, 3336, 3464, 3592, 3720, 3848
C_D_A, C_D_B = 3872, 4384
C_MERGE = 4896
NEGM = -30000.0

W_NAMES = [("ffn1_norm", [D]), ("ffn1_w_in", [D, 2 * FFN]), ("ffn1_w_out", [FFN, D]),
           ("mix_norm", [D]), ("w_in", [D, IN_PROJ]),
           ("sgu_v_norm", [512]), ("sgu_w", [4, 128, 128]), ("sgu_b", [4, 128]),
           ("ssm_conv_w", [4, 1024]), ("ssm_conv_b", [1024]), ("ssm_dt_bias", [8]), ("ssm_a_log", [8]),
           ("ssm_d", [8]), ("ssm_norm", [512]),
           ("nsa_q_norm", [64]), ("nsa_k_norm", [64]), ("nsa_pe_k", [32, 64]), ("nsa_w1_k", [2048, 128]),
           ("nsa_w2_k", [128, 64]), ("nsa_pe_v", [32, 64]), ("nsa_w1_v", [2048, 128]), ("nsa_w2_v", [128, 64]),
           ("conv_dw_w", [31, 512]), ("conv_dw_b", [512]), ("conv_norm", [512]),
           ("w_branch", [4, 512, 1024]), ("w_out", [D, D]),
           ("ffn2_norm", [D]), ("ffn2_w_in", [D, 2 * FFN]), ("ffn2_w_out", [FFN, D])]


def host_consts():
    c = np.zeros((128, 6, 128), np.float32)
    r = np.arange(128)
    c[:, 0, :] = np.eye(128)
    c[:, 1, :] = (r[None, :] <= r[:, None])
    c[:, 2, :] = np.where(r[None, :] >= r[:, None], 0.0, NEGM)
    c[:, 3, :] = (r[:, None] <= r[None, :])
    c[:, 4, :] = 1.0
    c[:, 5, :] = 16.0 * r[:, None] - r[None, :]
    return c


def host_nsa_consts():
    t = np.arange(S)
    hi, lo = (t // 64).astype(np.float32), (t % 64).astype(np.float32)
    slopes = np.array([2.0 ** (-(i + 1)) for i in range(8)], np.float32)
    qal = np.zeros((4, 2, 16, 4, 128), np.float32)
    for h in range(8):
        g, r = h // 4, h % 4
        sl = slopes[h]
        qal[0, g, :, r, :] = (-sl * 64.0 * hi).reshape(16, 128)
        qal[1, g, :, r, :] = (-sl * lo).reshape(16, 128)
        qal[2, g, :, r, :] = sl * 64.0
        qal[3, g, :, r, :] = sl
    kal = np.stack([np.ones(S, np.float32), np.ones(S, np.float32), hi, lo], 0)
    cm = np.arange(128) * 16 + 15.5
    kalc = np.stack([np.ones(128), np.ones(128), np.floor(cm / 64.0), cm - 64.0 * np.floor(cm / 64.0)], 0).astype(np.float32)
    onehot = (np.arange(32)[:, None] == (t // 64)[None, :]).astype(np.float32)
    r = np.arange(128)
    mc = np.where(r[None, :] >= r[:, None], 0.0, NEGM).astype(np.float32)
    mf = np.where(r[:, None] > r[None, :], 0.0, NEGM).astype(np.float32)
    c_start = np.arange(128) * 16
    s_start = np.arange(32) * 64
    ov = ((c_start[:, None] <= s_start[None, :] + 63) & (c_start[:, None] + 31 >= s_start[None, :])).astype(np.float32)
    selp = np.zeros((128, 2, 63), np.float32)
    for npr in range(63):
        d = npr - 31
        for q in range(128):
            up = q >= 64
            if d < -1 or (d == -1 and up):
                selp[q, 0, npr], selp[q, 1, npr] = 1.0, 0.0
            elif (d == -1 and not up) or d == 0 or (d == 1 and up):
                selp[q, 0, npr], selp[q, 1, npr] = 0.0, 1e9
            else:
                selp[q, 0, npr], selp[q, 1, npr] = 0.0, -1e9
    return {"nc_qal": qal.reshape(4, 2, 8192), "nc_kal": kal, "nc_kalc": kalc, "nc_onehot": onehot,
            "nc_mc": np.tile(mc, (1, 4)), "nc_mf": np.tile(mf, (1, 4)), "nc_ov": ov, "nc_selp": selp}


_NSAC = host_nsa_consts()


def build_program(cfg):
    depth = cfg.get("depth", DEPTH)
    mixers = cfg.get("mixers", "ABCD")
    nc = bass.Bass("TRN2", target_bir_lowering=False)

    def din(name, shape, dtype=F32):
        return nc.dram_tensor(name, list(shape), dtype, kind="ExternalInput").ap()

    x_d = din("x", [S, D])
    cst_d = din("cst", [128, 6, 128])
    nsa_c = {k: din(k, list(v.shape)) for k, v in _NSAC.items()}
    w_d = {}
    for nm, shp in W_NAMES:
        w_d[nm] = din(nm, [depth] + shp)
    out_d = nc.dram_tensor("out", [S, D], F32, kind="ExternalOutput").ap()
    xspill_d = nc.dram_tensor("xspill", [128, NKC * S], F32, kind="Internal").ap()

    with ExitStack() as stack:
        sc = Sched(nc, stack)
        sc.make_psum()

        xT = sc.sbuf("xT", [128, NKC, S], F32)
        hT = sc.sbuf("hT", [128, NKC, S], BF16)
        mT = sc.sbuf("mT", [128, NKC, S], BF16)
        actT = sc.sbuf("actT", [128, 4, S], BF16)
        yT = actT
        cst = sc.sbuf("cst", [128, 6, 128], F32)
        ident = Buf(cst.ap[:, 0, :], "ident")
        ident_bf = sc.sbuf("ident_bf", [128, 128], BF16)
        ones_bf = sc.sbuf("ones_bf", [128, 128], BF16)
        gains = sc.sbuf("gains", [128, 3 * DEPTH, NKC], F32)
        iobuf = [sc.sbuf("io%d" % i, [128, D], F32) for i in range(2)]
        tmp_a = [sc.sbuf("tmpa%d" % i, [128, 512], F32) for i in range(3)]
        sq_bf = [sc.sbuf("sq%d" % i, [128, 512], BF16) for i in range(2)]
        rstd = sc.sbuf("rstd", [128, 512], F32)
        small = sc.sbuf("small", [128, 64], F32)
        selp_sb = sc.sbuf("selp", [128, 126], F32)
        ring = WRing(sc, "wr", 4, 4096)
        tmp_rr = [0]

        def tmpf():
            tmp_rr[0] += 1
            return tmp_a[tmp_rr[0] % 3]

        xflat = xT.ap.rearrange("p c t -> p (c t)")
        xflat_bf = xflat.bitcast(BF16)

        def scr(name, off, shape, dtype):
            n = int(np.prod(shape[1:]))
            if dtype == F32:
                assert off % 4 == 0
                ap = xflat[0:shape[0], off // 4: off // 4 + n]
            else:
                ap = xflat_bf[0:shape[0], off // 2: off // 2 + n]
            if len(shape) == 3:
                ap = ap.rearrange("p (a b) -> p a b", a=shape[1])
            return Buf(ap, name)

        sc.dma("sp", cst.ap, cst_d, writes=[cst], skey="c0")
        sc.op("dve", lambda h: h.memset(ones_bf.ap, 1.0), writes=[ones_bf])
        sc.dma("sp", selp_sb.ap, nsa_c["nc_selp"].rearrange("p a n -> p (a n)"), writes=[selp_sb], skey="c0")
        sc.copy("dve", ident_bf.ap, ident.ap, reads=[cst], writes=[ident_bf])
        for l in range(depth):
            for wi, nm in enumerate(("ffn1_norm", "ffn2_norm", "mix_norm")):
                sc.dma("sp", gains.ap[:, 3 * l + wi, :], w_d[nm][l].rearrange("(c p) -> p c", p=128),
                       writes=[(gains, (l, wi))], skey="c1", slow=True)

        for t in range(S // 128):
            io = iobuf[t % 2]
            sc.dma("sp", io.ap, x_d[t * 128:(t + 1) * 128, :], writes=[io], skey=("io", t % 2))
            for half in range(2):
                ps = sc.psum()
                for j in range(4):
                    c = half * 4 + j
                    sc.transpose(ps.ap[:, j * 128:(j + 1) * 128], io.ap[:, c * 128:(c + 1) * 128], ident.ap,
                                 reads=[io, cst], writes=[(ps, j)])
                sc.copy("dve", xT.ap[:, half * 4:half * 4 + 4, t * 128:(t + 1) * 128],
                        ps.ap.rearrange("p (c t) -> p c t", c=4),
                        reads=[ps], writes=[(xT, (half * 4 + j, t // 4)) for j in range(4)])

        RS_LN = set(cfg.get("rs_ln", ()))

        def rstd_op(site, o_ap, i_ap, ibuf, obuf, scale):
            if site in RS_LN:
                sc.act(o_ap, i_ap, AF.Ln, reads=[ibuf], writes=[obuf], bias=EPS, scale=scale)
                sc.act(o_ap, o_ap, AF.Exp, reads=[obuf], writes=[obuf], scale=-0.5)
            else:
                sc.act(o_ap, i_ap, AF.Sqrt, reads=[ibuf], writes=[obuf], bias=EPS, scale=scale)
                sc.op("dve", lambda h: h.reciprocal(o_ap, o_ap), reads=[obuf], writes=[obuf])

        def rmsnorm_to_hT(gidx):
            for tt in range(NTT):
                tsl = slice(tt * 512, (tt + 1) * 512)
                ps = sc.psum()
                for c in range(NKC):
                    sq = sq_bf[c % 2]
                    sc.act(sq.ap, xT.ap[:, c, tsl], AF.Square, reads=[(xT, (c, tt))], writes=[sq])
                    sc.mm(ps.ap, ones_bf.ap, sq.ap, c == 0, c == NKC - 1, reads=[ones_bf, sq], writes=[ps])
                rstd_op("hT", rstd.ap, ps.ap, ps, rstd, 1.0 / D)
                for c in range(NKC):
                    sc.stt("dve", hT.ap[:, c, tsl], xT.ap[:, c, tsl],
                           gains.ap[:, gidx, c:c + 1], rstd.ap, ALU.mult, ALU.mult,
                           reads=[(xT, (c, tt)), gains, rstd], writes=[(hT, (c, tt))])

        def wview8(slot, cw):
            return slot.ap[:, 0:8 * cw].rearrange("p (k c) -> p k c", k=8)

        def add_win(l, c0, cw):
            return ring.add([(lambda s, cw=cw: s[:, 0:8 * cw].rearrange("p (k c) -> p k c", k=8),
                              w_d["w_in"][l][:, c0:c0 + cw].rearrange("(k p) c -> p k c", p=128))])

        def inproj_fm(ps_ap, wslot, wv, col0, tok_sl, tt):
            for k in range(NKC):
                sc.mm(ps_ap, wv[:, k, col0:col0 + 128], hT.ap[:, k, tok_sl], k == 0, k == NKC - 1,
                      reads=[wslot, (hT, (k, tt))], writes=[])

        def ffn(l, which, base=None, plan_only=False):
            w_in = w_d["ffn%d_w_in" % which][l]
            w_out = w_d["ffn%d_w_out" % which][l]
            groups = [(f0, min(4, 22 - f0)) for f0 in range(0, 22, 4)]
            if base is not None:
                rmsnorm_to_hT(3 * l + (which - 1))
            for (f0, nf) in (groups if base is None else []):
                cw = nf * 128
                for off in (0, FFN):
                    ring.add([(lambda s, cw=cw: s[:, 0:8 * cw].rearrange("p (k c) -> p k c", k=8),
                               w_in[:, off + f0 * 128: off + f0 * 128 + cw].rearrange("(k p) c -> p k c", p=128))])
                ring.add([(lambda s, nf=nf: s[:, 0:nf * 1024].rearrange("p (f c) -> p f c", f=nf),
                           w_out[f0 * 128:(f0 + nf) * 128, :].rearrange("(f p) c -> p f c", p=128))])
            if base is None:
                base = len(ring.plan) - 3 * len(groups)
                if plan_only:
                    return base
                rmsnorm_to_hT(3 * l + (which - 1))
            for gi, (f0, nf) in enumerate(groups):
                cw = nf * 128
                wg = ring.get(base + 3 * gi, live=1)
                wu = ring.get(base + 3 * gi + 1, live=2)
                wgv = wview8(wg, cw)
                wuv = wview8(wu, cw)
                for tt in range(NTT):
                    tsl = slice(tt * 512, (tt + 1) * 512)
                    for fi in range(nf):
                        pg = sc.psum()
                        pu = sc.psum()
                        for k in range(NKC):
                            sc.mm(pg.ap, wgv[:, k, fi * 128:(fi + 1) * 128], hT.ap[:, k, tsl], k == 0, k == NKC - 1,
                                  reads=[wg, (hT, (k, tt))], writes=[pg])
                        for k in range(NKC):
                            sc.mm(pu.ap, wuv[:, k, fi * 128:(fi + 1) * 128], hT.ap[:, k, tsl], k == 0, k == NKC - 1,
                                  reads=[wu, (hT, (k, tt))], writes=[pu])
                        tm = tmpf()
                        sc.act(tm.ap, pg.ap, AF.Silu, reads=[pg], writes=[tm])
                        sc.tt("dve", actT.ap[:, fi, tsl], tm.ap, pu.ap, ALU.mult,
                              reads=[tm, pu], writes=[(actT, (fi, tt))])
                wo = ring.get(base + 3 * gi + 2, live=1)
                wov = wo.ap[:, 0:nf * 1024].rearrange("p (f c) -> p f c", f=nf)
                for tt in range(NTT):
                    tsl = slice(tt * 512, (tt + 1) * 512)
                    for dc in range(NKC):
                        po = sc.psum()
                        for fi in range(nf):
                            sc.mm(po.ap, wov[:, fi, dc * 128:(dc + 1) * 128], actT.ap[:, fi, tsl], fi == 0, fi == nf - 1,
                                  reads=[wo, (actT, (fi, tt))], writes=[po])
                        sc.stt("dve", xT.ap[:, dc, tsl], po.ap, 0.5, xT.ap[:, dc, tsl], ALU.mult, ALU.add,
                               reads=[po, (xT, (dc, tt))], writes=[(xT, (dc, tt))])

        def load_T(dst_fn, src2d, rows, nchunk, stage):
            sc.dma("sp", stage.ap[0:rows, 0:nchunk * 128], src2d, writes=[stage], skey="ldT")
            ps = sc.psum()
            for c in range(nchunk):
                sc.transpose(ps.ap[:, c * 128:c * 128 + rows], stage.ap[0:rows, c * 128:(c + 1) * 128],
                             ident.ap[0:rows, 0:rows], reads=[stage, cst], writes=[(ps, c)])
            for c in range(nchunk):
                sc.copy("dve", dst_fn(c), ps.ap[:, c * 128:c * 128 + rows], reads=[ps], writes=[])

        def gelu_psum(p_ap, out_ap, pbuf, outbuf_w):
            n = p_ap.shape[-1] if len(p_ap.shape) == 2 else None
            t1 = tmpf()
            t2 = tmpf()
            a1 = t1.ap[:, 0:512] if n is None else t1.ap[:, 0:n]
            a2 = t2.ap[:, 0:512] if n is None else t2.ap[:, 0:n]
            sc.act(a1, p_ap, AF.Square, reads=[pbuf], writes=[t1])
            sc.ts("dve", a1, a1, 0.044715, 1.0, ALU.mult, ALU.add, reads=[t1], writes=[t1])
            sc.tt("dve", a1, a1, p_ap, ALU.mult, reads=[t1, pbuf], writes=[t1])
            sc.act(a2, a1, AF.Sigmoid, reads=[t1], writes=[t2], scale=1.5957691216057308)
            sc.tt("dve", out_ap, a2, p_ap, ALU.mult, reads=[t2, pbuf], writes=outbuf_w)

        def merge(l, i, first, base=None, plan_only=False):
            if base is None:
                base = len(ring.plan)
                wb_d = w_d["w_branch"][l][i]
                ring.add([(lambda s: s[:, 0:4096].rearrange("p (f c) -> p f c", f=4),
                           wb_d.rearrange("(f p) c -> p f c", p=128))])
                for half in range(2):
                    add_win(l, C_MERGE + i * 1024 + half * 512, 512)
                if plan_only:
                    return base
            wb = ring.get(base, live=1)
            wbv = wb.ap[:, 0:4096].rearrange("p (f c) -> p f c", f=4)
            for half in range(2):
                wl = ring.get(base + 1 + half, live=2 + half)
                wlv = wview8(wl, 512)
                for tt in range(NTT):
                    tsl = slice(tt * 512, (tt + 1) * 512)
                    for j in range(4):
                        dc = half * 4 + j
                        pl = sc.psum()
                        pb = sc.psum()
                        for k in range(NKC):
                            sc.mm(pl.ap, wlv[:, k, j * 128:(j + 1) * 128], hT.ap[:, k, tsl], k == 0, k == NKC - 1,
                                  reads=[wl, (hT, (k, tt))], writes=[pl])
                        for k in range(4):
                            sc.mm(pb.ap, wbv[:, k, dc * 128:(dc + 1) * 128], yT.ap[:, k, tsl], k == 0, k == 3,
                                  reads=[wb, (yT, (k, tt))], writes=[pb])
                        tm = tmpf()
                        sc.act(tm.ap, pl.ap, AF.Sigmoid, reads=[pl], writes=[tm])
                        if first:
                            sc.tt("dve", mT.ap[:, dc, tsl], tm.ap, pb.ap, ALU.mult,
                                  reads=[tm, pb], writes=[(mT, (dc, tt))])
                        else:
                            sc.tt("dve", tm.ap, tm.ap, pb.ap, ALU.mult, reads=[tm, pb], writes=[tm])
                            sc.tt("pool", mT.ap[:, dc, tsl], mT.ap[:, dc, tsl], tm.ap, ALU.add,
                                  reads=[tm, (mT, (dc, tt))], writes=[(mT, (dc, tt))])

        def out_proj(l, base=None, plan_only=False):
            if base is None:
                base = len(ring.plan)
                for half in range(2):
                    ring.add([(lambda s: s[:, 0:4096].rearrange("p (k c) -> p k c", k=8),
                               w_d["w_out"][l][:, half * 512:(half + 1) * 512].rearrange("(k p) c -> p k c", p=128))])
                if plan_only:
                    return base
            for half in range(2):
                wo = ring.get(base + half, live=1)
                wov = wview8(wo, 512)
                for tt in range(NTT):
                    tsl = slice(tt * 512, (tt + 1) * 512)
                    for j in range(4):
                        dc = half * 4 + j
                        po = sc.psum()
                        for k in range(NKC):
                            sc.mm(po.ap, wov[:, k, j * 128:(j + 1) * 128], mT.ap[:, k, tsl], k == 0, k == NKC - 1,
                                  reads=[wo, (mT, (k, tt))], writes=[po])
                        sc.tt("dve", xT.ap[:, dc, tsl], po.ap, xT.ap[:, dc, tsl], ALU.add,
                              reads=[po, (xT, (dc, tt))], writes=[(xT, (dc, tt))])

        def mixer_D(l, base=None, plan_only=False):
            if base is None:
                base = len(ring.plan)
                add_win(l, C_D_A, 512)
                add_win(l, C_D_B, 512)
                if plan_only:
                    return base
            u = scr("D_u", 0, [128, 30 + S], BF16)
            dg = scr("D_dg", 4352, [128, 31, 128], BF16)
            wD = scr("D_w", 12288, [128, 4, 32], F32)
            bg = scr("D_bg", 12800, [128, 8], F32)
            stage = scr("D_stage", 13056, [128, 512], F32)
            load_T(lambda c: wD.ap[:, c, 0:31], w_d["conv_dw_w"][l], 31, 4, stage)
            sc.dma("sp", bg.ap[:, 0:4], w_d["conv_dw_b"][l].rearrange("(c p) -> p c", p=128), writes=[(bg, 0)],
                   skey="c1", slow=True)
            sc.dma("sp", bg.ap[:, 4:8], w_d["conv_norm"][l].rearrange("(c p) -> p c", p=128), writes=[(bg, 1)],
                   skey="c1", slow=True)
            sc.op("dve", lambda h: h.memset(u.ap[:, 0:30], 0.0), writes=[(u, "pad")])
            ssb = sc.reserve(4)
            wa = ring.get(base, live=1)
            wb = ring.get(base + 1, live=2)
            wav, wbv = wview8(wa, 512), wview8(wb, 512)
            for c in range(4):
                sc.tt("dve", dg.ap, ident_bf.ap.unsqueeze(1).to_broadcast([128, 31, 128]),
                      wD.ap[:, c, 0:31].unsqueeze(2).to_broadcast([128, 31, 128]), ALU.mult,
                      reads=[ident_bf, wD], writes=[dg])
                for tt in range(NTT):
                    tsl = slice(tt * 512, (tt + 1) * 512)
                    pa, pb = sc.psum(), sc.psum()
                    for k in range(NKC):
                        sc.mm(pa.ap, wav[:, k, c * 128:(c + 1) * 128], hT.ap[:, k, tsl], k == 0, k == NKC - 1,
                              reads=[wa, (hT, (k, tt))], writes=[pa])
                    for k in range(NKC):
                        sc.mm(pb.ap, wbv[:, k, c * 128:(c + 1) * 128], hT.ap[:, k, tsl], k == 0, k == NKC - 1,
                              reads=[wb, (hT, (k, tt))], writes=[pb])
                    tm = tmpf()
                    sc.act(tm.ap, pb.ap, AF.Sigmoid, reads=[pb], writes=[tm])
                    sc.tt("dve", u.ap[:, 30 + tt * 512: 30 + (tt + 1) * 512], tm.ap, pa.ap, ALU.mult,
                          reads=[tm, pa], writes=[(u, tt)])
                for tt in range(NTT):
                    tsl = slice(tt * 512, (tt + 1) * 512)
                    pcv = sc.psum()
                    for k in range(31):
                        sc.mm(pcv.ap, dg.ap[:, k, :], u.ap[:, tt * 512 + k: tt * 512 + k + 512], k == 0, k == 30,
                              reads=[dg, u], writes=[pcv])
                    sq = sq_bf[tt % 2]
                    sc.act(sq.ap, pcv.ap, AF.Square, reads=[pcv, bg], writes=[sq], bias=bg.ap[:, c:c + 1])
                    sc.mm(ssb[tt].ap, ones_bf.ap, sq.ap, c == 0, c == 3, reads=[ones_bf, sq], writes=[ssb[tt]])
                    sc.act(yT.ap[:, c, tsl], pcv.ap, AF.Identity, reads=[pcv, bg], writes=[(yT, (c, tt))],
                           bias=bg.ap[:, c:c + 1])
            for tt in range(NTT):
                tsl = slice(tt * 512, (tt + 1) * 512)
                rstd_op("D", rstd.ap, ssb[tt].ap, ssb[tt], rstd, 1.0 / 512)
                for c in range(4):
                    tm = tmpf()
                    sc.stt("dve", tm.ap, yT.ap[:, c, tsl], bg.ap[:, 4 + c:5 + c], rstd.ap, ALU.mult, ALU.mult,
                           reads=[(yT, (c, tt)), bg, rstd], writes=[tm])
                    sc.act(yT.ap[:, c, tsl], tm.ap, AF.Silu, reads=[tm], writes=[(yT, (c, tt))])
            sc.release(ssb)

        def mixer_A(l, base=None, plan_only=False):
            if base is None:
                base = len(ring.plan)
                add_win(l, C_A_U, 512)
                add_win(l, C_A_V, 512)
                if plan_only:
                    return base
            stage = scr("A_stage", 0, [128, 512], F32)
            wTs = scr("A_wT", 2048, [128, 4, 128], BF16)
            gbc = scr("A_gbc", 3072, [128, 512], F32)
            brow_f = scr("A_browf", 5120, [1, 512], F32)
            brow = scr("A_brow", 7168, [1, 512], BF16)
            ug = scr("A_ug", 8192, [128, 4, 512], BF16)
            vg2 = [scr("A_vg%d" % i, 12288 + i * 2048, [128, 512], F32) for i in range(2)]
            vn2 = [scr("A_vn%d" % i, 16384 + i * 1024, [128, 512], BF16) for i in range(2)]
            ssv = scr("A_ssv", 18432, [128, 8], F32)
            for g in range(4):
                sc.dma("sp", stage.ap[:, g * 128:(g + 1) * 128], w_d["sgu_w"][l][g], writes=[(stage, g)], skey="ldA")
            sc.tt("dve", stage.ap.rearrange("p (g s) -> p g s", g=4), stage.ap.rearrange("p (g s) -> p g s", g=4),
                  cst.ap[:, 1:2, :].to_broadcast([128, 4, 128]), ALU.mult, reads=[stage, cst], writes=[stage])
            ps = sc.psum()
            for g in range(4):
                sc.transpose(ps.ap[:, g * 128:(g + 1) * 128], stage.ap[:, g * 128:(g + 1) * 128], ident.ap,
                             reads=[stage, cst], writes=[(ps, g)])
            sc.copy("dve", wTs.ap, ps.ap.rearrange("p (g s) -> p g s", g=4), reads=[ps], writes=[wTs])
            sc.dma("sp", gbc.ap, w_d["sgu_v_norm"][l].partition_broadcast(128), writes=[gbc], skey="ldA")
            sc.dma("sp", brow_f.ap, w_d["sgu_b"][l].rearrange("g t -> (g t)")[None, :], writes=[brow_f], skey="ldA")
            sc.copy("dve", brow.ap, brow_f.ap, reads=[brow_f], writes=[brow])
            wu = ring.get(base, live=1)
            wv = ring.get(base + 1, live=2)
            wuv, wvv = wview8(wu, 512), wview8(wv, 512)
            def a_u(tt, g):
                tsl = slice(tt * 512, (tt + 1) * 512)
                pu = sc.psum()
                for k in range(NKC):
                    sc.mm(pu.ap, wuv[:, k, g * 128:(g + 1) * 128], hT.ap[:, k, tsl], k == 0, k == NKC - 1,
                          reads=[wu, (hT, (k, tt))], writes=[pu])
                gelu_psum(pu.ap, ug.ap[:, g, :], pu, [(ug, g)])

            def a_vproj(tt, sub):
                t0 = tt * 512 + sub * 128
                pv = sc.psum()
                for k in range(NKC):
                    sc.mm(pv.ap, hT.ap[:, k, t0:t0 + 128], wvv[:, k, :], k == 0, k == NKC - 1,
                          reads=[wv, (hT, (k, tt))], writes=[pv])
                return pv

            def a_vrest(tt, sub, pv):
                t0 = tt * 512 + sub * 128
                par = sub % 2
                vgp, vnp = vg2[par], vn2[par]
                gelu_psum(pv.ap, vgp.ap, pv, [vgp])
                tj = tmpf()
                so = par * 4
                sc.op("act", lambda h, o=tj.ap, i=vgp.ap, a=ssv.ap[:, so:so + 1]: h.activation(o, i, AF.Square, accum_out=a),
                      reads=[vgp], writes=[tj, (ssv, so)])
                sc.act(ssv.ap[:, so + 1:so + 2], ssv.ap[:, so:so + 1], AF.Sqrt, reads=[(ssv, so)], writes=[(ssv, so + 1)],
                       bias=EPS, scale=1.0 / 512)
                sc.op("dve", lambda h, o=ssv.ap[:, so + 2:so + 3], i=ssv.ap[:, so + 1:so + 2]: h.reciprocal(o, i),
                      reads=[(ssv, so + 1)], writes=[(ssv, so + 2)])
                sc.stt("dve", vnp.ap, vgp.ap, ssv.ap[:, so + 2:so + 3], gbc.ap, ALU.mult, ALU.mult,
                       reads=[vgp, (ssv, so + 2), gbc], writes=[vnp])
                pm = sc.psum()
                for g in range(4):
                    sc.mm(pm.ap[:, g * 128:(g + 1) * 128], vnp.ap[:, g * 128:(g + 1) * 128], wTs.ap[:, g, :], True, False,
                          reads=[vnp, wTs], writes=[(pm, g)])
                    sc.mm(pm.ap[:, g * 128:(g + 1) * 128], ones_bf.ap[0:1, :], brow.ap[0:1, g * 128:(g + 1) * 128],
                          False, True, reads=[ones_bf, brow], writes=[(pm, g)])
                sc.tt("dve", yT.ap[:, :, t0:t0 + 128], ug.ap[:, :, sub * 128:(sub + 1) * 128],
                      pm.ap.rearrange("p (g t) -> p g t", g=4), ALU.mult,
                      reads=[ug, pm], writes=[(yT, (g, tt)) for g in range(4)])

            for tt in range(NTT):
                for g in range(4):
                    a_u(tt, g)
                pend = a_vproj(tt, 0)
                for sub in range(4):
                    nxt = a_vproj(tt, sub + 1) if sub + 1 < 4 else None
                    a_vrest(tt, sub, pend)
                    pend = nxt

        def mixer_B(l, base=None, plan_only=False):
            if base is None:
                base = len(ring.plan)
                add_win(l, C_B_X, 512)
                add_win(l, C_B_B, 512)
                add_win(l, C_B_Z, 512)
                add_win(l, C_B_DT, 8)
                if plan_only:
                    return base
            raw = scr("B_raw", 0, [128, 3 + S], F32)
            xsT = scr("B_xsT", 8448, [128, 4, S], BF16)
            bcT = scr("B_bcT", 24832, [128, 4, S], BF16)
            zsT = scr("B_zsT", 41216, [128, 4, S], BF16)
            o = 57600
            cw = scr("B_cw", o, [128, 8, 4], F32); o += 128
            cb = scr("B_cb", o, [128, 8], F32); o += 32
            dtT = scr("B_dt", o, [128, 16, 8], F32); o += 512
            aT = scr("B_a", o, [128, 16, 8], F32); o += 512
            dtb_bc = scr("B_dtb", o, [128, 8], F32); o += 32
            A_bc = scr("B_A", o, [128, 8], F32); o += 32
            Dp = scr("B_Dp", o, [128, 4], F32); o += 16
            gn = scr("B_gn", o, [128, 4], F32); o += 16
            sm = scr("B_sm", o, [128, 64], F32); o += 256
            xdt = scr("B_xdt", o, [128, 512], BF16); o += 1024
            xdd = scr("B_xdd", o, [128, 512], BF16); o += 1024
            btok = scr("B_btok", o, [128, 256], BF16); o += 512
            Hs = scr("B_H", o, [128, 512], F32); o += 2048
            Hbf = scr("B_Hbf", o, [128, 512], BF16); o += 1024
            assert o <= 65536
            ST = scr("B_ST", 0, [128, 8, 128], BF16)
            CTs = scr("B_CTs", 2048, [128, 8, 128], BF16)
            yz = scr("B_yz", 4096, [128, 512], F32)
            sqb = scr("B_sqb", 6144, [128, 512], BF16)
            rsb = scr("B_rs", 7168, [128, 128], F32)
            rsb2 = [rsb, scr("B_rs1", 7680, [128, 128], F32)]
            stage = iobuf[0]
            atri = Buf(iobuf[0].ap.rearrange("p (h l) -> p h l", h=8), "B_atri")
            argm = Buf(iobuf[1].ap.rearrange("p (h l) -> p h l", h=8), "B_argm")

            for hf in range(2):
                load_T(lambda c, hf=hf: cw.ap[:, hf * 4 + c, 0:4], w_d["ssm_conv_w"][l][:, hf * 512:(hf + 1) * 512], 4, 4,
                       stage)
            sc.dma("sp", cb.ap, w_d["ssm_conv_b"][l].rearrange("(c p) -> p c", p=128), writes=[cb], skey="c1", slow=True)
            sc.dma("sp", gn.ap, w_d["ssm_norm"][l].rearrange("(c p) -> p c", p=128), writes=[gn], skey="c1", slow=True)
            sc.dma("sp", dtb_bc.ap, w_d["ssm_dt_bias"][l].partition_broadcast(128), writes=[dtb_bc], skey="c1")
            sc.dma("sp", A_bc.ap, w_d["ssm_a_log"][l].partition_broadcast(128), writes=[A_bc], skey="c1")
            for c in range(4):
                for hf in range(2):
                    sc.dma("sp", Dp.ap[hf * 64:(hf + 1) * 64, c:c + 1],
                           w_d["ssm_d"][l][2 * c + hf:2 * c + hf + 1].partition_broadcast(64), writes=[(Dp, (c, hf))],
                           skey="c1")
            sc.act(A_bc.ap, A_bc.ap, AF.Exp, reads=[A_bc], writes=[A_bc])
            sc.ts("dve", A_bc.ap, A_bc.ap, -1.0, None, ALU.mult, ALU.bypass, reads=[A_bc], writes=[A_bc])
            sc.op("dve", lambda h: h.memset(raw.ap[:, 0:3], 0.0), writes=[(raw, "pad")])

            for blk in range(2):
                wx = ring.get(base + blk, live=1)
                wxv = wview8(wx, 512)
                for cc in range(4):
                    c = blk * 4 + cc
                    dst = xsT if blk == 0 else bcT
                    for tt in range(NTT):
                        tsl = slice(tt * 512, (tt + 1) * 512)
                        px = sc.psum()
                        for k in range(NKC):
                            sc.mm(px.ap, wxv[:, k, cc * 128:(cc + 1) * 128], hT.ap[:, k, tsl], k == 0, k == NKC - 1,
                                  reads=[wx, (hT, (k, tt))], writes=[px])
                        sc.op("act", lambda h, o=raw.ap[:, 3 + tt * 512:3 + (tt + 1) * 512], i=px.ap: h.copy(o, i),
                              reads=[px], writes=[(raw, tt)])
                        tm = tmpf()
                        sc.ts("dve", tm.ap, raw.ap[:, tt * 512:tt * 512 + 512], cw.ap[:, c, 0:1], cb.ap[:, c:c + 1],
                              ALU.mult, ALU.add, reads=[raw, cw, cb], writes=[tm])
                        for k in range(1, 4):
                            sc.stt("dve", tm.ap, raw.ap[:, tt * 512 + k:tt * 512 + k + 512], cw.ap[:, c, k:k + 1], tm.ap,
                                   ALU.mult, ALU.add, reads=[raw, cw, tm], writes=[tm])
                        sc.act(dst.ap[:, cc, tsl], tm.ap, AF.Silu, reads=[tm], writes=[(dst, (cc, tt))])
            wz = ring.get(base + 2, live=1)
            wzv = wview8(wz, 512)
            for c in range(4):
                for tt in range(NTT):
                    tsl = slice(tt * 512, (tt + 1) * 512)
                    pz = sc.psum()
                    for k in range(NKC):
                        sc.mm(pz.ap, wzv[:, k, c * 128:(c + 1) * 128], hT.ap[:, k, tsl], k == 0, k == NKC - 1,
                              reads=[wz, (hT, (k, tt))], writes=[pz])
                    sc.act(zsT.ap[:, c, tsl], pz.ap, AF.Silu, reads=[pz], writes=[(zsT, (c, tt))])
            wdt = ring.get(base + 3, live=1)
            wdtv = wview8(wdt, 8)
            pdt = sc.psum()
            for ci in range(16):
                for k in range(NKC):
                    sc.mm(pdt.ap[:, ci * 8:(ci + 1) * 8], hT.ap[:, k, ci * 128:(ci + 1) * 128], wdtv[:, k, :], k == 0,
                          k == NKC - 1, reads=[wdt, (hT, (k, ci // 4))], writes=[(pdt, ci)])
            sc.tt("dve", dtT.ap, pdt.ap[:, 0:128].rearrange("p (c h) -> p c h", c=16),
                  dtb_bc.ap.unsqueeze(1).to_broadcast([128, 16, 8]), ALU.add, reads=[pdt, dtb_bc], writes=[dtT])
            sc.act(dtT.ap, dtT.ap, AF.Exp, reads=[dtT], writes=[dtT])
            sc.act(dtT.ap, dtT.ap, AF.Ln, reads=[dtT], writes=[dtT], bias=1.0)
            sc.tt("dve", aT.ap, dtT.ap, A_bc.ap.unsqueeze(1).to_broadcast([128, 16, 8]), ALU.mult,
                  reads=[dtT, A_bc], writes=[aT])

            triU = cst.ap[:, 3, :]
            maskT = cst.ap[:, 2, :]
            ones_f = cst.ap[:, 4, :]
            ST2 = [(ST, ST.ap), (tmp_a[0], tmp_a[0].ap.bitcast(BF16).rearrange("p (h l) -> p h l", h=8))]
            CT2 = [(CTs, CTs.ap), (tmp_a[1], tmp_a[1].ap.bitcast(BF16).rearrange("p (h l) -> p h l", h=8))]
            XDT2 = [(xdt, xdt.ap), (sq_bf[0], sq_bf[0].ap)]
            XDD2 = [(xdd, xdd.ap), (sq_bf[1], sq_bf[1].ap)]
            BTK2 = [(btok, btok.ap), (tmp_a[2], tmp_a[2].ap.bitcast(BF16)[:, 0:256])]

            def front(ci):
                p = ci % 2
                so = p * 32
                tok = slice(ci * 128, (ci + 1) * 128)
                tt = ci // 4
                STb, STa = ST2[p]
                CTb, CTa = CT2[p]
                XTb, XTa = XDT2[p]
                XDb, XDa = XDD2[p]
                BKb, BKa = BTK2[p]
                a_c = aT.ap[:, ci, :]
                pc = sc.psum()
                sc.mm(pc.ap[:, 0:8], triU, a_c, True, True, reads=[cst, aT], writes=[pc])
                negacs = sm.ap[:, so:so + 8]
                sc.ts("dve", negacs, pc.ap[:, 0:8], -1.0, None, ALU.mult, ALU.bypass, reads=[pc],
                      writes=[(sm, ("neg", p))])
                sc.tt("dve", atri.ap, triU.unsqueeze(1).to_broadcast([128, 8, 128]),
                      a_c.unsqueeze(2).to_broadcast([128, 8, 128]), ALU.mult, reads=[cst, aT], writes=[atri])
                pr = [sc.psum(), sc.psum()]
                for hb in range(2):
                    sc.mm(pr[hb].ap, ones_f, atri.ap[:, hb * 4:(hb + 1) * 4, :], True, True,
                          reads=[cst, atri], writes=[pr[hb]])
                for hb in range(2):
                    sc.tt("dve", argm.ap[:, hb * 4:(hb + 1) * 4, :], pr[hb].ap.rearrange("p (h l) -> p h l", h=4),
                          maskT.unsqueeze(1).to_broadcast([128, 4, 128]), ALU.add, reads=[pr[hb], cst],
                          writes=[(argm, hb)])
                for h in range(8):
                    sc.act(argm.ap[:, h, :], argm.ap[:, h, :], AF.Exp, reads=[(argm, h // 4), (sm, ("neg", p))],
                           writes=[(argm, h // 4)], bias=sm.ap[:, so + h:so + h + 1])
                pg = sc.psum()
                for g in range(2):
                    sc.mm(pg.ap[:, g * 128:(g + 1) * 128], bcT.ap[:, g, tok], bcT.ap[:, 2 + g, tok], True, True,
                          reads=[(bcT, (g, tt)), (bcT, (2 + g, tt))], writes=[(pg, g)])
                for g in range(2):
                    sc.tt("dve", STa[:, g * 4:(g + 1) * 4, :], argm.ap[:, g * 4:(g + 1) * 4, :],
                          pg.ap[:, g * 128:(g + 1) * 128].unsqueeze(1).to_broadcast([128, 4, 128]), ALU.mult,
                          reads=[(argm, g), pg], writes=[(STb, g)])
                for hb in range(2):
                    sc.act(atri.ap[:, hb * 4:(hb + 1) * 4, :], pr[hb].ap.rearrange("p (h l) -> p h l", h=4), AF.Exp,
                           reads=[pr[hb]], writes=[atri])
                for g in range(2):
                    sc.tt("dve", CTa[:, g * 4:(g + 1) * 4, :], atri.ap[:, g * 4:(g + 1) * 4, :],
                          bcT.ap[:, 2 + g, tok].unsqueeze(1).to_broadcast([128, 4, 128]), ALU.mult,
                          reads=[atri, (bcT, (2 + g, tt))], writes=[(CTb, g)])
                ptb = sc.psum()
                ptv = ptb.ap.bitcast(BF16)
                for c in range(4):
                    sc.transpose(ptv[:, c * 128:(c + 1) * 128], xsT.ap[:, c, tok], ident_bf.ap,
                                 reads=[(xsT, (c, tt)), ident_bf], writes=[(ptb, c)])
                for g in range(2):
                    sc.transpose(ptv[:, 512 + g * 128:512 + (g + 1) * 128], bcT.ap[:, g, tok], ident_bf.ap,
                                 reads=[(bcT, (g, tt)), ident_bf], writes=[(ptb, 4 + g)])
                dec = sm.ap[:, so + 8:so + 16]
                for hb in range(2):
                    last = pr[hb].ap.rearrange("p (h l) -> p h l", h=4)[:, :, 127]
                    sc.tt("dve", sm.ap[:, so + 8 + hb * 4:so + 12 + hb * 4], last, sm.ap[:, so + hb * 4:so + hb * 4 + 4],
                          ALU.add, reads=[pr[hb], (sm, ("neg", p))], writes=[(sm, ("dec", p))])
                    sc.act(sm.ap[:, so + 16 + hb * 4:so + 20 + hb * 4], last, AF.Exp, reads=[pr[hb]],
                           writes=[(sm, ("eA", p))])
                sc.act(dec, dec, AF.Exp, reads=[(sm, ("dec", p))], writes=[(sm, ("dec", p))])
                sc.tt("dve", sm.ap[:, so + 24:so + 32], dec, dtT.ap[:, ci, :], ALU.mult, reads=[(sm, ("dec", p)), dtT],
                      writes=[(sm, ("dtdec", p))])
                xtok = ptv[:, 0:512].rearrange("p (h e) -> p h e", h=8)
                sc.tt("dve", XTa.rearrange("p (h e) -> p h e", h=8), xtok,
                      dtT.ap[:, ci, :].unsqueeze(2).to_broadcast([128, 8, 64]), ALU.mult, reads=[ptb, dtT], writes=[XTb])
                sc.tt("dve", XDa.rearrange("p (h e) -> p h e", h=8), xtok,
                      sm.ap[:, so + 24:so + 32].unsqueeze(2).to_broadcast([128, 8, 64]), ALU.mult,
                      reads=[ptb, (sm, ("dtdec", p))], writes=[XDb])
                sc.op("act", lambda h, o_=BKa, i_=ptv[:, 512:768]: h.copy(o_, i_), reads=[ptb], writes=[BKb])

            def back(ci):
                p = ci % 2
                so = p * 32
                tok = slice(ci * 128, (ci + 1) * 128)
                tt = ci // 4
                STb, STa = ST2[p]
                CTb, CTa = CT2[p]
                XTb, XTa = XDT2[p]
                XDb, XDa = XDD2[p]
                BKb, BKa = BTK2[p]
                eA = sm.ap[:, so + 16:so + 24]
                py = sc.psum()
                for h in range(8):
                    outp = py.ap[(h % 2) * 64:(h % 2) * 64 + 64, (h // 2) * 128:(h // 2) * 128 + 128]
                    sc.mm(outp, XTa[:, h * 64:(h + 1) * 64], STa[:, h, :], True, ci == 0,
                          reads=[XTb, (STb, h // 4)], writes=[(py, h)])
                    if ci > 0:
                        sc.mm(outp, Hbf.ap[:, h * 64:(h + 1) * 64], CTa[:, h, :], False, True,
                              reads=[Hbf, (CTb, h // 4)], writes=[(py, h)])
                pS = sc.psum()
                for g in range(2):
                    sc.mm(pS.ap[:, g * 256:(g + 1) * 256], BKa[:, g * 128:(g + 1) * 128], XDa[:, g * 256:(g + 1) * 256],
                          True, True, reads=[BKb, XDb], writes=[(pS, g)])
                if ci == 0:
                    sc.copy("dve", Hs.ap, pS.ap, reads=[pS], writes=[Hs])
                else:
                    sc.tt("dve", Hs.ap.rearrange("p (h e) -> p h e", h=8), Hs.ap.rearrange("p (h e) -> p h e", h=8),
                          eA.unsqueeze(2).to_broadcast([128, 8, 64]), ALU.mult, reads=[Hs, (sm, ("eA", p))], writes=[Hs])
                    sc.tt("dve", Hs.ap, Hs.ap, pS.ap, ALU.add, reads=[Hs, pS], writes=[Hs])
                if ci < 15:
                    sc.copy("pool", Hbf.ap, Hs.ap, reads=[Hs], writes=[Hbf])
                for c in range(4):
                    sc.stt("dve", yz.ap[:, c * 128:(c + 1) * 128], xsT.ap[:, c, tok], Dp.ap[:, c:c + 1],
                           py.ap[:, c * 128:(c + 1) * 128], ALU.mult, ALU.add,
                           reads=[(xsT, (c, tt)), Dp, py], writes=[(yz, c)])
                sc.tt("dve", yz.ap.rearrange("p (c t) -> p c t", c=4), yz.ap.rearrange("p (c t) -> p c t", c=4),
                      zsT.ap[:, :, tok], ALU.mult, reads=[yz] + [(zsT, (c, tt)) for c in range(4)], writes=[yz])
                sc.act(sqb.ap, yz.ap, AF.Square, reads=[yz], writes=[sqb])
                sc.copy("pool", yT.ap[:, :, tok], yz.ap.rearrange("p (c t) -> p c t", c=4), reads=[yz],
                        writes=[(yT, (c, tt)) for c in range(4)])
                pss = sc.psum()
                for c in range(4):
                    sc.mm(pss.ap[:, 0:128], ones_bf.ap, sqb.ap[:, c * 128:(c + 1) * 128], c == 0, c == 3,
                          reads=[ones_bf, sqb], writes=[pss])
                rs = rsb2[p]
                sc.act(rs.ap, pss.ap[:, 0:128], AF.Sqrt, reads=[pss], writes=[rs], bias=EPS, scale=1.0 / 512)

            def back_b(ci):
                p = ci % 2
                tok = slice(ci * 128, (ci + 1) * 128)
                tt = ci // 4
                rs = rsb2[p]
                sc.op("dve", lambda h, a=rs.ap: h.reciprocal(a, a), reads=[rs], writes=[rs])
                for c in range(4):
                    sc.stt("dve", yT.ap[:, c, tok], yT.ap[:, c, tok], gn.ap[:, c:c + 1], rs.ap,
                           ALU.mult, ALU.mult, reads=[(yT, (c, tt)), gn, rs], writes=[(yT, (c, tt))])

            front(0)
            for ci in range(16):
                if ci + 1 < 16:
                    front(ci + 1)
                back(ci)
                if ci > 0:
                    back_b(ci - 1)
            back_b(15)

        def mixer_C(l, base=None, plan_only=False):
            if base is None:
                base = len(ring.plan)
                add_win(l, C_C_KC, 512)
                add_win(l, C_C_KW, 280)
                for kv in range(2):
                    ring.add([(lambda s_: s_[0:64, 0:4096].rearrange("p (j h) -> p j h", j=32),
                               w_d["nsa_w1_" + "kv"[kv]][l].rearrange("(j e) h -> e j h", e=64))])
                add_win(l, C_C_Q, 512)
                if plan_only:
                    return base
            Qa = [scr("C_Qa%d" % g, g * 16384, [128, 8192], BF16) for g in range(2)]
            Ks = [scr("C_Ks%d" % g, 32768 + g * 4096, [128, 2048], BF16) for g in range(2)]
            Kw = [scr("C_Kw%d" % g, 40960 + g * 4096, [128, 2048], BF16) for g in range(2)]
            Vs = scr("C_Vs", 49152, [128, 16 * 2 * 66], BF16)
            Vw = scr("C_Vw", 53376, [128, 16 * 2 * 66], BF16)
            Kc = scr("C_Kc", 57600, [128, 2 * 128], BF16)
            Vc = scr("C_Vc", 58112, [128, 2 * 98], BF16)
            gate = scr("C_gate", 58624, [128, 16 * 24], F32)
            PT = [scr("C_PT%d" % i, 60160 + i * 1024, [128, 512], BF16) for i in range(2)]
            o = 62208
            gqk = scr("C_gqk", o, [128, 2], F32); o += 8
            cbias = scr("C_cb", o, [128, 2], F32); o += 8
            w2 = scr("C_w2", o, [128, 2 * 64], BF16); o += 256
            peT = scr("C_peT", o, [64, 64], F32); o += 256
            peTb = scr("C_peTb", o, [64, 64], BF16); o += 128
            mcb = scr("C_mc", o, [128, 512], BF16); o += 1024
            mfb = scr("C_mf", o, [128, 512], BF16); o += 1024
            assert o <= 65536, o
            Vs4 = Vs.ap.rearrange("p (k g e) -> p k g e", k=16, g=2)
            Vw4 = Vw.ap.rearrange("p (k g e) -> p k g e", k=16, g=2)
            Vc3 = Vc.ap.rearrange("p (g e) -> p g e", g=2)
            Kc3 = Kc.ap.rearrange("p (g c) -> p g c", g=2)
            gate3 = gate.ap.rearrange("p (j c) -> p j c", j=16)
            kcvc = scr("C_kcvc", 0, [64, 4 * 2048], BF16)
            kcvc3 = kcvc.ap.rearrange("p (a t) -> p a t", a=4)
            yq = iobuf[0]
            sel = iobuf[1]
            Dcq = cst.ap[:, 5, :]

            sc.dma("sp", gqk.ap[0:64, 0:1], w_d["nsa_q_norm"][l].rearrange("(e o) -> e o", o=1), writes=[(gqk, 0)],
                   skey="ldC")
            sc.dma("sp", gqk.ap[0:64, 1:2], w_d["nsa_k_norm"][l].rearrange("(e o) -> e o", o=1), writes=[(gqk, 1)],
                   skey="ldC")
            sc.ts("dve", gqk.ap[0:64, 0:1], gqk.ap[0:64, 0:1], 0.125, None, ALU.mult, ALU.bypass,
                  reads=[(gqk, 0)], writes=[(gqk, 0)])
            for kv, nm in enumerate(("nsa_pe_k", "nsa_pe_v")):
                sc.dma("sp", peT.ap[:, kv * 32:(kv + 1) * 32], w_d[nm][l].rearrange("j e -> e j"), writes=[(peT, kv)],
                       skey="ldC", slow=True)
                sc.dma("pool", w2.ap[:, kv * 64:(kv + 1) * 64], w_d["nsa_w2_" + "kv"[kv]][l], writes=[(w2, kv)], skey="ldC2")
            sc.copy("dve", peTb.ap, peT.ap, reads=[peT], writes=[peTb])
            sc.dma("pool", mcb.ap, nsa_c["nc_mc"], writes=[mcb], skey="ldC2")
            sc.dma("pool", mfb.ap, nsa_c["nc_mf"], writes=[mfb], skey="ldC2")
            for g in range(2):
                sc.dma("pool", Kw[g].ap[64:68, :], nsa_c["nc_kal"], writes=[(Kw[g], "al")], skey="ldC2")
                sc.dma("pool", Ks[g].ap[96:128, :], nsa_c["nc_onehot"], writes=[(Ks[g], "oh")], skey="ldC2")
                sc.op("dve", lambda h, a=Ks[g].ap[64:96, :]: h.memset(a, 0.0), writes=[(Ks[g], "z")])
                sc.dma("pool", Ks[g].ap[64:68, :], nsa_c["nc_kal"], reads=[(Ks[g], "z")], writes=[(Ks[g], "al")], skey="ldC2")
                sc.dma("pool", Kc3[64:68, g, :], nsa_c["nc_kalc"], writes=[(Kc, ("al", g))], skey="ldC2")
                sc.dma("pool", Vc3[:, g, 65:97], nsa_c["nc_ov"], writes=[(Vc, ("ov", g))], skey="ldC2")
                sc.op("dve", lambda h, a=Vc3[:, g, 64:65]: h.memset(a, 1.0), writes=[(Vc, ("one", g))])
            sc.op("dve", lambda h, a=Vs4[:, :, :, 64:65]: h.memset(a, 1.0), writes=[(Vs, "one")])
            sc.op("dve", lambda h, a=Vw4[:, :, :, 64:65]: h.memset(a, 1.0), writes=[(Vw, "one")])


            n64 = [0]

            def norm64(p_ap, pbuf, gcol, out_ap, out_w, n):
                n64[0] += 1
                sq = sq_bf[n64[0] % 2]
                sc.act(sq.ap[0:64, 0:n], p_ap, AF.Square, reads=[pbuf], writes=[sq])
                pss = sc.psum()
                sc.mm(pss.ap[0:64, 0:n], ones_bf.ap[0:64, 0:64], sq.ap[0:64, 0:n], True, True, reads=[ones_bf, sq],
                      writes=[pss])
                rs = tmpf()
                rstd_op("Ck", rs.ap[0:64, 0:n], pss.ap[0:64, 0:n], pss, rs, 1.0 / 64)
                sc.stt("dve", out_ap, p_ap, gqk.ap[0:64, gcol:gcol + 1], rs.ap[0:64, 0:n], ALU.mult, ALU.mult,
                       reads=[pbuf, gqk, rs], writes=out_w)

            w1 = ring.get(base, live=1)
            w1v = wview8(w1, 512)
            for a in range(4):
                for tt in range(NTT):
                    tsl = slice(tt * 512, (tt + 1) * 512)
                    pk = sc.psum()
                    for k in range(NKC):
                        sc.mm(pk.ap[0:64, :], w1v[:, k, a * 64:(a + 1) * 64], hT.ap[:, k, tsl], k == 0, k == NKC - 1,
                              reads=[w1, (hT, (k, tt))], writes=[pk])
                    sc.op("act", lambda h, o_=kcvc3[:, a, tsl], i_=pk.ap[0:64, :]: h.copy(o_, i_), reads=[pk],
                          writes=[(kcvc, (a, tt))])
            def pipe(tasks):
                pend = tasks[0][0]()
                for i, (_, cons) in enumerate(tasks):
                    nxt = tasks[i + 1][0]() if i + 1 < len(tasks) else None
                    cons(pend)
                    pend = nxt

            def kproj(wslot, wv, col0, tt):
                def f():
                    tsl = slice(tt * 512, (tt + 1) * 512)
                    pk = sc.psum()
                    for k in range(NKC):
                        sc.mm(pk.ap[0:64, :], wv[:, k, col0:col0 + 64], hT.ap[:, k, tsl], k == 0, k == NKC - 1,
                              reads=[wslot, (hT, (k, tt))], writes=[pk])
                    return pk
                return f

            tasks = []
            for g in range(2):
                for tt in range(NTT):
                    tsl = slice(tt * 512, (tt + 1) * 512)
                    tasks.append((kproj(w1, w1v, 256 + g * 64, tt),
                                  lambda pk, g=g, tt=tt, tsl=tsl: norm64(pk.ap[0:64, :], pk, 1, Ks[g].ap[0:64, tsl],
                                                                          [(Ks[g], ("k", tt))], 512)))
            pipe(tasks)
            for kb in range(16):
                pv = sc.psum()
                for k in range(NKC):
                    sc.mm(pv.ap[:, 0:128], hT.ap[:, k, kb * 128:(kb + 1) * 128], w1v[:, k, 384:512], k == 0, k == NKC - 1,
                          reads=[w1, (hT, (k, kb // 4))], writes=[pv])
                sc.copy("dve", Vs4[:, kb, :, 0:64], pv.ap[:, 0:128].rearrange("p (g e) -> p g e", g=2), reads=[pv],
                        writes=[(Vs, ("v", kb))])
            w2s = ring.get(base + 1, live=1)
            w2v = wview8(w2s, 280)
            tasks = []
            for g in range(2):
                for tt in range(NTT):
                    tsl = slice(tt * 512, (tt + 1) * 512)
                    tasks.append((kproj(w2s, w2v, g * 64, tt),
                                  lambda pk, g=g, tt=tt, tsl=tsl: norm64(pk.ap[0:64, :], pk, 1, Kw[g].ap[0:64, tsl],
                                                                          [(Kw[g], ("k", tt))], 512)))
            pipe(tasks)
            for kb in range(16):
                pv = sc.psum()
                for k in range(NKC):
                    sc.mm(pv.ap[:, 0:152], hT.ap[:, k, kb * 128:(kb + 1) * 128], w2v[:, k, 128:280], k == 0, k == NKC - 1,
                          reads=[w2s, (hT, (k, kb // 4))], writes=[pv])
                sc.copy("dve", Vw4[:, kb, :, 0:64], pv.ap[:, 0:128].rearrange("p (g e) -> p g e", g=2), reads=[pv],
                        writes=[(Vw, ("v", kb))])
                sc.act(gate3[:, kb, :], pv.ap[:, 128:152], AF.Sigmoid, reads=[pv], writes=[(gate, kb)])
            for kv in range(2):
                ww = ring.get(base + 2 + kv, live=1)
                wwv = ww.ap[0:64, 0:4096].rearrange("p (j h) -> p j h", j=32)
                pb_ = sc.psum()
                for j in range(32):
                    sc.mm(pb_.ap[:, 0:1], wwv[:, j, :], peTb.ap[:, kv * 32 + j:kv * 32 + j + 1], j == 0, j == 31,
                          reads=[ww, peTb], writes=[pb_])
                sc.copy("dve", cbias.ap[:, kv:kv + 1], pb_.ap[:, 0:1], reads=[pb_], writes=[(cbias, kv)])
                for g in range(2):
                    ph = sc.psum()
                    for j in range(32):
                        sc.mm(ph.ap[:, 0:127], wwv[:, j, :], kcvc3[:, kv * 2 + g, j:j + 2017:16], j == 0, j == 31,
                              reads=[ww, kcvc], writes=[ph])
                    hb_ = tmpf()
                    sc.ts("dve", hb_.ap[:, 0:127], ph.ap[:, 0:127], cbias.ap[:, kv:kv + 1], None, ALU.add, ALU.bypass,
                          reads=[ph, (cbias, kv)], writes=[hb_])
                    hid = sq_bf[1]
                    gelu_psum(hb_.ap[:, 0:127], hid.ap[:, 0:127], hb_, [hid])
                    if kv == 0:
                        pc_ = sc.psum()
                        sc.mm(pc_.ap[0:64, 0:127], w2.ap[:, 0:64], hid.ap[:, 0:127], True, True, reads=[(w2, 0), hid],
                              writes=[pc_])
                        norm64(pc_.ap[0:64, 0:127], pc_, 1, Kc3[0:64, g, 0:127], [(Kc, ("k", g))], 127)
                    else:
                        pc_ = sc.psum()
                        sc.mm(pc_.ap[0:127, 0:64], hid.ap[:, 0:127], w2.ap[:, 64:128], True, True, reads=[(w2, 1), hid],
                              writes=[pc_])
                        sc.copy("dve", Vc3[0:127, g, 0:64], pc_.ap[0:127, 0:64], reads=[pc_], writes=[(Vc, ("v", g))])
            sc.barrier()
            for g in range(2):
                sc.op("dve", lambda h, a=Qa[g].ap[64:128, :]: h.memset(a, 0.0), writes=[(Qa[g], "z")])
                sc.dma("pool", Qa[g].ap[64:68, :], nsa_c["nc_qal"][:, g, :], reads=[(Qa[g], "z")], writes=[(Qa[g], "al")],
                       skey="ldC2")
            wq = ring.get(base + 4, live=1)
            wqv = wview8(wq, 512)
            def qnorm(pq, hh, tt):
                g, r = hh // 4, hh % 4
                Qv = Qa[g].ap[0:64, :].rearrange("p (j r q) -> p j r q", j=16, r=4)
                sq = sq_bf[(hh * 4 + tt) % 2]
                sc.act(sq.ap[0:64, :], pq.ap[0:64, :], AF.Square, reads=[pq], writes=[sq])
                pss = sc.psum()
                sc.mm(pss.ap[0:64, :], ones_bf.ap[0:64, 0:64], sq.ap[0:64, :], True, True, reads=[ones_bf, sq],
                      writes=[pss])
                rs = tmpf()
                rstd_op("Cq", rs.ap[0:64, :], pss.ap[0:64, :], pss, rs, 1.0 / 64)
                sc.stt("dve", Qv[:, 4 * tt:4 * tt + 4, r, :], pq.ap[0:64, :].rearrange("p (j q) -> p j q", j=4),
                       gqk.ap[0:64, 0:1], rs.ap[0:64, :].rearrange("p (j q) -> p j q", j=4), ALU.mult, ALU.mult,
                       reads=[pq, gqk, rs], writes=[(Qa[g], ("q", hh, tt))])

            tasks = []
            for hh in range(8):
                for tt in range(NTT):
                    tasks.append((kproj(wq, wqv, hh * 64, tt), lambda pq, hh=hh, tt=tt: qnorm(pq, hh, tt)))
            pipe(tasks)

            if cfg.get("c_stop", 9) < 5:
                return
            sc.barrier()
            sc.mark("L%d C attn" % l)
            NB = cfg.get("c_nj", 16)
            acc_c = sc.reserve(2)
            acc_s = sc.reserve(1)[0]
            acc_w = sc.reserve(1)[0]
            order = [(j, g) for j in range(NB) for g in range(2)]
            pt_rr = [0]
            selp_v = selp_sb.ap.rearrange("p (a n) -> p a n", a=2)

            def cmpsel(idx):
                j, g = order[idx]
                par = idx % 2
                pc_ = acc_c[par]
                so = par * 16
                ps = sc.psum()
                sc.mm(ps.ap[0:127, :], Kc3[0:68, g, 0:127], Qa[g].ap[0:68, j * 512:(j + 1) * 512], True, True,
                      reads=[(Qa[g], ("blk", j))], writes=[ps])
                mk = tmpf()
                sc.ts("dve", mk.ap[0:127, 0:128], Dcq[0:127, :], float(128 * j - 31), NEGM, ALU.is_gt, ALU.mult,
                      reads=[cst], writes=[mk])
                sc.tt("dve", sel.ap[0:127, 0:512].rearrange("p (r q) -> p r q", r=4),
                      ps.ap[0:127, :].rearrange("p (r q) -> p r q", r=4),
                      mk.ap[0:127, 0:128].unsqueeze(1).to_broadcast([127, 4, 128]), ALU.add, reads=[ps, mk],
                      writes=[(sel, "s")])
                pt_rr[0] += 1
                pt = PT[pt_rr[0] % 2]
                sc.act(pt.ap[0:127, :], sel.ap[0:127, 0:512], AF.Exp, reads=[(sel, "s")], writes=[pt])
                for r in range(4):
                    sc.mm(pc_.ap[:, r * 128:r * 128 + 97], pt.ap[0:127, r * 128:(r + 1) * 128], Vc3[0:127, g, 0:97],
                          True, True, reads=[pt], writes=[(pc_, r)])
                pc4 = pc_.ap.rearrange("p (r c) -> p r c", r=4)
                rc = small.ap[:, so:so + 4]
                sc.ts("dve", rc, pc4[:, :, 64], 1e-30, None, ALU.max, ALU.bypass, reads=[pc_], writes=[(small, ("rc", par))])
                sc.op("dve", lambda h, a=rc: h.reciprocal(a, a), reads=[(small, ("rc", par))], writes=[(small, ("rc", par))])
                imp = sel.ap[:, 512:544]
                sc.ts("dve", imp, pc4[:, 0, 65:97], small.ap[:, so:so + 1], None, ALU.mult, ALU.bypass,
                      reads=[pc_, (small, ("rc", par))], writes=[(sel, "imp")])
                for r in range(1, 4):
                    sc.stt("dve", imp, pc4[:, r, 65:97], small.ap[:, so + r:so + r + 1], imp, ALU.mult, ALU.add,
                           reads=[pc_, (small, ("rc", par)), (sel, "imp")], writes=[(sel, "imp")])
                sc.tt("dve", small.ap[:, so + 4:so + 8], rc, gate3[:, j, g * 12:g * 12 + 12:3], ALU.mult,
                      reads=[(small, ("rc", par)), (gate, j)], writes=[(small, ("cfc", par))])
                sc.tt("dve", imp, imp, selp_v[:, 0, 31 - 2 * j:63 - 2 * j], ALU.mult, reads=[(sel, "imp"), selp_sb],
                      writes=[(sel, "imp")])
                sc.tt("dve", imp, imp, selp_v[:, 1, 31 - 2 * j:63 - 2 * j], ALU.add, reads=[(sel, "imp"), selp_sb],
                      writes=[(sel, "imp")])
                sc.op("dve", lambda h, a=sel.ap[:, 512:513]: h.memset(a, 1e9), reads=[(sel, "imp")], writes=[(sel, "imp")])
                top8 = sel.ap[:, 544:552]
                sc.op("dve", lambda h, o_=top8, i_=imp: h.max(o_, i_), reads=[(sel, "imp")], writes=[(sel, "top")])
                sneg = sel.ap[:, 640:768]
                sc.op("dve", lambda h, a=sel.ap[:, 640:736]: h.memset(a, 0.0), writes=[(sel, "sneg")])
                sc.ts("dve", sel.ap[:, 736:768], imp, sel.ap[:, 551:552], -32768.0, ALU.is_lt, ALU.mult,
                      reads=[(sel, "imp"), (sel, "top")], writes=[(sel, "sneg")])

            def cmpsel_b(idx):
                j, g = order[idx]
                sneg = sel.ap[:, 640:768]
                ptr = sc.psum()
                sc.transpose(ptr.ap[:, 0:128], sneg, ident.ap, reads=[(sel, "sneg"), cst], writes=[ptr])
                Qv = Qa[g].ap[96:128, j * 512:(j + 1) * 512].rearrange("p (r q) -> p r q", r=4)
                sc.copy("dve", Qv, ptr.ap[96:128, 0:128].unsqueeze(1).to_broadcast([32, 4, 128]), reads=[ptr],
                        writes=[(Qa[g], ("blk", j))])

            def attends(idx):
                j, g = order[idx]
                items = []
                for kb in range(j + 1):
                    items.append((Ks[g].ap[:, kb * 128:(kb + 1) * 128], 128, mcb if kb == j else None, acc_s,
                                  Vs4[:, kb, g, 0:65], kb == 0))
                kbs = [kb for kb in (j - 2, j - 1, j) if kb >= 0]
                for kb in kbs:
                    mb = mcb if kb == j else (mfb if kb == j - 2 else None)
                    items.append((Kw[g].ap[0:68, kb * 128:(kb + 1) * 128], 68, mb, acc_w, Vw4[:, kb, g, 0:65],
                                  kb == kbs[0]))

                def qk(it):
                    Kt, krows, maskb, po, Vap, first = it
                    ps = sc.psum()
                    sc.mm(ps.ap, Kt, Qa[g].ap[0:krows, j * 512:(j + 1) * 512], True, maskb is None,
                          reads=[(Qa[g], ("blk", j))], writes=[ps])
                    if maskb is not None:
                        sc.mm(ps.ap, ident_bf.ap, maskb.ap, False, True, reads=[], writes=[ps])
                    return ps

                def pv(it, ps):
                    Kt, krows, maskb, po, Vap, first = it
                    pt_rr[0] += 1
                    pt = PT[pt_rr[0] % 2]
                    sc.act(pt.ap, ps.ap, AF.Exp, reads=[ps], writes=[pt])
                    for r in range(4):
                        sc.op("pe", lambda h, o_=po.ap[:, r * 128:r * 128 + 65], a_=pt.ap[:, r * 128:(r + 1) * 128],
                              b_=Vap, st=(first and r == 0): h.matmul(o_, a_, b_, start=st, stop=True,
                                                                       skip_group_check=True),
                              reads=[pt], writes=[po])

                pend = qk(items[0])
                for i, it in enumerate(items):
                    nxt = qk(items[i + 1]) if i + 1 < len(items) else None
                    pv(it, pend)
                    pend = nxt

            def combine(idx):
                j, g = order[idx]
                par = idx % 2
                so = par * 16
                yv = yq.ap[:, g * 256:(g + 1) * 256].rearrange("p (r e) -> p r e", r=4)
                for bi, pob in enumerate((acc_c[par], acc_s, acc_w)):
                    p4 = pob.ap.rearrange("p (r c) -> p r c", r=4)
                    if bi == 0:
                        cf = small.ap[:, so + 4:so + 8]
                        cfk = (small, ("cfc", par))
                    else:
                        cf = small.ap[:, 32 + bi * 4:36 + bi * 4]
                        cfk = (small, ("cf", bi))
                        sc.op("dve", lambda h, o_=cf, i_=p4[:, :, 64]: h.reciprocal(o_, i_), reads=[pob], writes=[cfk])
                        sc.tt("dve", cf, cf, gate3[:, j, g * 12 + bi:g * 12 + 12:3], ALU.mult,
                              reads=[cfk, (gate, j)], writes=[cfk])
                    cfb = cf.unsqueeze(2).to_broadcast([128, 4, 64])
                    if bi == 0:
                        sc.tt("dve", yv, p4[:, :, 0:64], cfb, ALU.mult, reads=[pob, cfk], writes=[(yq, g)])
                    else:
                        tm = tmpf()
                        tv = tm.ap[:, 0:256].rearrange("p (r e) -> p r e", r=4)
                        sc.tt("dve", tv, p4[:, :, 0:64], cfb, ALU.mult, reads=[pob, cfk], writes=[tm])
                        sc.tt("pool", yv, yv, tv, ALU.add, reads=[tm, (yq, g)], writes=[(yq, g)])
                if g == 1:
                    tok = slice(j * 128, (j + 1) * 128)
                    pty = sc.psum()
                    for c in range(4):
                        sc.transpose(pty.ap[:, c * 128:(c + 1) * 128], yq.ap[:, c * 128:(c + 1) * 128], ident.ap,
                                     reads=[yq, cst], writes=[(pty, c)])
                    sc.copy("dve", yT.ap[:, :, tok], pty.ap.rearrange("p (c t) -> p c t", c=4), reads=[pty],
                            writes=[(yT, (c, j // 4)) for c in range(4)])

            cmpsel(0)
            cmpsel_b(0)
            for idx in range(len(order)):
                if idx + 1 < len(order):
                    cmpsel(idx + 1)
                attends(idx)
                if idx + 1 < len(order):
                    cmpsel_b(idx + 1)
                combine(idx)
            sc.release(acc_c + [acc_s, acc_w])

        spill_v = xspill_d.rearrange("p (c t) -> p c t", c=NKC)
        mixfn = {"A": mixer_A, "B": mixer_B, "C": mixer_C, "D": mixer_D}
        stages = []
        for l in range(depth):
            if cfg.get("ffn1", True):
                stages.append(("L%d ffn1" % l, lambda l=l: ffn(l, 1, plan_only=True), lambda b, l=l: ffn(l, 1, base=b)))
            if cfg.get("mix", True):
                stages.append(("L%d mixpre" % l, None, lambda b, l=l: mix_pre(l)))
                first = True
                for mi, mname in enumerate("ABCD"):
                    if mname not in mixers:
                        continue
                    stages.append(("L%d mixer %s" % (l, mname), lambda l=l, f=mixfn[mname]: f(l, plan_only=True),
                                   lambda b, l=l, f=mixfn[mname]: f(l, base=b)))
                    stages.append(("L%d merge %s" % (l, mname), lambda l=l, mi=mi: merge(l, mi, False, plan_only=True),
                                   lambda b, l=l, mi=mi, first=first: (merge(l, mi, first, base=b), sc.barrier())))
                    first = False
                stages.append(("L%d outproj" % l, lambda l=l: out_proj(l, plan_only=True),
                               lambda b, l=l: (sc.dma("sp", xT.ap, spill_v, writes=[xT], skey="spill"), out_proj(l, base=b))))
            if cfg.get("ffn2", True):
                stages.append(("L%d ffn2" % l, lambda l=l: ffn(l, 2, plan_only=True), lambda b, l=l: ffn(l, 2, base=b)))

        def mix_pre(l):
            rmsnorm_to_hT(3 * l + 2)
            sc.dma("sp", spill_v, xT.ap, reads=[xT], skey="spill")
            sc.barrier()

        bases = {}

        def plan(i):
            if i < len(stages) and i not in bases:
                bases[i] = stages[i][1]() if stages[i][1] is not None else None

        for i, (nm, pf, rf) in enumerate(stages):
            plan(i)
            k = i + 1
            plan(k)
            while k < len(stages) and stages[k][1] is None:
                k += 1
                plan(k)
            sc.mark(nm)
            rf(bases[i])

        finals = []
        for t in range(S // 128):
            io = iobuf[t % 2]
            for half in range(2):
                ps = sc.psum()
                for j in range(4):
                    c = half * 4 + j
                    sc.transpose(ps.ap[:, j * 128:(j + 1) * 128], xT.ap[:, c, t * 128:(t + 1) * 128], ident.ap,
                                 reads=[(xT, (c, t // 4)), cst], writes=[(ps, j)])
                sc.copy("dve", io.ap[:, half * 512:(half + 1) * 512], ps.ap, reads=[ps], writes=[(io, half)])
            sc.dma("sp", out_d[t * 128:(t + 1) * 128, :], io.ap, reads=[io], skey=("io", t % 2))
        for key in (("io", 0), ("io", 1)):
            ds = sc.dsems[key]
            finals.append((ds[0], ds[1]))
        sc.mark("end")
        print("ENGINE OPS", {n: (e.count, len(e.ops)) for n, e in sc.eng.items()}, flush=True)
        print("MARKS", sc.marks, flush=True)
        sc.emit(finals)
    return nc


_CST = host_consts()


def kernel(**inputs):
    cfg = inputs.pop("_cfg", {})
    depth = cfg.get("depth", DEPTH)
    nc = build_program(cfg)
    x = np.ascontiguousarray(inputs["x"], dtype=np.float32)
    shared = {"cst": _CST}
    shared.update(_NSAC)
    for nm, _ in W_NAMES:
        shared[nm] = np.ascontiguousarray(np.asarray(inputs[nm], dtype=np.float32)[:depth])
    in_maps = []
    for b in range(8):
        m = dict(shared)
        m["x"] = x[b]
        in_maps.append(m)
    res = run_bass_kernel_spmd(nc, in_maps, core_ids=list(range(8)))
    return np.stack([np.asarray(r["out"], dtype=np.float32) for r in res.results], axis=0)
```

```python
import math
from contextlib import ExitStack

import numpy as np
import concourse.bass as bass
import concourse.mybir as mybir
from concourse.bass_utils import run_bass_kernel_spmd

F32 = mybir.dt.float32
BF16 = mybir.dt.bfloat16
AF = mybir.ActivationFunctionType
ALU = mybir.AluOpType
AX = mybir.AxisListType

D = 1024
S = 2048
DEPTH = 4
FFN = 2816
NKC = D // 128
NTT = S // 512
EPS = 1e-6
IN_PROJ = 8992


class Buf:
    def __init__(self, ap, name):
        self.ap = ap
        self.name = name
        self.st = {}

    def __getitem__(self, idx):
        return self.ap[idx]


class Eng:
    def __init__(self, name, sem):
        self.name = name
        self.sem = sem
        self.count = 0
        self.waited = {}
        self.ops = []


class Sched:
    def __init__(self, nc, stack):
        self.nc = nc
        self.stack = stack
        self.eng = {}
        for n in ("pe", "dve", "act", "pool", "sp"):
            self.eng[n] = Eng(n, stack.enter_context(nc.semaphore("s_" + n)))
        self.dsems = {}
        self.psum_banks = []
        self.psum_rr = 0
        self.n_ops = 0
        self.marks = []

    def sbuf(self, name, shape, dtype):
        t = self.stack.enter_context(self.nc.sbuf_tensor("sb_" + name, list(shape), dtype))
        return Buf(t[:], name)

    def view(self, ap, name):
        return Buf(ap, name)

    def dsem(self, key):
        if key not in self.dsems:
            self.dsems[key] = [self.stack.enter_context(self.nc.semaphore("d_" + str(key))), 0]
        return self.dsems[key]

    @staticmethod
    def _norm(acc):
        out = []
        for a in acc:
            if isinstance(a, tuple):
                out.append(a)
            else:
                out.append((a, None))
        return out

    @staticmethod
    def _states(buf, key):
        if key is None:
            return list(buf.st.values())
        r = []
        if key in buf.st:
            r.append(buf.st[key])
        if None in buf.st:
            r.append(buf.st[None])
        return r

    def _collect(self, engname, reads, writes):
        deps = []
        for (b, k) in reads:
            for st in self._states(b, k):
                if st[0] is not None:
                    deps.append((st[0], "raw"))
        for (b, k) in writes:
            for st in self._states(b, k):
                if st[0] is not None:
                    deps.append((st[0], "waw"))
                for t in st[1].values():
                    deps.append((t, "war"))
        return deps

    def _emit_waits(self, e, deps):
        for (tok, kind) in deps:
            sem, val, teng = tok[0], tok[1], tok[2]
            if teng is None:
                val = max(val, tok[3][1])
            if teng == e.name:
                if e.name == "pe":
                    continue
                if kind != "raw":
                    continue
            sid = id(sem)
            if e.waited.get(sid, 0) >= val:
                continue
            e.waited[sid] = val
            e.ops.append(("wait", sem, val))

    def _update(self, tok, reads, writes):
        sid = id(tok[0])
        for (b, k) in writes:
            if k is None:
                b.st = {None: [tok, {}]}
            else:
                b.st[k] = [tok, {}]
        for (b, k) in reads:
            st = b.st.setdefault(k, [None, {}])
            old = st[1].get(sid)
            if old is None or old[1] < tok[1]:
                st[1][sid] = tok

    def op(self, engname, fn, reads=(), writes=()):
        e = self.eng[engname]
        reads = self._norm(reads)
        writes = self._norm(writes)
        self._emit_waits(e, self._collect(engname, reads, writes))
        e.count += 1
        tok = (e.sem, e.count, e.name)
        e.ops.append(("op", fn, e.sem, 1))
        self._update(tok, reads, writes)
        self.n_ops += 1
        return tok

    def dma(self, engname, out, in_, reads=(), writes=(), skey=None, slow=False):
        e = self.eng[engname]
        reads = self._norm(reads)
        writes = self._norm(writes)
        self._emit_waits(e, self._collect(engname, reads, writes))
        ds = self.dsem(skey)
        if ds[1] > 0 and e.waited.get(id(ds[0]), 0) < ds[1]:
            e.waited[id(ds[0])] = ds[1]
            e.ops.append(("wait", ds[0], ds[1]))
        ds[1] += 16
        tok = (ds[0], ds[1], None, ds)
        if slow:
            e.ops.append(("op", lambda h: h.dma_start(out=out, in_=in_, allow_slow_non_contiguous=True), ds[0], 16))
        else:
            e.ops.append(("op", lambda h: h.dma_start(out=out, in_=in_), ds[0], 16))
        self._update(tok, reads, writes)
        self.n_ops += 1
        return tok

    def mark(self, name):
        self.marks.append((name, self.eng["pe"].count))

    def barrier(self):
        toks = [(e.sem, e.count, "*") for e in self.eng.values() if e.count > 0]
        toks += [(ds[0], ds[1], None, ds) for ds in self.dsems.values() if ds[1] > 0]
        for e in self.eng.values():
            for tk in toks:
                sem, val = tk[0], tk[1]
                if sem is e.sem and e.name != "sp":
                    pass
                sid = id(sem)
                if e.waited.get(sid, 0) >= val:
                    continue
                e.waited[sid] = val
                e.ops.append(("wait", sem, val))

    def make_psum(self):
        for i in range(8):
            t = self.stack.enter_context(self.nc.psum_tensor("ps%d" % i, [128, 512], F32))
            self.psum_banks.append(Buf(t[:], "ps%d" % i))
        self.psum_rot = list(self.psum_banks)

    def psum(self):
        b = self.psum_rot[self.psum_rr % len(self.psum_rot)]
        self.psum_rr += 1
        return b

    def reserve(self, n):
        got = self.psum_rot[-n:]
        self.psum_rot = self.psum_rot[:-n]
        return got

    def release(self, banks):
        self.psum_rot = self.psum_rot + list(banks)

    def mm(self, out, lhsT, rhs, start, stop, reads, writes):
        return self.op("pe", lambda h: h.matmul(out, lhsT, rhs, start=start, stop=stop), reads, writes)

    def transpose(self, out, in_, ident, reads, writes):
        return self.op("pe", lambda h: h.transpose(out, in_, ident), reads, writes)

    def act(self, out, in_, func, reads, writes, bias=0.0, scale=1.0, eng="act"):
        return self.op(eng, lambda h: h.activation(out, in_, func, bias=bias, scale=scale), reads, writes)

    def tt(self, eng, out, in0, in1, op, reads, writes):
        return self.op(eng, lambda h: h.tensor_tensor(out, in0, in1, op), reads, writes)

    def ts(self, eng, out, in0, s1, s2, op0, op1, reads, writes):
        return self.op(eng, lambda h: h.tensor_scalar(out, in0, s1, s2, op0, op1), reads, writes)

    def stt(self, eng, out, in0, scalar, in1, op0, op1, reads, writes):
        return self.op(eng, lambda h: h.scalar_tensor_tensor(out, in0, scalar, in1, op0, op1), reads, writes)

    def copy(self, eng, out, in_, reads, writes):
        return self.op(eng, lambda h: h.tensor_copy(out, in_), reads, writes)

    def emit(self, final_waits):
        nc = self.nc
        hmap = {"pe": "tensor", "dve": "vector", "act": "scalar", "pool": "gpsimd", "sp": "sync"}
        with nc.Block() as block:
            for n, e in self.eng.items():
                ops = list(e.ops)
                if n == "sp":
                    ops = ops + [("wait", s, v) for (s, v) in final_waits]

                def body(h, ops=ops):
                    for o in ops:
                        if o[0] == "wait":
                            h.wait_ge(o[1], o[2])
                        else:
                            o[1](h).then_inc(o[2], o[3])

                getattr(block, hmap[n])(body)


class WRing:
    def __init__(self, sc, name, nslots, elems):
        self.sc = sc
        self.slots = [sc.sbuf("%s%d" % (name, i), [128, elems], BF16) for i in range(nslots)]
        self.n = nslots
        self.plan = []
        self.issued = 0
        self.name = name

    def add(self, parts):
        self.plan.append(parts)
        return len(self.plan) - 1

    def _issue(self, hi):
        hi = min(len(self.plan), hi)
        while self.issued < hi:
            j = self.issued
            slot = self.slots[j % self.n]
            for pi, (vf, src) in enumerate(self.plan[j]):
                self.sc.dma("pool", vf(slot.ap), src, reads=(), writes=[slot],
                            skey=(self.name, j % self.n))
            self.issued += 1

    def done(self, j):
        self._issue(j + self.n + 1)

    def get(self, i, live=None):
        if live is None:
            self._issue(i + 1)
        else:
            self._issue(max(i + 1, i - live + 1 + self.n))
        return self.slots[i % self.n]


C_A_U, C_A_V = 0, 512
C_B_Z, C_B_X, C_B_B, C_B_C, C_B_DT = 1024, 1536, 2048, 2304, 2560
C_C_Q, C_C_KC, C_C_VC, C_C_KS, C_C_VS, C_C_KW, C_C_VW, C_C_G = 2568, 3080, 3208, 3336, 3464, 3592, 3720, 3848
C_D_A, C_D_B = 3872, 4384
C_MERGE = 4896
NEGM = -30000.0

W_NAMES = [("ffn1_norm", [D]), ("ffn1_w_in", [D, 2 * FFN]), ("ffn1_w_out", [FFN, D]),
           ("mix_norm", [D]), ("w_in", [D, IN_PROJ]),
           ("sgu_v_norm", [512]), ("sgu_w", [4, 128, 128]), ("sgu_b", [4, 128]),
           ("ssm_conv_w", [4, 1024]), ("ssm_conv_b", [1024]), ("ssm_dt_bias", [8]), ("ssm_a_log", [8]),
           ("ssm_d", [8]), ("ssm_norm", [512]),
           ("nsa_q_norm", [64]), ("nsa_k_norm", [64]), ("nsa_pe_k", [32, 64]), ("nsa_w1_k", [2048, 128]),
           ("nsa_w2_k", [128, 64]), ("nsa_pe_v", [32, 64]), ("nsa_w1_v", [2048, 128]), ("nsa_w2_v", [128, 64]),
           ("conv_dw_w", [31, 512]), ("conv_dw_b", [512]), ("conv_norm", [512]),
           ("w_branch", [4, 512, 1024]), ("w_out", [D, D]),
           ("ffn2_norm", [D]), ("ffn2_w_in", [D, 2 * FFN]), ("ffn2_w_out", [FFN, D])]


def host_consts():
    c = np.zeros((128, 6, 128), np.float32)
    r = np.arange(128)
    c[:, 0, :] = np.eye(128)
    c[:, 1, :] = (r[None, :] <= r[:, None])
    c[:, 2, :] = np.where(r[None, :] >= r[:, None], 0.0, NEGM)
    c[:, 3, :] = (r[:, None] <= r[None, :])
    c[:, 4, :] = 1.0
    c[:, 5, :] = 16.0 * r[:, None] - r[None, :]
    return c


def host_nsa_consts():
    t = np.arange(S)
    hi, lo = (t // 64).astype(np.float32), (t % 64).astype(np.float32)
    slopes = np.array([2.0 ** (-(i + 1)) for i in range(8)], np.float32)
    qal = np.zeros((4, 2, 16, 4, 128), np.float32)
    for h in range(8):
        g, r = h // 4, h % 4
        sl = slopes[h]
        qal[0, g, :, r, :] = (-sl * 64.0 * hi).reshape(16, 128)
        qal[1, g, :, r, :] = (-sl * lo).reshape(16, 128)
        qal[2, g, :, r, :] = sl * 64.0
        qal[3, g, :, r, :] = sl
    kal = np.stack([np.ones(S, np.float32), np.ones(S, np.float32), hi, lo], 0)
    cm = np.arange(128) * 16 + 15.5
    kalc = np.stack([np.ones(128), np.ones(128), np.floor(cm / 64.0), cm - 64.0 * np.floor(cm / 64.0)], 0).astype(np.float32)
    onehot = (np.arange(32)[:, None] == (t // 64)[None, :]).astype(np.float32)
    r = np.arange(128)
    mc = np.where(r[None, :] >= r[:, None], 0.0, NEGM).astype(np.float32)
    mf = np.where(r[:, None] > r[None, :], 0.0, NEGM).astype(np.float32)
    c_start = np.arange(128) * 16
    s_start = np.arange(32) * 64
    ov = ((c_start[:, None] <= s_start[None, :] + 63) & (c_start[:, None] + 31 >= s_start[None, :])).astype(np.float32)
    selp = np.zeros((128, 2, 63), np.float32)
    for npr in range(63):
        d = npr - 31
        for q in range(128):
            up = q >= 64
            if d < -1 or (d == -1 and up):
                selp[q, 0, npr], selp[q, 1, npr] = 1.0, 0.0
            elif (d == -1 and not up) or d == 0 or (d == 1 and up):
                selp[q, 0, npr], selp[q, 1, npr] = 0.0, 1e9
            else:
                selp[q, 0, npr], selp[q, 1, npr] = 0.0, -1e9
    return {"nc_qal": qal.reshape(4, 2, 8192), "nc_kal": kal, "nc_kalc": kalc, "nc_onehot": onehot,
            "nc_mc": np.tile(mc, (1, 4)), "nc_mf": np.tile(mf, (1, 4)), "nc_ov": ov, "nc_selp": selp}


_NSAC = host_nsa_consts()


def build_program(cfg):
    depth = cfg.get("depth", DEPTH)
    mixers = cfg.get("mixers", "ABCD")
    nc = bass.Bass("TRN2", target_bir_lowering=False)

    def din(name, shape, dtype=F32):
        return nc.dram_tensor(name, list(shape), dtype, kind="ExternalInput").ap()

    x_d = din("x", [S, D])
    cst_d = din("cst", [128, 6, 128])
    nsa_c = {k: din(k, list(v.shape)) for k, v in _NSAC.items()}
    w_d = {}
    for nm, shp in W_NAMES:
        w_d[nm] = din(nm, [depth] + shp)
    out_d = nc.dram_tensor("out", [S, D], F32, kind="ExternalOutput").ap()
    xspill_d = nc.dram_tensor("xspill", [128, NKC * S], F32, kind="Internal").ap()

    with ExitStack() as stack:
        sc = Sched(nc, stack)
        sc.make_psum()

        xT = sc.sbuf("xT", [128, NKC, S], F32)
        hT = sc.sbuf("hT", [128, NKC, S], BF16)
        mT = sc.sbuf("mT", [128, NKC, S], BF16)
        actT = sc.sbuf("actT", [128, 4, S], BF16)
        yT = actT
        cst = sc.sbuf("cst", [128, 6, 128], F32)
        ident = Buf(cst.ap[:, 0, :], "ident")
        ident_bf = sc.sbuf("ident_bf", [128, 128], BF16)
        ones_bf = sc.sbuf("ones_bf", [128, 128], BF16)
        gains = sc.sbuf("gains", [128, 3 * DEPTH, NKC], F32)
        iobuf = [sc.sbuf("io%d" % i, [128, D], F32) for i in range(2)]
        tmp_a = [sc.sbuf("tmpa%d" % i, [128, 512], F32) for i in range(3)]
        sq_bf = [sc.sbuf("sq%d" % i, [128, 512], BF16) for i in range(2)]
        rstd = sc.sbuf("rstd", [128, 512], F32)
        small = sc.sbuf("small", [128, 64], F32)
        selp_sb = sc.sbuf("selp", [128, 126], F32)
        ring = WRing(sc, "wr", 4, 4096)
        tmp_rr = [0]

        def tmpf():
            tmp_rr[0] += 1
            return tmp_a[tmp_rr[0] % 3]

        xflat = xT.ap.rearrange("p c t -> p (c t)")
        xflat_bf = xflat.bitcast(BF16)

        def scr(name, off, shape, dtype):
            n = int(np.prod(shape[1:]))
            if dtype == F32:
                assert off % 4 == 0
                ap = xflat[0:shape[0], off // 4: off // 4 + n]
            else:
                ap = xflat_bf[0:shape[0], off // 2: off // 2 + n]
            if len(shape) == 3:
                ap = ap.rearrange("p (a b) -> p a b", a=shape[1])
            return Buf(ap, name)

        sc.dma("sp", cst.ap, cst_d, writes=[cst], skey="c0")
        sc.op("dve", lambda h: h.memset(ones_bf.ap, 1.0), writes=[ones_bf])
        sc.dma("sp", selp_sb.ap, nsa_c["nc_selp"].rearrange("p a n -> p (a n)"), writes=[selp_sb], skey="c0")
        sc.copy("dve", ident_bf.ap, ident.ap, reads=[cst], writes=[ident_bf])
        for l in range(depth):
            for wi, nm in enumerate(("ffn1_norm", "ffn2_norm", "mix_norm")):
                sc.dma("sp", gains.ap[:, 3 * l + wi, :], w_d[nm][l].rearrange("(c p) -> p c", p=128),
                       writes=[(gains, (l, wi))], skey="c1", slow=True)

        for t in range(S // 128):
            io = iobuf[t % 2]
            sc.dma("sp", io.ap, x_d[t * 128:(t + 1) * 128, :], writes=[io], skey=("io", t % 2))
            for half in range(2):
                ps = sc.psum()
                for j in range(4):
                    c = half * 4 + j
                    sc.transpose(ps.ap[:, j * 128:(j + 1) * 128], io.ap[:, c * 128:(c + 1) * 128], ident.ap,
                                 reads=[io, cst], writes=[(ps, j)])
                sc.copy("dve", xT.ap[:, half * 4:half * 4 + 4, t * 128:(t + 1) * 128],
                        ps.ap.rearrange("p (c t) -> p c t", c=4),
                        reads=[ps], writes=[(xT, (half * 4 + j, t // 4)) for j in range(4)])

        RS_LN = set(cfg.get("rs_ln", ()))

        def rstd_op(site, o_ap, i_ap, ibuf, obuf, scale):
            if site in RS_LN:
                sc.act(o_ap, i_ap, AF.Ln, reads=[ibuf], writes=[obuf], bias=EPS, scale=scale)
                sc.act(o_ap, o_ap, AF.Exp, reads=[obuf], writes=[obuf], scale=-0.5)
            else:
                sc.act(o_ap, i_ap, AF.Sqrt, reads=[ibuf], writes=[obuf], bias=EPS, scale=scale)
                sc.op("dve", lambda h: h.reciprocal(o_ap, o_ap), reads=[obuf], writes=[obuf])

        def rmsnorm_to_hT(gidx):
            for tt in range(NTT):
                tsl = slice(tt * 512, (tt + 1) * 512)
                ps = sc.psum()
                for c in range(NKC):
                    sq = sq_bf[c % 2]
                    sc.act(sq.ap, xT.ap[:, c, tsl], AF.Square, reads=[(xT, (c, tt))], writes=[sq])
                    sc.mm(ps.ap, ones_bf.ap, sq.ap, c == 0, c == NKC - 1, reads=[ones_bf, sq], writes=[ps])
                rstd_op("hT", rstd.ap, ps.ap, ps, rstd, 1.0 / D)
                for c in range(NKC):
                    sc.stt("dve", hT.ap[:, c, tsl], xT.ap[:, c, tsl],
                           gains.ap[:, gidx, c:c + 1], rstd.ap, ALU.mult, ALU.mult,
                           reads=[(xT, (c, tt)), gains, rstd], writes=[(hT, (c, tt))])

        def wview8(slot, cw):
            return slot.ap[:, 0:8 * cw].rearrange("p (k c) -> p k c", k=8)

        def add_win(l, c0, cw):
            return ring.add([(lambda s, cw=cw: s[:, 0:8 * cw].rearrange("p (k c) -> p k c", k=8),
                              w_d["w_in"][l][:, c0:c0 + cw].rearrange("(k p) c -> p k c", p=128))])

        def inproj_fm(ps_ap, wslot, wv, col0, tok_sl, tt):
            for k in range(NKC):
                sc.mm(ps_ap, wv[:, k, col0:col0 + 128], hT.ap[:, k, tok_sl], k == 0, k == NKC - 1,
                      reads=[wslot, (hT, (k, tt))], writes=[])

        def ffn(l, which, base=None, plan_only=False):
            w_in = w_d["ffn%d_w_in" % which][l]
            w_out = w_d["ffn%d_w_out" % which][l]
            groups = [(f0, min(4, 22 - f0)) for f0 in range(0, 22, 4)]
            if base is not None:
                rmsnorm_to_hT(3 * l + (which - 1))
            for (f0, nf) in (groups if base is None else []):
                cw = nf * 128
                for off in (0, FFN):
                    ring.add([(lambda s, cw=cw: s[:, 0:8 * cw].rearrange("p (k c) -> p k c", k=8),
                               w_in[:, off + f0 * 128: off + f0 * 128 + cw].rearrange("(k p) c -> p k c", p=128))])
                ring.add([(lambda s, nf=nf: s[:, 0:nf * 1024].rearrange("p (f c) -> p f c", f=nf),
                           w_out[f0 * 128:(f0 + nf) * 128, :].rearrange("(f p) c -> p f c", p=128))])
            if base is None:
                base = len(ring.plan) - 3 * len(groups)
                if plan_only:
                    return base
                rmsnorm_to_hT(3 * l + (which - 1))
            for gi, (f0, nf) in enumerate(groups):
                cw = nf * 128
                wg = ring.get(base + 3 * gi, live=1)
                wu = ring.get(base + 3 * gi + 1, live=2)
                wgv = wview8(wg, cw)
                wuv = wview8(wu, cw)
                for tt in range(NTT):
                    tsl = slice(tt * 512, (tt + 1) * 512)
                    for fi in range(nf):
                        pg = sc.psum()
                        pu = sc.psum()
                        for k in range(NKC):
                            sc.mm(pg.ap, wgv[:, k, fi * 128:(fi + 1) * 128], hT.ap[:, k, tsl], k == 0, k == NKC - 1,
                                  reads=[wg, (hT, (k, tt))], writes=[pg])
                        for k in range(NKC):
                            sc.mm(pu.ap, wuv[:, k, fi * 128:(fi + 1) * 128], hT.ap[:, k, tsl], k == 0, k == NKC - 1,
                                  reads=[wu, (hT, (k, tt))], writes=[pu])
                        tm = tmpf()
                        sc.act(tm.ap, pg.ap, AF.Silu, reads=[pg], writes=[tm])
                        sc.tt("dve", actT.ap[:, fi, tsl], tm.ap, pu.ap, ALU.mult,
                              reads=[tm, pu], writes=[(actT, (fi, tt))])
                wo = ring.get(base + 3 * gi + 2, live=1)
                wov = wo.ap[:, 0:nf * 1024].rearrange("p (f c) -> p f c", f=nf)
                for tt in range(NTT):
                    tsl = slice(tt * 512, (tt + 1) * 512)
                    for dc in range(NKC):
                        po = sc.psum()
                        for fi in range(nf):
                            sc.mm(po.ap, wov[:, fi, dc * 128:(dc + 1) * 128], actT.ap[:, fi, tsl], fi == 0, fi == nf - 1,
                                  reads=[wo, (actT, (fi, tt))], writes=[po])
                        sc.stt("dve", xT.ap[:, dc, tsl], po.ap, 0.5, xT.ap[:, dc, tsl], ALU.mult, ALU.add,
                               reads=[po, (xT, (dc, tt))], writes=[(xT, (dc, tt))])

        def load_T(dst_fn, src2d, rows, nchunk, stage):
            sc.dma("sp", stage.ap[0:rows, 0:nchunk * 128], src2d, writes=[stage], skey="ldT")
            ps = sc.psum()
            for c in range(nchunk):
                sc.transpose(ps.ap[:, c * 128:c * 128 + rows], stage.ap[0:rows, c * 128:(c + 1) * 128],
                             ident.ap[0:rows, 0:rows], reads=[stage, cst], writes=[(ps, c)])
            for c in range(nchunk):
                sc.copy("dve", dst_fn(c), ps.ap[:, c * 128:c * 128 + rows], reads=[ps], writes=[])

        def gelu_psum(p_ap, out_ap, pbuf, outbuf_w):
            n = p_ap.shape[-1] if len(p_ap.shape) == 2 else None
            t1 = tmpf()
            t2 = tmpf()
            a1 = t1.ap[:, 0:512] if n is None else t1.ap[:, 0:n]
            a2 = t2.ap[:, 0:512] if n is None else t2.ap[:, 0:n]
            sc.act(a1, p_ap, AF.Square, reads=[pbuf], writes=[t1])
            sc.ts("dve", a1, a1, 0.044715, 1.0, ALU.mult, ALU.add, reads=[t1], writes=[t1])
            sc.tt("dve", a1, a1, p_ap, ALU.mult, reads=[t1, pbuf], writes=[t1])
            sc.act(a2, a1, AF.Sigmoid, reads=[t1], writes=[t2], scale=1.5957691216057308)
            sc.tt("dve", out_ap, a2, p_ap, ALU.mult, reads=[t2, pbuf], writes=outbuf_w)

        def merge(l, i, first, base=None, plan_only=False):
            if base is None:
                base = len(ring.plan)
                wb_d = w_d["w_branch"][l][i]
                ring.add([(lambda s: s[:, 0:4096].rearrange("p (f c) -> p f c", f=4),
                           wb_d.rearrange("(f p) c -> p f c", p=128))])
                for half in range(2):
                    add_win(l, C_MERGE + i * 1024 + half * 512, 512)
                if plan_only:
                    return base
            wb = ring.get(base, live=1)
            wbv = wb.ap[:, 0:4096].rearrange("p (f c) -> p f c", f=4)
            for half in range(2):
                wl = ring.get(base + 1 + half, live=2 + half)
                wlv = wview8(wl, 512)
                for tt in range(NTT):
                    tsl = slice(tt * 512, (tt + 1) * 512)
                    for j in range(4):
                        dc = half * 4 + j
                        pl = sc.psum()
                        pb = sc.psum()
                        for k in range(NKC):
                            sc.mm(pl.ap, wlv[:, k, j * 128:(j + 1) * 128], hT.ap[:, k, tsl], k == 0, k == NKC - 1,
                                  reads=[wl, (hT, (k, tt))], writes=[pl])
                        for k in range(4):
                            sc.mm(pb.ap, wbv[:, k, dc * 128:(dc + 1) * 128], yT.ap[:, k, tsl], k == 0, k == 3,
                                  reads=[wb, (yT, (k, tt))], writes=[pb])
                        tm = tmpf()
                        sc.act(tm.ap, pl.ap, AF.Sigmoid, reads=[pl], writes=[tm])
                        if first:
                            sc.tt("dve", mT.ap[:, dc, tsl], tm.ap, pb.ap, ALU.mult,
                                  reads=[tm, pb], writes=[(mT, (dc, tt))])
                        else:
                            sc.tt("dve", tm.ap, tm.ap, pb.ap, ALU.mult, reads=[tm, pb], writes=[tm])
                            sc.tt("pool", mT.ap[:, dc, tsl], mT.ap[:, dc, tsl], tm.ap, ALU.add,
                                  reads=[tm, (mT, (dc, tt))], writes=[(mT, (dc, tt))])

        def out_proj(l, base=None, plan_only=False):
            if base is None:
                base = len(ring.plan)
                for half in range(2):
                    ring.add([(lambda s: s[:, 0:4096].rearrange("p (k c) -> p k c", k=8),
                               w_d["w_out"][l][:, half * 512:(half + 1) * 512].rearrange("(k p) c -> p k c", p=128))])
                if plan_only:
                    return base
            for half in range(2):
                wo = ring.get(base + half, live=1)
                wov = wview8(wo, 512)
                for tt in range(NTT):
                    tsl = slice(tt * 512, (tt + 1) * 512)
                    for j in range(4):
                        dc = half * 4 + j
                        po = sc.psum()
                        for k in range(NKC):
                            sc.mm(po.ap, wov[:, k, j * 128:(j + 1) * 128], mT.ap[:, k, tsl], k == 0, k == NKC - 1,
                                  reads=[wo, (mT, (k, tt))], writes=[po])
                        sc.tt("dve", xT.ap[:, dc, tsl], po.ap, xT.ap[:, dc, tsl], ALU.add,
                              reads=[po, (xT, (dc, tt))], writes=[(xT, (dc, tt))])

        def mixer_D(l, base=None, plan_only=False):
            if base is None:
                base = len(ring.plan)
                add_win(l, C_D_A, 512)
                add_win(l, C_D_B, 512)
                if plan_only:
                    return base
            u = scr("D_u", 0, [128, 30 + S], BF16)
            dg = scr("D_dg", 4352, [128, 31, 128], BF16)
            wD = scr("D_w", 12288, [128, 4, 32], F32)
            bg = scr("D_bg", 12800, [128, 8], F32)
            stage = scr("D_stage", 13056, [128, 512], F32)
            load_T(lambda c: wD.ap[:, c, 0:31], w_d["conv_dw_w"][l], 31, 4, stage)
            sc.dma("sp", bg.ap[:, 0:4], w_d["conv_dw_b"][l].rearrange("(c p) -> p c", p=128), writes=[(bg, 0)],
                   skey="c1", slow=True)
            sc.dma("sp", bg.ap[:, 4:8], w_d["conv_norm"][l].rearrange("(c p) -> p c", p=128), writes=[(bg, 1)],
                   skey="c1", slow=True)
            sc.op("dve", lambda h: h.memset(u.ap[:, 0:30], 0.0), writes=[(u, "pad")])
            ssb = sc.reserve(4)
            wa = ring.get(base, live=1)
            wb = ring.get(base + 1, live=2)
            wav, wbv = wview8(wa, 512), wview8(wb, 512)
            for c in range(4):
                sc.tt("dve", dg.ap, ident_bf.ap.unsqueeze(1).to_broadcast([128, 31, 128]),
                      wD.ap[:, c, 0:31].unsqueeze(2).to_broadcast([128, 31, 128]), ALU.mult,
                      reads=[ident_bf, wD], writes=[dg])
                for tt in range(NTT):
                    tsl = slice(tt * 512, (tt + 1) * 512)
                    pa, pb = sc.psum(), sc.psum()
                    for k in range(NKC):
                        sc.mm(pa.ap, wav[:, k, c * 128:(c + 1) * 128], hT.ap[:, k, tsl], k == 0, k == NKC - 1,
                              reads=[wa, (hT, (k, tt))], writes=[pa])
                    for k in range(NKC):
                        sc.mm(pb.ap, wbv[:, k, c * 128:(c + 1) * 128], hT.ap[:, k, tsl], k == 0, k == NKC - 1,
                              reads=[wb, (hT, (k, tt))], writes=[pb])
                    tm = tmpf()
                    sc.act(tm.ap, pb.ap, AF.Sigmoid, reads=[pb], writes=[tm])
                    sc.tt("dve", u.ap[:, 30 + tt * 512: 30 + (tt + 1) * 512], tm.ap, pa.ap, ALU.mult,
                          reads=[tm, pa], writes=[(u, tt)])
                for tt in range(NTT):
                    tsl = slice(tt * 512, (tt + 1) * 512)
                    pcv = sc.psum()
                    for k in range(31):
                        sc.mm(pcv.ap, dg.ap[:, k, :], u.ap[:, tt * 512 + k: tt * 512 + k + 512], k == 0, k == 30,
                              reads=[dg, u], writes=[pcv])
                    sq = sq_bf[tt % 2]
                    sc.act(sq.ap, pcv.ap, AF.Square, reads=[pcv, bg], writes=[sq], bias=bg.ap[:, c:c + 1])
                    sc.mm(ssb[tt].ap, ones_bf.ap, sq.ap, c == 0, c == 3, reads=[ones_bf, sq], writes=[ssb[tt]])
                    sc.act(yT.ap[:, c, tsl], pcv.ap, AF.Identity, reads=[pcv, bg], writes=[(yT, (c, tt))],
                           bias=bg.ap[:, c:c + 1])
            for tt in range(NTT):
                tsl = slice(tt * 512, (tt + 1) * 512)
                rstd_op("D", rstd.ap, ssb[tt].ap, ssb[tt], rstd, 1.0 / 512)
                for c in range(4):
                    tm = tmpf()
                    sc.stt("dve", tm.ap, yT.ap[:, c, tsl], bg.ap[:, 4 + c:5 + c], rstd.ap, ALU.mult, ALU.mult,
                           reads=[(yT, (c, tt)), bg, rstd], writes=[tm])
                    sc.act(yT.ap[:, c, tsl], tm.ap, AF.Silu, reads=[tm], writes=[(yT, (c, tt))])
            sc.release(ssb)

        def mixer_A(l, base=None, plan_only=False):
            if base is None:
                base = len(ring.plan)
                add_win(l, C_A_U, 512)
                add_win(l, C_A_V, 512)
                if plan_only:
                    return base
            stage = scr("A_stage", 0, [128, 512], F32)
            wTs = scr("A_wT", 2048, [128, 4, 128], BF16)
            gbc = scr("A_gbc", 3072, [128, 512], F32)
            brow_f = scr("A_browf", 5120, [1, 512], F32)
            brow = scr("A_brow", 7168, [1, 512], BF16)
            ug = scr("A_ug", 8192, [128, 4, 512], BF16)
            vg2 = [scr("A_vg%d" % i, 12288 + i * 2048, [128, 512], F32) for i in range(2)]
            vn2 = [scr("A_vn%d" % i, 16384 + i * 1024, [128, 512], BF16) for i in range(2)]
            ssv = scr("A_ssv", 18432, [128, 8], F32)
            for g in range(4):
                sc.dma("sp", stage.ap[:, g * 128:(g + 1) * 128], w_d["sgu_w"][l][g], writes=[(stage, g)], skey="ldA")
            sc.tt("dve", stage.ap.rearrange("p (g s) -> p g s", g=4), stage.ap.rearrange("p (g s) -> p g s", g=4),
                  cst.ap[:, 1:2, :].to_broadcast([128, 4, 128]), ALU.mult, reads=[stage, cst], writes=[stage])
            ps = sc.psum()
            for g in range(4):
                sc.transpose(ps.ap[:, g * 128:(g + 1) * 128], stage.ap[:, g * 128:(g + 1) * 128], ident.ap,
                             reads=[stage, cst], writes=[(ps, g)])
            sc.copy("dve", wTs.ap, ps.ap.rearrange("p (g s) -> p g s", g=4), reads=[ps], writes=[wTs])
            sc.dma("sp", gbc.ap, w_d["sgu_v_norm"][l].partition_broadcast(128), writes=[gbc], skey="ldA")
            sc.dma("sp", brow_f.ap, w_d["sgu_b"][l].rearrange("g t -> (g t)")[None, :], writes=[brow_f], skey="ldA")
            sc.copy("dve", brow.ap, brow_f.ap, reads=[brow_f], writes=[brow])
            wu = ring.get(base, live=1)
            wv = ring.get(base + 1, live=2)
            wuv, wvv = wview8(wu, 512), wview8(wv, 512)
            def a_u(tt, g):
                tsl = slice(tt * 512, (tt + 1) * 512)
                pu = sc.psum()
                for k in range(NKC):
                    sc.mm(pu.ap, wuv[:, k, g * 128:(g + 1) * 128], hT.ap[:, k, tsl], k == 0, k == NKC - 1,
                          reads=[wu, (hT, (k, tt))], writes=[pu])
                gelu_psum(pu.ap, ug.ap[:, g, :], pu, [(ug, g)])

            def a_vproj(tt, sub):
                t0 = tt * 512 + sub * 128
                pv = sc.psum()
                for k in range(NKC):
                    sc.mm(pv.ap, hT.ap[:, k, t0:t0 + 128], wvv[:, k, :], k == 0, k == NKC - 1,
                          reads=[wv, (hT, (k, tt))], writes=[pv])
                return pv

            def a_vrest(tt, sub, pv):
                t0 = tt * 512 + sub * 128
                par = sub % 2
                vgp, vnp = vg2[par], vn2[par]
                gelu_psum(pv.ap, vgp.ap, pv, [vgp])
                tj = tmpf()
                so = par * 4
                sc.op("act", lambda h, o=tj.ap, i=vgp.ap, a=ssv.ap[:, so:so + 1]: h.activation(o, i, AF.Square, accum_out=a),
                      reads=[vgp], writes=[tj, (ssv, so)])
                sc.act(ssv.ap[:, so + 1:so + 2], ssv.ap[:, so:so + 1], AF.Sqrt, reads=[(ssv, so)], writes=[(ssv, so + 1)],
                       bias=EPS, scale=1.0 / 512)
                sc.op("dve", lambda h, o=ssv.ap[:, so + 2:so + 3], i=ssv.ap[:, so + 1:so + 2]: h.reciprocal(o, i),
                      reads=[(ssv, so + 1)], writes=[(ssv, so + 2)])
                sc.stt("dve", vnp.ap, vgp.ap, ssv.ap[:, so + 2:so + 3], gbc.ap, ALU.mult, ALU.mult,
                       reads=[vgp, (ssv, so + 2), gbc], writes=[vnp])
                pm = sc.psum()
                for g in range(4):
                    sc.mm(pm.ap[:, g * 128:(g + 1) * 128], vnp.ap[:, g * 128:(g + 1) * 128], wTs.ap[:, g, :], True, False,
                          reads=[vnp, wTs], writes=[(pm, g)])
                    sc.mm(pm.ap[:, g * 128:(g + 1) * 128], ones_bf.ap[0:1, :], brow.ap[0:1, g * 128:(g + 1) * 128],
                          False, True, reads=[ones_bf, brow], writes=[(pm, g)])
                sc.tt("dve", yT.ap[:, :, t0:t0 + 128], ug.ap[:, :, sub * 128:(sub + 1) * 128],
                      pm.ap.rearrange("p (g t) -> p g t", g=4), ALU.mult,
                      reads=[ug, pm], writes=[(yT, (g, tt)) for g in range(4)])

            for tt in range(NTT):
                for g in range(4):
                    a_u(tt, g)
                pend = a_vproj(tt, 0)
                for sub in range(4):
                    nxt = a_vproj(tt, sub + 1) if sub + 1 < 4 else None
                    a_vrest(tt, sub, pend)
                    pend = nxt

        def mixer_B(l, base=None, plan_only=False):
            if base is None:
                base = len(ring.plan)
                add_win(l, C_B_X, 512)
                add_win(l, C_B_B, 512)
                add_win(l, C_B_Z, 512)
                add_win(l, C_B_DT, 8)
                if plan_only:
                    return base
            raw = scr("B_raw", 0, [128, 3 + S], F32)
            xsT = scr("B_xsT", 8448, [128, 4, S], BF16)
            bcT = scr("B_bcT", 24832, [128, 4, S], BF16)
            zsT = scr("B_zsT", 41216, [128, 4, S], BF16)
            o = 57600
            cw = scr("B_cw", o, [128, 8, 4], F32); o += 128
            cb = scr("B_cb", o, [128, 8], F32); o += 32
            dtT = scr("B_dt", o, [128, 16, 8], F32); o += 512
            aT = scr("B_a", o, [128, 16, 8], F32); o += 512
            dtb_bc = scr("B_dtb", o, [128, 8], F32); o += 32
            A_bc = scr("B_A", o, [128, 8], F32); o += 32
            Dp = scr("B_Dp", o, [128, 4], F32); o += 16
            gn = scr("B_gn", o, [128, 4], F32); o += 16
            sm = scr("B_sm", o, [128, 64], F32); o += 256
            xdt = scr("B_xdt", o, [128, 512], BF16); o += 1024
            xdd = scr("B_xdd", o, [128, 512], BF16); o += 1024
            btok = scr("B_btok", o, [128, 256], BF16); o += 512
            Hs = scr("B_H", o, [128, 512], F32); o += 2048
            Hbf = scr("B_Hbf", o, [128, 512], BF16); o += 1024
            assert o <= 65536
            ST = scr("B_ST", 0, [128, 8, 128], BF16)
            CTs = scr("B_CTs", 2048, [128, 8, 128], BF16)
            yz = scr("B_yz", 4096, [128, 512], F32)
            sqb = scr("B_sqb", 6144, [128, 512], BF16)
            rsb = scr("B_rs", 7168, [128, 128], F32)
            rsb2 = [rsb, scr("B_rs1", 7680, [128, 128], F32)]
            stage = iobuf[0]
            atri = Buf(iobuf[0].ap.rearrange("p (h l) -> p h l", h=8), "B_atri")
            argm = Buf(iobuf[1].ap.rearrange("p (h l) -> p h l", h=8), "B_argm")

            for hf in range(2):
                load_T(lambda c, hf=hf: cw.ap[:, hf * 4 + c, 0:4], w_d["ssm_conv_w"][l][:, hf * 512:(hf + 1) * 512], 4, 4,
                       stage)
            sc.dma("sp", cb.ap, w_d["ssm_conv_b"][l].rearrange("(c p) -> p c", p=128), writes=[cb], skey="c1", slow=True)
            sc.dma("sp", gn.ap, w_d["ssm_norm"][l].rearrange("(c p) -> p c", p=128), writes=[gn], skey="c1", slow=True)
            sc.dma("sp", dtb_bc.ap, w_d["ssm_dt_bias"][l].partition_broadcast(128), writes=[dtb_bc], skey="c1")
            sc.dma("sp", A_bc.ap, w_d["ssm_a_log"][l].partition_broadcast(128), writes=[A_bc], skey="c1")
            for c in range(4):
                for hf in range(2):
                    sc.dma("sp", Dp.ap[hf * 64:(hf + 1) * 64, c:c + 1],
                           w_d["ssm_d"][l][2 * c + hf:2 * c + hf + 1].partition_broadcast(64), writes=[(Dp, (c, hf))],
                           skey="c1")
            sc.act(A_bc.ap, A_bc.ap, AF.Exp, reads=[A_bc], writes=[A_bc])
            sc.ts("dve", A_bc.ap, A_bc.ap, -1.0, None, ALU.mult, ALU.bypass, reads=[A_bc], writes=[A_bc])
            sc.op("dve", lambda h: h.memset(raw.ap[:, 0:3], 0.0), writes=[(raw, "pad")])

            for blk in range(2):
                wx = ring.get(base + blk, live=1)
                wxv = wview8(wx, 512)
                for cc in range(4):
                    c = blk * 4 + cc
                    dst = xsT if blk == 0 else bcT
                    for tt in range(NTT):
                        tsl = slice(tt * 512, (tt + 1) * 512)
                        px = sc.psum()
                        for k in range(NKC):
                            sc.mm(px.ap, wxv[:, k, cc * 128:(cc + 1) * 128], hT.ap[:, k, tsl], k == 0, k == NKC - 1,
                                  reads=[wx, (hT, (k, tt))], writes=[px])
                        sc.copy("dve", raw.ap[:, 3 + tt * 512:3 + (tt + 1) * 512], px.ap, reads=[px], writes=[(raw, tt)])
                        tm = tmpf()
                        sc.ts("dve", tm.ap, raw.ap[:, tt * 512:tt * 512 + 512], cw.ap[:, c, 0:1], cb.ap[:, c:c + 1],
                              ALU.mult, ALU.add, reads=[raw, cw, cb], writes=[tm])
                        for k in range(1, 4):
                            sc.stt("dve", tm.ap, raw.ap[:, tt * 512 + k:tt * 512 + k + 512], cw.ap[:, c, k:k + 1], tm.ap,
                                   ALU.mult, ALU.add, reads=[raw, cw, tm], writes=[tm])
                        sc.act(dst.ap[:, cc, tsl], tm.ap, AF.Silu, reads=[tm], writes=[(dst, (cc, tt))])
            wz = ring.get(base + 2, live=1)
            wzv = wview8(wz, 512)
            for c in range(4):
                for tt in range(NTT):
                    tsl = slice(tt * 512, (tt + 1) * 512)
                    pz = sc.psum()
                    for k in range(NKC):
                        sc.mm(pz.ap, wzv[:, k, c * 128:(c + 1) * 128], hT.ap[:, k, tsl], k == 0, k == NKC - 1,
                              reads=[wz, (hT, (k, tt))], writes=[pz])
                    sc.act(zsT.ap[:, c, tsl], pz.ap, AF.Silu, reads=[pz], writes=[(zsT, (c, tt))])
            wdt = ring.get(base + 3, live=1)
            wdtv = wview8(wdt, 8)
            pdt = sc.psum()
            for ci in range(16):
                for k in range(NKC):
                    sc.mm(pdt.ap[:, ci * 8:(ci + 1) * 8], hT.ap[:, k, ci * 128:(ci + 1) * 128], wdtv[:, k, :], k == 0,
                          k == NKC - 1, reads=[wdt, (hT, (k, ci // 4))], writes=[(pdt, ci)])
            sc.tt("dve", dtT.ap, pdt.ap[:, 0:128].rearrange("p (c h) -> p c h", c=16),
                  dtb_bc.ap.unsqueeze(1).to_broadcast([128, 16, 8]), ALU.add, reads=[pdt, dtb_bc], writes=[dtT])
            sc.act(dtT.ap, dtT.ap, AF.Exp, reads=[dtT], writes=[dtT])
            sc.act(dtT.ap, dtT.ap, AF.Ln, reads=[dtT], writes=[dtT], bias=1.0)
            sc.tt("dve", aT.ap, dtT.ap, A_bc.ap.unsqueeze(1).to_broadcast([128, 16, 8]), ALU.mult,
                  reads=[dtT, A_bc], writes=[aT])

            triU = cst.ap[:, 3, :]
            maskT = cst.ap[:, 2, :]
            ones_f = cst.ap[:, 4, :]
            ST2 = [(ST, ST.ap), (tmp_a[0], tmp_a[0].ap.bitcast(BF16).rearrange("p (h l) -> p h l", h=8))]
            CT2 = [(CTs, CTs.ap), (tmp_a[1], tmp_a[1].ap.bitcast(BF16).rearrange("p (h l) -> p h l", h=8))]
            XDT2 = [(xdt, xdt.ap), (sq_bf[0], sq_bf[0].ap)]
            XDD2 = [(xdd, xdd.ap), (sq_bf[1], sq_bf[1].ap)]
            BTK2 = [(btok, btok.ap), (tmp_a[2], tmp_a[2].ap.bitcast(BF16)[:, 0:256])]

            def front(ci):
                p = ci % 2
                so = p * 32
                tok = slice(ci * 128, (ci + 1) * 128)
                tt = ci // 4
                STb, STa = ST2[p]
                CTb, CTa = CT2[p]
                XTb, XTa = XDT2[p]
                XDb, XDa = XDD2[p]
                BKb, BKa = BTK2[p]
                a_c = aT.ap[:, ci, :]
                pc = sc.psum()
                sc.mm(pc.ap[:, 0:8], triU, a_c, True, True, reads=[cst, aT], writes=[pc])
                negacs = sm.ap[:, so:so + 8]
                sc.ts("dve", negacs, pc.ap[:, 0:8], -1.0, None, ALU.mult, ALU.bypass, reads=[pc],
                      writes=[(sm, ("neg", p))])
                sc.tt("dve", atri.ap, triU.unsqueeze(1).to_broadcast([128, 8, 128]),
                      a_c.unsqueeze(2).to_broadcast([128, 8, 128]), ALU.mult, reads=[cst, aT], writes=[atri])
                pr = [sc.psum(), sc.psum()]
                for hb in range(2):
                    sc.mm(pr[hb].ap, ones_f, atri.ap[:, hb * 4:(hb + 1) * 4, :], True, True,
                          reads=[cst, atri], writes=[pr[hb]])
                for hb in range(2):
                    sc.tt("dve", argm.ap[:, hb * 4:(hb + 1) * 4, :], pr[hb].ap.rearrange("p (h l) -> p h l", h=4),
                          maskT.unsqueeze(1).to_broadcast([128, 4, 128]), ALU.add, reads=[pr[hb], cst],
                          writes=[(argm, hb)])
                for h in range(8):
                    sc.act(argm.ap[:, h, :], argm.ap[:, h, :], AF.Exp, reads=[(argm, h // 4), (sm, ("neg", p))],
                           writes=[(argm, h // 4)], bias=sm.ap[:, so + h:so + h + 1])
                pg = sc.psum()
                for g in range(2):
                    sc.mm(pg.ap[:, g * 128:(g + 1) * 128], bcT.ap[:, g, tok], bcT.ap[:, 2 + g, tok], True, True,
                          reads=[(bcT, (g, tt)), (bcT, (2 + g, tt))], writes=[(pg, g)])
                for g in range(2):
                    sc.tt("dve", STa[:, g * 4:(g + 1) * 4, :], argm.ap[:, g * 4:(g + 1) * 4, :],
                          pg.ap[:, g * 128:(g + 1) * 128].unsqueeze(1).to_broadcast([128, 4, 128]), ALU.mult,
                          reads=[(argm, g), pg], writes=[(STb, g)])
                for hb in range(2):
                    sc.act(atri.ap[:, hb * 4:(hb + 1) * 4, :], pr[hb].ap.rearrange("p (h l) -> p h l", h=4), AF.Exp,
                           reads=[pr[hb]], writes=[atri])
                for g in range(2):
                    sc.tt("dve", CTa[:, g * 4:(g + 1) * 4, :], atri.ap[:, g * 4:(g + 1) * 4, :],
                          bcT.ap[:, 2 + g, tok].unsqueeze(1).to_broadcast([128, 4, 128]), ALU.mult,
                          reads=[atri, (bcT, (2 + g, tt))], writes=[(CTb, g)])
                ptb = sc.psum()
                ptv = ptb.ap.bitcast(BF16)
                for c in range(4):
                    sc.transpose(ptv[:, c * 128:(c + 1) * 128], xsT.ap[:, c, tok], ident_bf.ap,
                                 reads=[(xsT, (c, tt)), ident_bf], writes=[(ptb, c)])
                for g in range(2):
                    sc.transpose(ptv[:, 512 + g * 128:512 + (g + 1) * 128], bcT.ap[:, g, tok], ident_bf.ap,
                                 reads=[(bcT, (g, tt)), ident_bf], writes=[(ptb, 4 + g)])
                dec = sm.ap[:, so + 8:so + 16]
                for hb in range(2):
                    last = pr[hb].ap.rearrange("p (h l) -> p h l", h=4)[:, :, 127]
                    sc.tt("dve", sm.ap[:, so + 8 + hb * 4:so + 12 + hb * 4], last, sm.ap[:, so + hb * 4:so + hb * 4 + 4],
                          ALU.add, reads=[pr[hb], (sm, ("neg", p))], writes=[(sm, ("dec", p))])
                    sc.act(sm.ap[:, so + 16 + hb * 4:so + 20 + hb * 4], last, AF.Exp, reads=[pr[hb]],
                           writes=[(sm, ("eA", p))])
                sc.act(dec, dec, AF.Exp, reads=[(sm, ("dec", p))], writes=[(sm, ("dec", p))])
                sc.tt("dve", sm.ap[:, so + 24:so + 32], dec, dtT.ap[:, ci, :], ALU.mult, reads=[(sm, ("dec", p)), dtT],
                      writes=[(sm, ("dtdec", p))])
                xtok = ptv[:, 0:512].rearrange("p (h e) -> p h e", h=8)
                sc.tt("dve", XTa.rearrange("p (h e) -> p h e", h=8), xtok,
                      dtT.ap[:, ci, :].unsqueeze(2).to_broadcast([128, 8, 64]), ALU.mult, reads=[ptb, dtT], writes=[XTb])
                sc.tt("dve", XDa.rearrange("p (h e) -> p h e", h=8), xtok,
                      sm.ap[:, so + 24:so + 32].unsqueeze(2).to_broadcast([128, 8, 64]), ALU.mult,
                      reads=[ptb, (sm, ("dtdec", p))], writes=[XDb])
                sc.op("act", lambda h, o_=BKa, i_=ptv[:, 512:768]: h.copy(o_, i_), reads=[ptb], writes=[BKb])

            def back(ci):
                p = ci % 2
                so = p * 32
                tok = slice(ci * 128, (ci + 1) * 128)
                tt = ci // 4
                STb, STa = ST2[p]
                CTb, CTa = CT2[p]
                XTb, XTa = XDT2[p]
                XDb, XDa = XDD2[p]
                BKb, BKa = BTK2[p]
                eA = sm.ap[:, so + 16:so + 24]
                py = sc.psum()
                for h in range(8):
                    outp = py.ap[(h % 2) * 64:(h % 2) * 64 + 64, (h // 2) * 128:(h // 2) * 128 + 128]
                    sc.mm(outp, XTa[:, h * 64:(h + 1) * 64], STa[:, h, :], True, ci == 0,
                          reads=[XTb, (STb, h // 4)], writes=[(py, h)])
                    if ci > 0:
                        sc.mm(outp, Hbf.ap[:, h * 64:(h + 1) * 64], CTa[:, h, :], False, True,
                              reads=[Hbf, (CTb, h // 4)], writes=[(py, h)])
                pS = sc.psum()
                for g in range(2):
                    sc.mm(pS.ap[:, g * 256:(g + 1) * 256], BKa[:, g * 128:(g + 1) * 128], XDa[:, g * 256:(g + 1) * 256],
                          True, True, reads=[BKb, XDb], writes=[(pS, g)])
                if ci == 0:
                    sc.copy("dve", Hs.ap, pS.ap, reads=[pS], writes=[Hs])
                else:
                    sc.tt("dve", Hs.ap.rearrange("p (h e) -> p h e", h=8), Hs.ap.rearrange("p (h e) -> p h e", h=8),
                          eA.unsqueeze(2).to_broadcast([128, 8, 64]), ALU.mult, reads=[Hs, (sm, ("eA", p))], writes=[Hs])
                    sc.tt("dve", Hs.ap, Hs.ap, pS.ap, ALU.add, reads=[Hs, pS], writes=[Hs])
                if ci < 15:
                    sc.copy("pool", Hbf.ap, Hs.ap, reads=[Hs], writes=[Hbf])
                for c in range(4):
                    sc.stt("dve", yz.ap[:, c * 128:(c + 1) * 128], xsT.ap[:, c, tok], Dp.ap[:, c:c + 1],
                           py.ap[:, c * 128:(c + 1) * 128], ALU.mult, ALU.add,
                           reads=[(xsT, (c, tt)), Dp, py], writes=[(yz, c)])
                sc.tt("dve", yz.ap.rearrange("p (c t) -> p c t", c=4), yz.ap.rearrange("p (c t) -> p c t", c=4),
                      zsT.ap[:, :, tok], ALU.mult, reads=[yz] + [(zsT, (c, tt)) for c in range(4)], writes=[yz])
                sc.act(sqb.ap, yz.ap, AF.Square, reads=[yz], writes=[sqb])
                sc.copy("pool", yT.ap[:, :, tok], yz.ap.rearrange("p (c t) -> p c t", c=4), reads=[yz],
                        writes=[(yT, (c, tt)) for c in range(4)])
                pss = sc.psum()
                for c in range(4):
                    sc.mm(pss.ap[:, 0:128], ones_bf.ap, sqb.ap[:, c * 128:(c + 1) * 128], c == 0, c == 3,
                          reads=[ones_bf, sqb], writes=[pss])
                rs = rsb2[p]
                sc.act(rs.ap, pss.ap[:, 0:128], AF.Sqrt, reads=[pss], writes=[rs], bias=EPS, scale=1.0 / 512)

            def back_b(ci):
                p = ci % 2
                tok = slice(ci * 128, (ci + 1) * 128)
                tt = ci // 4
                rs = rsb2[p]
                sc.op("dve", lambda h, a=rs.ap: h.reciprocal(a, a), reads=[rs], writes=[rs])
                for c in range(4):
                    sc.stt("dve", yT.ap[:, c, tok], yT.ap[:, c, tok], gn.ap[:, c:c + 1], rs.ap,
                           ALU.mult, ALU.mult, reads=[(yT, (c, tt)), gn, rs], writes=[(yT, (c, tt))])

            front(0)
            for ci in range(16):
                if ci + 1 < 16:
                    front(ci + 1)
                back(ci)
                if ci > 0:
                    back_b(ci - 1)
            back_b(15)

        def mixer_C(l, base=None, plan_only=False):
            if base is None:
                base = len(ring.plan)
                add_win(l, C_C_KC, 512)
                add_win(l, C_C_KW, 280)
                for kv in range(2):
                    ring.add([(lambda s_: s_[0:64, 0:4096].rearrange("p (j h) -> p j h", j=32),
                               w_d["nsa_w1_" + "kv"[kv]][l].rearrange("(j e) h -> e j h", e=64))])
                add_win(l, C_C_Q, 512)
                if plan_only:
                    return base
            Qa = [scr("C_Qa%d" % g, g * 16384, [128, 8192], BF16) for g in range(2)]
            Ks = [scr("C_Ks%d" % g, 32768 + g * 4096, [128, 2048], BF16) for g in range(2)]
            Kw = [scr("C_Kw%d" % g, 40960 + g * 4096, [128, 2048], BF16) for g in range(2)]
            Vs = scr("C_Vs", 49152, [128, 16 * 2 * 66], BF16)
            Vw = scr("C_Vw", 53376, [128, 16 * 2 * 66], BF16)
            Kc = scr("C_Kc", 57600, [128, 2 * 128], BF16)
            Vc = scr("C_Vc", 58112, [128, 2 * 98], BF16)
            gate = scr("C_gate", 58624, [128, 16 * 24], F32)
            PT = [scr("C_PT%d" % i, 60160 + i * 1024, [128, 512], BF16) for i in range(2)]
            o = 62208
            gqk = scr("C_gqk", o, [128, 2], F32); o += 8
            cbias = scr("C_cb", o, [128, 2], F32); o += 8
            w2 = scr("C_w2", o, [128, 2 * 64], BF16); o += 256
            peT = scr("C_peT", o, [64, 64], F32); o += 256
            peTb = scr("C_peTb", o, [64, 64], BF16); o += 128
            mcb = scr("C_mc", o, [128, 512], BF16); o += 1024
            mfb = scr("C_mf", o, [128, 512], BF16); o += 1024
            assert o <= 65536, o
            Vs4 = Vs.ap.rearrange("p (k g e) -> p k g e", k=16, g=2)
            Vw4 = Vw.ap.rearrange("p (k g e) -> p k g e", k=16, g=2)
            Vc3 = Vc.ap.rearrange("p (g e) -> p g e", g=2)
            Kc3 = Kc.ap.rearrange("p (g c) -> p g c", g=2)
            gate3 = gate.ap.rearrange("p (j c) -> p j c", j=16)
            kcvc = scr("C_kcvc", 0, [64, 4 * 2048], BF16)
            kcvc3 = kcvc.ap.rearrange("p (a t) -> p a t", a=4)
            yq = iobuf[0]
            sel = iobuf[1]
            Dcq = cst.ap[:, 5, :]

            sc.dma("sp", gqk.ap[0:64, 0:1], w_d["nsa_q_norm"][l].rearrange("(e o) -> e o", o=1), writes=[(gqk, 0)],
                   skey="ldC")
            sc.dma("sp", gqk.ap[0:64, 1:2], w_d["nsa_k_norm"][l].rearrange("(e o) -> e o", o=1), writes=[(gqk, 1)],
                   skey="ldC")
            sc.ts("dve", gqk.ap[0:64, 0:1], gqk.ap[0:64, 0:1], 0.125, None, ALU.mult, ALU.bypass,
                  reads=[(gqk, 0)], writes=[(gqk, 0)])
            for kv, nm in enumerate(("nsa_pe_k", "nsa_pe_v")):
                sc.dma("sp", peT.ap[:, kv * 32:(kv + 1) * 32], w_d[nm][l].rearrange("j e -> e j"), writes=[(peT, kv)],
                       skey="ldC", slow=True)
                sc.dma("pool", w2.ap[:, kv * 64:(kv + 1) * 64], w_d["nsa_w2_" + "kv"[kv]][l], writes=[(w2, kv)], skey="ldC2")
            sc.copy("dve", peTb.ap, peT.ap, reads=[peT], writes=[peTb])
            sc.dma("pool", mcb.ap, nsa_c["nc_mc"], writes=[mcb], skey="ldC2")
            sc.dma("pool", mfb.ap, nsa_c["nc_mf"], writes=[mfb], skey="ldC2")
            for g in range(2):
                sc.dma("pool", Kw[g].ap[64:68, :], nsa_c["nc_kal"], writes=[(Kw[g], "al")], skey="ldC2")
                sc.dma("pool", Ks[g].ap[96:128, :], nsa_c["nc_onehot"], writes=[(Ks[g], "oh")], skey="ldC2")
                sc.op("dve", lambda h, a=Ks[g].ap[64:96, :]: h.memset(a, 0.0), writes=[(Ks[g], "z")])
                sc.dma("pool", Ks[g].ap[64:68, :], nsa_c["nc_kal"], reads=[(Ks[g], "z")], writes=[(Ks[g], "al")], skey="ldC2")
                sc.dma("pool", Kc3[64:68, g, :], nsa_c["nc_kalc"], writes=[(Kc, ("al", g))], skey="ldC2")
                sc.dma("pool", Vc3[:, g, 65:97], nsa_c["nc_ov"], writes=[(Vc, ("ov", g))], skey="ldC2")
                sc.op("dve", lambda h, a=Vc3[:, g, 64:65]: h.memset(a, 1.0), writes=[(Vc, ("one", g))])
            sc.op("dve", lambda h, a=Vs4[:, :, :, 64:65]: h.memset(a, 1.0), writes=[(Vs, "one")])
            sc.op("dve", lambda h, a=Vw4[:, :, :, 64:65]: h.memset(a, 1.0), writes=[(Vw, "one")])


            n64 = [0]

            def norm64(p_ap, pbuf, gcol, out_ap, out_w, n):
                n64[0] += 1
                sq = sq_bf[n64[0] % 2]
                sc.act(sq.ap[0:64, 0:n], p_ap, AF.Square, reads=[pbuf], writes=[sq])
                pss = sc.psum()
                sc.mm(pss.ap[0:64, 0:n], ones_bf.ap[0:64, 0:64], sq.ap[0:64, 0:n], True, True, reads=[ones_bf, sq],
                      writes=[pss])
                rs = tmpf()
                rstd_op("Ck", rs.ap[0:64, 0:n], pss.ap[0:64, 0:n], pss, rs, 1.0 / 64)
                sc.stt("dve", out_ap, p_ap, gqk.ap[0:64, gcol:gcol + 1], rs.ap[0:64, 0:n], ALU.mult, ALU.mult,
                       reads=[pbuf, gqk, rs], writes=out_w)

            w1 = ring.get(base, live=1)
            w1v = wview8(w1, 512)
            for a in range(4):
                for tt in range(NTT):
                    tsl = slice(tt * 512, (tt + 1) * 512)
                    pk = sc.psum()
                    for k in range(NKC):
                        sc.mm(pk.ap[0:64, :], w1v[:, k, a * 64:(a + 1) * 64], hT.ap[:, k, tsl], k == 0, k == NKC - 1,
                              reads=[w1, (hT, (k, tt))], writes=[pk])
                    sc.op("act", lambda h, o_=kcvc3[:, a, tsl], i_=pk.ap[0:64, :]: h.copy(o_, i_), reads=[pk],
                          writes=[(kcvc, (a, tt))])
            def pipe(tasks):
                pend = tasks[0][0]()
                for i, (_, cons) in enumerate(tasks):
                    nxt = tasks[i + 1][0]() if i + 1 < len(tasks) else None
                    cons(pend)
                    pend = nxt

            def kproj(wslot, wv, col0, tt):
                def f():
                    tsl = slice(tt * 512, (tt + 1) * 512)
                    pk = sc.psum()
                    for k in range(NKC):
                        sc.mm(pk.ap[0:64, :], wv[:, k, col0:col0 + 64], hT.ap[:, k, tsl], k == 0, k == NKC - 1,
                              reads=[wslot, (hT, (k, tt))], writes=[pk])
                    return pk
                return f

            tasks = []
            for g in range(2):
                for tt in range(NTT):
                    tsl = slice(tt * 512, (tt + 1) * 512)
                    tasks.append((kproj(w1, w1v, 256 + g * 64, tt),
                                  lambda pk, g=g, tt=tt, tsl=tsl: norm64(pk.ap[0:64, :], pk, 1, Ks[g].ap[0:64, tsl],
                                                                          [(Ks[g], ("k", tt))], 512)))
            pipe(tasks)
            for kb in range(16):
                pv = sc.psum()
                for k in range(NKC):
                    sc.mm(pv.ap[:, 0:128], hT.ap[:, k, kb * 128:(kb + 1) * 128], w1v[:, k, 384:512], k == 0, k == NKC - 1,
                          reads=[w1, (hT, (k, kb // 4))], writes=[pv])
                sc.copy("dve", Vs4[:, kb, :, 0:64], pv.ap[:, 0:128].rearrange("p (g e) -> p g e", g=2), reads=[pv],
                        writes=[(Vs, ("v", kb))])
            w2s = ring.get(base + 1, live=1)
            w2v = wview8(w2s, 280)
            tasks = []
            for g in range(2):
                for tt in range(NTT):
                    tsl = slice(tt * 512, (tt + 1) * 512)
                    tasks.append((kproj(w2s, w2v, g * 64, tt),
                                  lambda pk, g=g, tt=tt, tsl=tsl: norm64(pk.ap[0:64, :], pk, 1, Kw[g].ap[0:64, tsl],
                                                                          [(Kw[g], ("k", tt))], 512)))
            pipe(tasks)
            for kb in range(16):
                pv = sc.psum()
                for k in range(NKC):
                    sc.mm(pv.ap[:, 0:152], hT.ap[:, k, kb * 128:(kb + 1) * 128], w2v[:, k, 128:280], k == 0, k == NKC - 1,
                          reads=[w2s, (hT, (k, kb // 4))], writes=[pv])
                sc.copy("dve", Vw4[:, kb, :, 0:64], pv.ap[:, 0:128].rearrange("p (g e) -> p g e", g=2), reads=[pv],
                        writes=[(Vw, ("v", kb))])
                sc.act(gate3[:, kb, :], pv.ap[:, 128:152], AF.Sigmoid, reads=[pv], writes=[(gate, kb)])
            for kv in range(2):
                ww = ring.get(base + 2 + kv, live=1)
                wwv = ww.ap[0:64, 0:4096].rearrange("p (j h) -> p j h", j=32)
                pb_ = sc.psum()
                for j in range(32):
                    sc.mm(pb_.ap[:, 0:1], wwv[:, j, :], peTb.ap[:, kv * 32 + j:kv * 32 + j + 1], j == 0, j == 31,
                          reads=[ww, peTb], writes=[pb_])
                sc.copy("dve", cbias.ap[:, kv:kv + 1], pb_.ap[:, 0:1], reads=[pb_], writes=[(cbias, kv)])
                for g in range(2):
                    ph = sc.psum()
                    for j in range(32):
                        sc.mm(ph.ap[:, 0:127], wwv[:, j, :], kcvc3[:, kv * 2 + g, j:j + 2017:16], j == 0, j == 31,
                              reads=[ww, kcvc], writes=[ph])
                    hb_ = tmpf()
                    sc.ts("dve", hb_.ap[:, 0:127], ph.ap[:, 0:127], cbias.ap[:, kv:kv + 1], None, ALU.add, ALU.bypass,
                          reads=[ph, (cbias, kv)], writes=[hb_])
                    hid = sq_bf[1]
                    gelu_psum(hb_.ap[:, 0:127], hid.ap[:, 0:127], hb_, [hid])
                    if kv == 0:
                        pc_ = sc.psum()
                        sc.mm(pc_.ap[0:64, 0:127], w2.ap[:, 0:64], hid.ap[:, 0:127], True, True, reads=[(w2, 0), hid],
                              writes=[pc_])
                        norm64(pc_.ap[0:64, 0:127], pc_, 1, Kc3[0:64, g, 0:127], [(Kc, ("k", g))], 127)
                    else:
                        pc_ = sc.psum()
                        sc.mm(pc_.ap[0:127, 0:64], hid.ap[:, 0:127], w2.ap[:, 64:128], True, True, reads=[(w2, 1), hid],
                              writes=[pc_])
                        sc.copy("dve", Vc3[0:127, g, 0:64], pc_.ap[0:127, 0:64], reads=[pc_], writes=[(Vc, ("v", g))])
            sc.barrier()
            for g in range(2):
                sc.op("dve", lambda h, a=Qa[g].ap[64:128, :]: h.memset(a, 0.0), writes=[(Qa[g], "z")])
                sc.dma("pool", Qa[g].ap[64:68, :], nsa_c["nc_qal"][:, g, :], reads=[(Qa[g], "z")], writes=[(Qa[g], "al")],
                       skey="ldC2")
            wq = ring.get(base + 4, live=1)
            wqv = wview8(wq, 512)
            def qnorm(pq, hh, tt):
                g, r = hh // 4, hh % 4
                Qv = Qa[g].ap[0:64, :].rearrange("p (j r q) -> p j r q", j=16, r=4)
                sq = sq_bf[(hh * 4 + tt) % 2]
                sc.act(sq.ap[0:64, :], pq.ap[0:64, :], AF.Square, reads=[pq], writes=[sq])
                pss = sc.psum()
                sc.mm(pss.ap[0:64, :], ones_bf.ap[0:64, 0:64], sq.ap[0:64, :], True, True, reads=[ones_bf, sq],
                      writes=[pss])
                rs = tmpf()
                rstd_op("Cq", rs.ap[0:64, :], pss.ap[0:64, :], pss, rs, 1.0 / 64)
                sc.stt("dve", Qv[:, 4 * tt:4 * tt + 4, r, :], pq.ap[0:64, :].rearrange("p (j q) -> p j q", j=4),
                       gqk.ap[0:64, 0:1], rs.ap[0:64, :].rearrange("p (j q) -> p j q", j=4), ALU.mult, ALU.mult,
                       reads=[pq, gqk, rs], writes=[(Qa[g], ("q", hh, tt))])

            tasks = []
            for hh in range(8):
                for tt in range(NTT):
                    tasks.append((kproj(wq, wqv, hh * 64, tt), lambda pq, hh=hh, tt=tt: qnorm(pq, hh, tt)))
            pipe(tasks)

            if cfg.get("c_stop", 9) < 5:
                return
            sc.barrier()
            sc.mark("L%d C attn" % l)
            NB = cfg.get("c_nj", 16)
            acc_c = sc.reserve(2)
            acc_s = sc.reserve(1)[0]
            acc_w = sc.reserve(1)[0]
            order = [(j, g) for j in range(NB) for g in range(2)]
            pt_rr = [0]
            selp_v = selp_sb.ap.rearrange("p (a n) -> p a n", a=2)

            def cmpsel(idx):
                j, g = order[idx]
                par = idx % 2
                pc_ = acc_c[par]
                so = par * 16
                ps = sc.psum()
                sc.mm(ps.ap[0:127, :], Kc3[0:68, g, 0:127], Qa[g].ap[0:68, j * 512:(j + 1) * 512], True, True,
                      reads=[(Qa[g], ("blk", j))], writes=[ps])
                mk = tmpf()
                sc.ts("dve", mk.ap[0:127, 0:128], Dcq[0:127, :], float(128 * j - 31), NEGM, ALU.is_gt, ALU.mult,
                      reads=[cst], writes=[mk])
                sc.tt("dve", sel.ap[0:127, 0:512].rearrange("p (r q) -> p r q", r=4),
                      ps.ap[0:127, :].rearrange("p (r q) -> p r q", r=4),
                      mk.ap[0:127, 0:128].unsqueeze(1).to_broadcast([127, 4, 128]), ALU.add, reads=[ps, mk],
                      writes=[(sel, "s")])
                pt_rr[0] += 1
                pt = PT[pt_rr[0] % 2]
                sc.act(pt.ap[0:127, :], sel.ap[0:127, 0:512], AF.Exp, reads=[(sel, "s")], writes=[pt])
                for r in range(4):
                    sc.mm(pc_.ap[:, r * 128:r * 128 + 97], pt.ap[0:127, r * 128:(r + 1) * 128], Vc3[0:127, g, 0:97],
                          True, True, reads=[pt], writes=[(pc_, r)])
                pc4 = pc_.ap.rearrange("p (r c) -> p r c", r=4)
                rc = small.ap[:, so:so + 4]
                sc.ts("dve", rc, pc4[:, :, 64], 1e-30, None, ALU.max, ALU.bypass, reads=[pc_], writes=[(small, ("rc", par))])
                sc.op("dve", lambda h, a=rc: h.reciprocal(a, a), reads=[(small, ("rc", par))], writes=[(small, ("rc", par))])
                imp = sel.ap[:, 512:544]
                sc.ts("dve", imp, pc4[:, 0, 65:97], small.ap[:, so:so + 1], None, ALU.mult, ALU.bypass,
                      reads=[pc_, (small, ("rc", par))], writes=[(sel, "imp")])
                for r in range(1, 4):
                    sc.stt("dve", imp, pc4[:, r, 65:97], small.ap[:, so + r:so + r + 1], imp, ALU.mult, ALU.add,
                           reads=[pc_, (small, ("rc", par)), (sel, "imp")], writes=[(sel, "imp")])
                sc.tt("dve", small.ap[:, so + 4:so + 8], rc, gate3[:, j, g * 12:g * 12 + 12:3], ALU.mult,
                      reads=[(small, ("rc", par)), (gate, j)], writes=[(small, ("cfc", par))])
                sc.tt("dve", imp, imp, selp_v[:, 0, 31 - 2 * j:63 - 2 * j], ALU.mult, reads=[(sel, "imp"), selp_sb],
                      writes=[(sel, "imp")])
                sc.tt("dve", imp, imp, selp_v[:, 1, 31 - 2 * j:63 - 2 * j], ALU.add, reads=[(sel, "imp"), selp_sb],
                      writes=[(sel, "imp")])
                sc.op("dve", lambda h, a=sel.ap[:, 512:513]: h.memset(a, 1e9), reads=[(sel, "imp")], writes=[(sel, "imp")])
                top8 = sel.ap[:, 544:552]
                sc.op("dve", lambda h, o_=top8, i_=imp: h.max(o_, i_), reads=[(sel, "imp")], writes=[(sel, "top")])
                sneg = sel.ap[:, 640:768]
                sc.op("dve", lambda h, a=sel.ap[:, 640:736]: h.memset(a, 0.0), writes=[(sel, "sneg")])
                sc.ts("dve", sel.ap[:, 736:768], imp, sel.ap[:, 551:552], -32768.0, ALU.is_lt, ALU.mult,
                      reads=[(sel, "imp"), (sel, "top")], writes=[(sel, "sneg")])

            def cmpsel_b(idx):
                j, g = order[idx]
                sneg = sel.ap[:, 640:768]
                ptr = sc.psum()
                sc.transpose(ptr.ap[:, 0:128], sneg, ident.ap, reads=[(sel, "sneg"), cst], writes=[ptr])
                Qv = Qa[g].ap[96:128, j * 512:(j + 1) * 512].rearrange("p (r q) -> p r q", r=4)
                sc.copy("dve", Qv, ptr.ap[96:128, 0:128].unsqueeze(1).to_broadcast([32, 4, 128]), reads=[ptr],
                        writes=[(Qa[g], ("blk", j))])

            def attends(idx):
                j, g = order[idx]
                items = []
                for kb in range(j + 1):
                    items.append((Ks[g].ap[:, kb * 128:(kb + 1) * 128], 128, mcb if kb == j else None, acc_s,
                                  Vs4[:, kb, g, 0:65], kb == 0))
                kbs = [kb for kb in (j - 2, j - 1, j) if kb >= 0]
                for kb in kbs:
                    mb = mcb if kb == j else (mfb if kb == j - 2 else None)
                    items.append((Kw[g].ap[0:68, kb * 128:(kb + 1) * 128], 68, mb, acc_w, Vw4[:, kb, g, 0:65],
                                  kb == kbs[0]))

                def qk(it):
                    Kt, krows, maskb, po, Vap, first = it
                    ps = sc.psum()
                    sc.mm(ps.ap, Kt, Qa[g].ap[0:krows, j * 512:(j + 1) * 512], True, maskb is None,
                          reads=[(Qa[g], ("blk", j))], writes=[ps])
                    if maskb is not None:
                        sc.mm(ps.ap, ident_bf.ap, maskb.ap, False, True, reads=[], writes=[ps])
                    return ps

                def pv(it, ps):
                    Kt, krows, maskb, po, Vap, first = it
                    pt_rr[0] += 1
                    pt = PT[pt_rr[0] % 2]
                    sc.act(pt.ap, ps.ap, AF.Exp, reads=[ps], writes=[pt])
                    for r in range(4):
                        sc.op("pe", lambda h, o_=po.ap[:, r * 128:r * 128 + 65], a_=pt.ap[:, r * 128:(r + 1) * 128],
                              b_=Vap, st=(first and r == 0): h.matmul(o_, a_, b_, start=st, stop=True,
                                                                       skip_group_check=True),
                              reads=[pt], writes=[po])

                pend = qk(items[0])
                for i, it in enumerate(items):
                    nxt = qk(items[i + 1]) if i + 1 < len(items) else None
                    pv(it, pend)
                    pend = nxt

            def combine(idx):
                j, g = order[idx]
                par = idx % 2
                so = par * 16
                yv = yq.ap[:, g * 256:(g + 1) * 256].rearrange("p (r e) -> p r e", r=4)
                for bi, pob in enumerate((acc_c[par], acc_s, acc_w)):
                    p4 = pob.ap.rearrange("p (r c) -> p r c", r=4)
                    if bi == 0:
                        cf = small.ap[:, so + 4:so + 8]
                        cfk = (small, ("cfc", par))
                    else:
                        cf = small.ap[:, 32 + bi * 4:36 + bi * 4]
                        cfk = (small, ("cf", bi))
                        sc.op("dve", lambda h, o_=cf, i_=p4[:, :, 64]: h.reciprocal(o_, i_), reads=[pob], writes=[cfk])
                        sc.tt("dve", cf, cf, gate3[:, j, g * 12 + bi:g * 12 + 12:3], ALU.mult,
                              reads=[cfk, (gate, j)], writes=[cfk])
                    cfb = cf.unsqueeze(2).to_broadcast([128, 4, 64])
                    if bi == 0:
                        sc.tt("dve", yv, p4[:, :, 0:64], cfb, ALU.mult, reads=[pob, cfk], writes=[(yq, g)])
                    else:
                        tm = tmpf()
                        tv = tm.ap[:, 0:256].rearrange("p (r e) -> p r e", r=4)
                        sc.tt("dve", tv, p4[:, :, 0:64], cfb, ALU.mult, reads=[pob, cfk], writes=[tm])
                        sc.tt("pool", yv, yv, tv, ALU.add, reads=[tm, (yq, g)], writes=[(yq, g)])
                if g == 1:
                    tok = slice(j * 128, (j + 1) * 128)
                    pty = sc.psum()
                    for c in range(4):
                        sc.transpose(pty.ap[:, c * 128:(c + 1) * 128], yq.ap[:, c * 128:(c + 1) * 128], ident.ap,
                                     reads=[yq, cst], writes=[(pty, c)])
                    sc.copy("dve", yT.ap[:, :, tok], pty.ap.rearrange("p (c t) -> p c t", c=4), reads=[pty],
                            writes=[(yT, (c, j // 4)) for c in range(4)])

            cmpsel(0)
            cmpsel_b(0)
            for idx in range(len(order)):
                if idx + 1 < len(order):
                    cmpsel(idx + 1)
                attends(idx)
                if idx + 1 < len(order):
                    cmpsel_b(idx + 1)
                combine(idx)
            sc.release(acc_c + [acc_s, acc_w])

        spill_v = xspill_d.rearrange("p (c t) -> p c t", c=NKC)
        mixfn = {"A": mixer_A, "B": mixer_B, "C": mixer_C, "D": mixer_D}
        stages = []
        for l in range(depth):
            if cfg.get("ffn1", True):
                stages.append(("L%d ffn1" % l, lambda l=l: ffn(l, 1, plan_only=True), lambda b, l=l: ffn(l, 1, base=b)))
            if cfg.get("mix", True):
                stages.append(("L%d mixpre" % l, None, lambda b, l=l: mix_pre(l)))
                first = True
                for mi, mname in enumerate("ABCD"):
                    if mname not in mixers:
                        continue
                    stages.append(("L%d mixer %s" % (l, mname), lambda l=l, f=mixfn[mname]: f(l, plan_only=True),
                                   lambda b, l=l, f=mixfn[mname]: f(l, base=b)))
                    stages.append(("L%d merge %s" % (l, mname), lambda l=l, mi=mi: merge(l, mi, False, plan_only=True),
                                   lambda b, l=l, mi=mi, first=first: (merge(l, mi, first, base=b), sc.barrier())))
                    first = False
                stages.append(("L%d outproj" % l, lambda l=l: out_proj(l, plan_only=True),
                               lambda b, l=l: (sc.dma("sp", xT.ap, spill_v, writes=[xT], skey="spill"), out_proj(l, base=b))))
            if cfg.get("ffn2", True):
                stages.append(("L%d ffn2" % l, lambda l=l: ffn(l, 2, plan_only=True), lambda b, l=l: ffn(l, 2, base=b)))

        def mix_pre(l):
            rmsnorm_to_hT(3 * l + 2)
            sc.dma("sp", spill_v, xT.ap, reads=[xT], skey="spill")
            sc.barrier()

        bases = {}

        def plan(i):
            if i < len(stages) and i not in bases:
                bases[i] = stages[i][1]() if stages[i][1] is not None else None

        for i, (nm, pf, rf) in enumerate(stages):
            plan(i)
            k = i + 1
            plan(k)
            while k < len(stages) and stages[k][1] is None:
                k += 1
                plan(k)
            sc.mark(nm)
            rf(bases[i])

        finals = []
        for t in range(S // 128):
            io = iobuf[t % 2]
            for half in range(2):
                ps = sc.psum()
                for j in range(4):
                    c = half * 4 + j
                    sc.transpose(ps.ap[:, j * 128:(j + 1) * 128], xT.ap[:, c, t * 128:(t + 1) * 128], ident.ap,
                                 reads=[(xT, (c, t // 4)), cst], writes=[(ps, j)])
                sc.copy("dve", io.ap[:, half * 512:(half + 1) * 512], ps.ap, reads=[ps], writes=[(io, half)])
            sc.dma("sp", out_d[t * 128:(t + 1) * 128, :], io.ap, reads=[io], skey=("io", t % 2))
        for key in (("io", 0), ("io", 1)):
            ds = sc.dsems[key]
            finals.append((ds[0], ds[1]))
        sc.mark("end")
        print("ENGINE OPS", {n: (e.count, len(e.ops)) for n, e in sc.eng.items()}, flush=True)
        print("MARKS", sc.marks, flush=True)
        sc.emit(finals)
    return nc


_CST = host_consts()


def kernel(**inputs):
    cfg = inputs.pop("_cfg", {})
    depth = cfg.get("depth", DEPTH)
    nc = build_program(cfg)
    x = np.ascontiguousarray(inputs["x"], dtype=np.float32)
    shared = {"cst": _CST}
    shared.update(_NSAC)
    for nm, _ in W_NAMES:
        shared[nm] = np.ascontiguousarray(np.asarray(inputs[nm], dtype=np.float32)[:depth])
    in_maps = []
    for b in range(8):
        m = dict(shared)
        m["x"] = x[b]
        in_maps.append(m)
    res = run_bass_kernel_spmd(nc, in_maps, core_ids=list(range(8)))
    return np.stack([np.asarray(r["out"], dtype=np.float32) for r in res.results], axis=0)
```
